# Optimizing a Trainium2 kernel written in Bass

```python
import math
import jax
import jax.numpy as jnp
from jax import lax
import numpy as np


D_MODEL = 1024
BATCH = 2
SEQ = 8192
DEPTH = 4
DEC_BATCH = 8
DEC_SEQ = 32
PAST_LEN = 1024

CHUNK = 64
N_MIXERS = 3
N_A = (DEPTH + 2) // 3
N_B = (DEPTH + 1) // 3
N_C = DEPTH // 3
D_FF = 2816
D_RNN = 1280
LRU_BLOCKS = 16
LRU_BS = D_RNN // LRU_BLOCKS
CONV_W = 4
LRU_C = 8.0
POOL_WINDOWS = (2, 4, 8, 16)
POOL_GROUPS = 4
POOL_GW = D_MODEL // POOL_GROUPS
POOL_HIST = 15
N_HEADS = 8
HEAD_DIM = D_MODEL // (2 * N_HEADS)
NUM_BUCKETS = 32
MAX_DISTANCE = 128
QBLOCK = 128
EPS = 1e-6
NEG_INF = -1e30

kernel_name = 'hybrid_streaming_encoder_step'


def rmsnorm(x, g):
    x32 = x.astype(jnp.float32)
    y = x32 * lax.rsqrt(jnp.mean(x32 * x32, axis=-1, keepdims=True) + EPS)
    return (y * g.astype(jnp.float32)).astype(x.dtype)


def modulate(x, g, shift, scale):
    return rmsnorm(x, g) * (1 + scale[:, None, :]) + shift[:, None, :]


def adaln(c, w, b):
    return jnp.split(jax.nn.silu(c) @ w + b, 9, axis=-1)


def swiglu(u, w_in, w_out):
    a, b = jnp.split(u @ w_in, 2, axis=-1)
    return (jax.nn.silu(a) * b) @ w_out


def ffn_half(x, shift, scale, gate, g, w_in, w_out):
    u = modulate(x, g, shift, scale)
    return x + 0.5 * gate[:, None, :] * swiglu(u, w_in, w_out)


def causal_dwconv(x, buf, w, b):
    T = x.shape[1]
    xp = jnp.concatenate([buf.astype(x.dtype), x], axis=1)
    y = b + w[0] * xp[:, 0:T]
    for k in range(1, CONV_W):
        y = y + w[k] * xp[:, k:k + T]
    return y, xp[:, -(CONV_W - 1):]


def rg_lru_mixer(u, conv_buf, h0, w_in, conv_w, conv_b, ga_w, ga_b, gx_w, gx_b, lam, w_out):
    B, T, _ = u.shape
    gate_br, x_br = jnp.split(u @ w_in, 2, axis=-1)
    xc, new_buf = causal_dwconv(x_br, conv_buf, conv_w, conv_b)
    xb = xc.reshape(B, T, LRU_BLOCKS, LRU_BS)
    r = jax.nn.sigmoid(jnp.einsum('bthi,hij->bthj', xb, ga_w).reshape(B, T, D_RNN) + ga_b)
    i = jax.nn.sigmoid(jnp.einsum('bthi,hij->bthj', xb, gx_w).reshape(B, T, D_RNN) + gx_b)
    log_a = (-LRU_C * r.astype(jnp.float32)) * jax.nn.softplus(-lam.astype(jnp.float32))
    a = jnp.exp(log_a)
    b_in = jnp.sqrt(-jnp.expm1(2.0 * log_a)) * (i * xc).astype(jnp.float32)
    b_in = b_in.at[:, 0].add(a[:, 0] * h0.astype(jnp.float32))

    def combine(left, right):
        a1, b1 = left
        a2, b2 = right
        return a1 * a2, a2 * b1 + b2

    _, h = lax.associative_scan(combine, (a, b_in), axis=1)
    y = (jax.nn.gelu(gate_br) * h.astype(u.dtype)) @ w_out
    return y, new_buf, h[:, -1].astype(h0.dtype)


def pool_mixer(u_ext, n_new, w, b, scale):
    B, L, _ = u_ext.shape
    u32 = u_ext.astype(jnp.float32)
    cs = jnp.concatenate([jnp.zeros((B, 1, D_MODEL), jnp.float32), jnp.cumsum(u32, axis=1)], axis=1)
    t = jnp.arange(L - n_new, L)
    outs = []
    for g, wnd in enumerate(POOL_WINDOWS):
        sl = slice(g * POOL_GW, (g + 1) * POOL_GW)
        lo = jnp.maximum(t + 1 - wnd, 0)
        cnt = jnp.minimum(t + 1, wnd).astype(jnp.float32)
        mean = (cs[:, t + 1, sl] - cs[:, lo, sl]) / cnt[None, :, None]
        d = (mean - u32[:, t, sl]).astype(u_ext.dtype)
        outs.append(jnp.einsum('btc,cd->btd', d, w[g]))
    y = jnp.concatenate(outs, axis=-1) + b
    return (y * scale).astype(u_ext.dtype)


def t5_bucket(rel):
    nb = NUM_BUCKETS // 2
    max_exact = nb // 2
    ret = jnp.where(rel > 0, nb, 0)
    n = jnp.abs(rel)
    nf = jnp.maximum(n, 1).astype(jnp.float32)
    large = max_exact + (jnp.log(nf / max_exact) / math.log(MAX_DISTANCE / max_exact)
                         * (nb - max_exact)).astype(jnp.int32)
    large = jnp.minimum(large, nb - 1)
    return ret + jnp.where(n < max_exact, n, large)


def diff_lambda(lam_p, lam_init):
    lp = lam_p.astype(jnp.float32)
    return jnp.exp(jnp.sum(lp[0] * lp[1])) - jnp.exp(jnp.sum(lp[2] * lp[3])) + lam_init


def diff_attn_qkv(u, w_in, q_g, k_g):
    B, T, _ = u.shape
    q, k, v = jnp.split(u @ w_in, 3, axis=-1)
    q = rmsnorm(q.reshape(B, T, N_HEADS, 2, HEAD_DIM), q_g)
    k = rmsnorm(k.reshape(B, T, N_HEADS, 2, HEAD_DIM), k_g)
    v = v.reshape(B, T, N_HEADS, 2 * HEAD_DIM)
    return q, k, v


def diff_attn_core(q, k, v, q_pos, k_pos, rel_bias, lam):
    s = jnp.einsum('bqhcd,bkhcd->bchqk', q.astype(jnp.float32), k.astype(jnp.float32)) * (HEAD_DIM ** -0.5)
    bias = jnp.transpose(rel_bias.astype(jnp.float32)[t5_bucket(k_pos[None, :] - q_pos[:, None])], (2, 0, 1))
    mask = (k_pos[None, :] // CHUNK) <= (q_pos[:, None] // CHUNK)
    s = jnp.where(mask, s + bias, NEG_INF)
    p = jax.nn.softmax(s, axis=-1)
    pd = p[:, 0] - lam * p[:, 1]
    return jnp.einsum('bhqk,bkhd->bqhd', pd.astype(v.dtype), v)


def diff_attn_out(o, sub_g, lam_init, w_out):
    B, T = o.shape[:2]
    o = rmsnorm(o, sub_g) * (1.0 - lam_init)
    return o.reshape(B, T, D_MODEL) @ w_out


def diff_attn_prompt(u, w_in, q_g, k_g, sub_g, w_out, rel_bias, lam, lam_init):
    B, S, _ = u.shape
    q, k, v = diff_attn_qkv(u, w_in, q_g, k_g)
    pos = jnp.arange(S)

    def block(i):
        start = i * QBLOCK
        qb = lax.dynamic_slice_in_dim(q, start, QBLOCK, axis=1)
        return diff_attn_core(qb, k, v, start + jnp.arange(QBLOCK), pos, rel_bias, lam)

    o = lax.map(block, jnp.arange(S // QBLOCK))
    o = jnp.moveaxis(o, 0, 1).reshape(B, S, N_HEADS, 2 * HEAD_DIM)
    y = diff_attn_out(o, sub_g, lam_init, w_out)
    return y, k.reshape(B, S, N_HEADS, 2 * HEAD_DIM), v


def diff_attn_sample(u, ck, cv, w_in, q_g, k_g, sub_g, w_out, rel_bias, lam, lam_init):
    B, T, _ = u.shape
    P = ck.shape[1]
    q, k, v = diff_attn_qkv(u, w_in, q_g, k_g)
    kf = jnp.concatenate([ck.astype(k.dtype).reshape(B, P, N_HEADS, 2, HEAD_DIM), k], axis=1)
    vf = jnp.concatenate([cv.astype(v.dtype), v], axis=1)
    o = diff_attn_core(q, kf, vf, P + jnp.arange(T), jnp.arange(P + T), rel_bias, lam)
    y = diff_attn_out(o, sub_g, lam_init, w_out)
    return y, k.reshape(B, T, N_HEADS, 2 * HEAD_DIM), v


def setup_inputs(seed: int = 0) -> dict:
    key = jax.random.key(seed)
    ks = iter(jax.random.split(key, 48))
    nrm = lambda shape, s: jax.random.normal(next(ks), shape, jnp.float32) * s
    u_a = jax.random.uniform(next(ks), (N_A, D_RNN), jnp.float32, minval=0.9, maxval=0.999)
    a0 = u_a ** (1.0 / LRU_C)
    return {
        'x_prompt': nrm((BATCH, SEQ, D_MODEL), 1.0),
        'x_sample': nrm((DEC_BATCH, DEC_SEQ, D_MODEL), 1.0),
        'c_prompt': nrm((BATCH, D_MODEL), 1.0),
        'c_sample': nrm((DEC_BATCH, D_MODEL), 1.0),
        'state_lru_h': nrm((N_A, DEC_BATCH, D_RNN), 0.5),
        'state_lru_conv': nrm((N_A, DEC_BATCH, CONV_W - 1, D_RNN), 1.0),
        'state_pool': nrm((N_B, DEC_BATCH, POOL_HIST, D_MODEL), 1.0),
        'cache_k': nrm((N_C, DEC_BATCH, PAST_LEN, N_HEADS, 2 * HEAD_DIM), 1.0),
        'cache_v': nrm((N_C, DEC_BATCH, PAST_LEN, N_HEADS, 2 * HEAD_DIM), 1.0),
        'ada_w': nrm((DEPTH, D_MODEL, 9 * D_MODEL), D_MODEL ** -0.5),
        'ada_b': nrm((DEPTH, 9 * D_MODEL), 0.02),
        'norm_g': 1.0 + nrm((DEPTH, 3, D_MODEL), 0.02),
        'ffn_w_in': nrm((DEPTH, 2, D_MODEL, 2 * D_FF), D_MODEL ** -0.5),
        'ffn_w_out': nrm((DEPTH, 2, D_FF, D_MODEL), D_FF ** -0.5),
        'lru_w_in': nrm((N_A, D_MODEL, 2 * D_RNN), D_MODEL ** -0.5),
        'lru_conv_w': nrm((N_A, CONV_W, D_RNN), CONV_W ** -0.5),
        'lru_conv_b': nrm((N_A, D_RNN), 0.02),
        'lru_ga_w': nrm((N_A, LRU_BLOCKS, LRU_BS, LRU_BS), LRU_BS ** -0.5),
        'lru_ga_b': nrm((N_A, D_RNN), 0.02),
        'lru_gx_w': nrm((N_A, LRU_BLOCKS, LRU_BS, LRU_BS), LRU_BS ** -0.5),
        'lru_gx_b': nrm((N_A, D_RNN), 0.02),
        'lru_lambda': jnp.log(a0) - jnp.log1p(-a0),
        'lru_w_out': nrm((N_A, D_RNN, D_MODEL), D_RNN ** -0.5),
        'pool_w': nrm((N_B, POOL_GROUPS, POOL_GW, POOL_GW), POOL_GW ** -0.5),
        'pool_b': nrm((N_B, D_MODEL), 0.02),
        'pool_scale': 1.0 + nrm((N_B, D_MODEL), 0.02),
        'attn_w_in': nrm((N_C, D_MODEL, 3 * D_MODEL), D_MODEL ** -0.5),
        'attn_q_g': 1.0 + nrm((N_C, HEAD_DIM), 0.02),
        'attn_k_g': 1.0 + nrm((N_C, HEAD_DIM), 0.02),
        'attn_lambda': nrm((N_C, 4, HEAD_DIM), 0.1),
        'attn_sub_g': 1.0 + nrm((N_C, 2 * HEAD_DIM), 0.02),
        'attn_w_out': nrm((N_C, D_MODEL, D_MODEL), D_MODEL ** -0.5),
        'rel_bias': nrm((NUM_BUCKETS, N_HEADS), 0.5),
    }


def reference(x_prompt, x_sample, c_prompt, c_sample, state_lru_h, state_lru_conv, state_pool,
              cache_k, cache_v, ada_w, ada_b, norm_g, ffn_w_in, ffn_w_out, lru_w_in, lru_conv_w,
              lru_conv_b, lru_ga_w, lru_ga_b, lru_gx_w, lru_gx_b, lru_lambda, lru_w_out, pool_w,
              pool_b, pool_scale, attn_w_in, attn_q_g, attn_k_g, attn_lambda, attn_sub_g,
              attn_w_out, rel_bias):
    B, S, _ = x_prompt.shape
    T = x_sample.shape[1]
    xp, xs = x_prompt, x_sample
    h_p, h_s, cv_p, cv_s, pl_p, pl_s, k_p, v_p, k_s, v_s = [], [], [], [], [], [], [], [], [], []
    for l in range(DEPTH):
        kind = l % N_MIXERS
        j = l // N_MIXERS
        mp = adaln(c_prompt, ada_w[l], ada_b[l])
        ms = adaln(c_sample, ada_w[l], ada_b[l])
        xp = ffn_half(xp, mp[0], mp[1], mp[2], norm_g[l, 0], ffn_w_in[l, 0], ffn_w_out[l, 0])
        xs = ffn_half(xs, ms[0], ms[1], ms[2], norm_g[l, 0], ffn_w_in[l, 0], ffn_w_out[l, 0])
        up = modulate(xp, norm_g[l, 1], mp[3], mp[4])
        us = modulate(xs, norm_g[l, 1], ms[3], ms[4])
        if kind == 0:
            prm = (lru_w_in[j], lru_conv_w[j], lru_conv_b[j], lru_ga_w[j], lru_ga_b[j],
                   lru_gx_w[j], lru_gx_b[j], lru_lambda[j], lru_w_out[j])
            yp, bp, hp = rg_lru_mixer(up, jnp.zeros((B, CONV_W - 1, D_RNN), up.dtype),
                                      jnp.zeros((B, D_RNN), up.dtype), *prm)
            ys, bs, hs = rg_lru_mixer(us, state_lru_conv[j], state_lru_h[j], *prm)
            h_p.append(hp)
            h_s.append(hs)
            cv_p.append(bp)
            cv_s.append(bs)
        elif kind == 1:
            yp = pool_mixer(up, S, pool_w[j], pool_b[j], pool_scale[j])
            ext = jnp.concatenate([state_pool[j].astype(us.dtype), us], axis=1)
            ys = pool_mixer(ext, T, pool_w[j], pool_b[j], pool_scale[j])
            pl_p.append(up[:, -POOL_HIST:])
            pl_s.append(ext[:, -POOL_HIST:])
        else:
            lam_init = 0.8 - 0.6 * math.exp(-0.3 * l)
            lam = diff_lambda(attn_lambda[j], lam_init)
            prm = (attn_w_in[j], attn_q_g[j], attn_k_g[j], attn_sub_g[j], attn_w_out[j],
                   rel_bias, lam, lam_init)
            yp, kp, vp = diff_attn_prompt(up, *prm)
            ys, kn, vn = diff_attn_sample(us, cache_k[j], cache_v[j], *prm)
            k_p.append(kp)
            v_p.append(vp)
            k_s.append(kn)
            v_s.append(vn)
        xp = xp + mp[5][:, None, :] * yp
        xs = xs + ms[5][:, None, :] * ys
        xp = ffn_half(xp, mp[6], mp[7], mp[8], norm_g[l, 2], ffn_w_in[l, 1], ffn_w_out[l, 1])
        xs = ffn_half(xs, ms[6], ms[7], ms[8], norm_g[l, 2], ffn_w_in[l, 1], ffn_w_out[l, 1])
    return (xp, xs, jnp.stack(h_p), jnp.stack(h_s), jnp.stack(cv_p), jnp.stack(cv_s),
            jnp.stack(pl_p), jnp.stack(pl_s), jnp.stack(k_p), jnp.stack(v_p),
            jnp.stack(k_s), jnp.stack(v_s))
```

```python
import math
from contextlib import ExitStack

import numpy as np
import concourse.bass as bass
import concourse.mybir as mybir
from concourse.bass_utils import run_bass_kernel_spmd

F32 = mybir.dt.float32
BF16 = mybir.dt.bfloat16
AF = mybir.ActivationFunctionType
ALU = mybir.AluOpType
AX = mybir.AxisListType

D = 1024
NCH = 8
DFF = 2816
NHC = 22
DRNN = 1280
NBLK = 16
BS = 80
NHEAD = 8
EPS = 1e-6
NEG = -30000.0
ENGS = ["pe", "act", "dve", "pool", "sp"]
FFN_GROUPS = [(0, 6), (6, 12), (12, 18), (18, 22)]


class _Rec:
    def __init__(self):
        self.call = None

    def __getattr__(self, name):
        def f(*a, **k):
            self.call = (name, a, k)
            return self
        return f


class KB:
    def __init__(self, nc, ndma=24):
        self.nc = nc
        self.streams = {e: [] for e in ENGS}
        self.seq = {e: 0 for e in ENGS}
        self.prog = {}
        self.waited = {}
        self.state = {}
        self.ndma = ndma
        self.dsems = {}
        self.duse = {}
        self.dcnt = {}
        self.ccsems = []
        self.ccn = 0
        self.inherit = {}

    def setup_sems(self, stack):
        for e in ENGS:
            self.prog[e] = stack.enter_context(self.nc.semaphore("prog_" + e))
        for q in ["sp", "pool"]:
            self.dsems[q] = [stack.enter_context(self.nc.semaphore("d%s%d" % (q, i))) for i in range(self.ndma)]
            self.duse[q] = [0] * self.ndma
            self.dcnt[q] = 0
        self.ccsems = [stack.enter_context(self.nc.semaphore("cc%d" % i)) for i in range(24)]

    def _tk(self, tok):
        kind, a, v = tok
        return (kind, a if kind == "E" else id(a))

    def _wait(self, eng, tok):
        if tok is None:
            return
        kind, a, v = tok
        if kind == "E":
            if a == "pe" and eng == "pe":
                return
            sem = self.prog[a]
        else:
            sem = a
        key = (eng,) + self._tk(tok)
        if self.waited.get(key, 0) >= v:
            return
        self.waited[key] = v
        self.streams[eng].append(lambda e, sem=sem, v=v: e.wait_ge(sem, v))

    def _get(self, k):
        st = self.state.get(k)
        if st is None and k[0] in self.inherit:
            st = {"w": None, "r": dict(self.inherit[k[0]])}
            self.state[k] = st
        return st

    def new_phase(self, old_names, new_names):
        toks = {}
        old_names = set(old_names)
        for k in list(self.state.keys()):
            if k[0] in old_names:
                st = self.state.pop(k)
                cands = list(st["r"].values())
                if st["w"] is not None:
                    cands.append(st["w"])
                for t in cands:
                    tk = self._tk(t)
                    if tk not in toks or toks[tk][2] < t[2]:
                        toks[tk] = t
        for n in old_names:
            for tk, t in self.inherit.pop(n, {}).items():
                if tk not in toks or toks[tk][2] < t[2]:
                    toks[tk] = t
        for n in new_names:
            self.inherit[n] = dict(toks)

    def _deps(self, eng, reads, writes):
        for k in reads:
            st = self._get(k)
            if st:
                self._wait(eng, st["w"])
        for k in writes:
            st = self._get(k)
            if st:
                self._wait(eng, st["w"])
                for t in st["r"].values():
                    self._wait(eng, t)

    def _commit(self, tok, reads, writes):
        for k in writes:
            self.state[k] = {"w": tok, "r": {}}
        tk = self._tk(tok)
        for k in reads:
            st = self._get(k)
            if st is None:
                st = self.state.setdefault(k, {"w": None, "r": {}})
            old = st["r"].get(tk)
            if old is None or old[2] < tok[2]:
                st["r"][tk] = tok

    def op(self, eng, fn, reads=(), writes=()):
        self._deps(eng, reads, writes)
        self.seq[eng] += 1
        n = self.seq[eng]
        sem = self.prog[eng]
        rec = _Rec()
        fn(rec)
        name, a, k = rec.call
        self.streams[eng].append(lambda e, name=name, a=a, k=k, sem=sem: getattr(e, name)(*a, **k).then_inc(sem, 1))
        tok = ("E", eng, n)
        self._commit(tok, reads, writes)
        return tok

    def dma(self, q, out, in_, reads=(), writes=(), **kw):
        self._deps(q, reads, writes)
        i = self.dcnt[q] % self.ndma
        self.dcnt[q] += 1
        sem = self.dsems[q][i]
        if self.duse[q][i] > 0:
            self._wait(q, ("D", sem, 16 * self.duse[q][i]))
        self.duse[q][i] += 1
        v = 16 * self.duse[q][i]
        self.streams[q].append(
            lambda e, out=out, in_=in_, sem=sem, kw=kw: e.dma_start(out=out, in_=in_, **kw).then_inc(sem, 16))
        tok = ("D", sem, v)
        self._commit(tok, reads, writes)
        return tok

    def collective(self, groups, in_ap, out_ap, reads=(), writes=()):
        q = "pool"
        self._deps(q, reads, writes)
        sem = self.ccsems[self.ccn]
        self.ccn += 1
        self.streams[q].append(
            lambda e, sem=sem: e.collective_compute("AllGather", ALU.bypass, replica_groups=groups,
                                                    ins=[in_ap], outs=[out_ap]).then_inc(sem, 1))
        tok = ("D", sem, 1)
        self._commit(tok, reads, writes)
        return tok

    def alias(self, new_keys, old_keys):
        toks = {}
        for k in old_keys:
            st = self.state.get(k)
            if not st:
                continue
            cands = list(st["r"].values())
            if st["w"] is not None:
                cands.append(st["w"])
            for t in cands:
                tk = self._tk(t)
                if tk not in toks or toks[tk][2] < t[2]:
                    toks[tk] = t
        for k in new_keys:
            st = self.state.setdefault(k, {"w": None, "r": {}})
            for tk, t in toks.items():
                if tk not in st["r"] or st["r"][tk][2] < t[2]:
                    st["r"][tk] = t

    def keys_with_prefix(self, prefixes):
        return [k for k in self.state if (k[0] if isinstance(k, tuple) else k) in prefixes]

    def finish(self, final_tokens):
        for t in final_tokens:
            self._wait("sp", t)
        nc = self.nc
        S = self.streams
        with nc.Block() as block:
            @block.tensor
            def _(e):
                for f in S["pe"]:
                    f(e)

            @block.scalar
            def _(e):
                for f in S["act"]:
                    f(e)

            @block.vector
            def _(e):
                for f in S["dve"]:
                    f(e)

            @block.gpsimd
            def _(e):
                for f in S["pool"]:
                    f(e)

            @block.sync
            def _(e):
                for f in S["sp"]:
                    f(e)


def _t5_bucket(rel):
    nb, max_exact = 16, 8
    rel = np.asarray(rel)
    ret = np.where(rel > 0, nb, 0)
    n = np.abs(rel)
    nf = np.maximum(n, 1).astype(np.float32)
    large = max_exact + (np.log(nf / np.float32(max_exact)) / np.float32(math.log(128 / max_exact))
                         * np.float32(nb - max_exact)).astype(np.int32)
    large = np.minimum(large, nb - 1)
    return ret + np.where(n < max_exact, n, large)


class Cfg:
    def __init__(self, NP=2048, PAST=1024, kinds=(0, 1, 2, 0), debug=False):
        self.NP = NP
        self.NS = 32
        self.T = NP + 32
        self.TA = NP + 128
        self.PAST = PAST
        self.kinds = tuple(kinds)
        self.L = len(kinds)
        self.NA = sum(1 for k in kinds if k == 0)
        self.NB = sum(1 for k in kinds if k == 1)
        self.NCc = sum(1 for k in kinds if k == 2)
        self.debug = debug
        self.tiles = [(i * 512, 512, 0) for i in range(NP // 512)] + [(NP, 32, 1)]


class Builder:
    def __init__(self, cfg):
        self.cfg = cfg
        self.nc = bass.Bass("TRN2", target_bir_lowering=False)
        self.kb = KB(self.nc)
        self.final = []
        self.ring_n = 0
        self.ring_reserved = set()
        self.cnt = 0
        self.in_names = []
        self.out_names = []

    def din(self, name, shape, dt=F32):
        self.in_names.append(name)
        return self.nc.dram_tensor(name, list(shape), dt, kind="ExternalInput").ap()

    def dout(self, name, shape, dt=F32):
        self.out_names.append(name)
        return self.nc.dram_tensor(name, list(shape), dt, kind="ExternalOutput").ap()

    def sb(self, name, shape, dt=F32):
        return self.stack.enter_context(self.nc.sbuf_tensor(name, list(shape), dt))

    def nxt(self):
        self.cnt += 1
        return self.cnt

    def rload(self, src_ap, shape, q="pool", reads=(), reserve=False):
        while (self.ring_n % self.NRING) in self.ring_reserved:
            self.ring_n += 1
        i = self.ring_n % self.NRING
        self.ring_n += 1
        if reserve:
            self.ring_reserved.add(i)
        nelem = int(np.prod(shape[1:]))
        assert nelem <= 2048
        view = self.ring[i][0:shape[0], 0:nelem]
        if len(shape) == 3:
            view = view.rearrange("p (a b) -> p a b", a=shape[1])
        key = ("ring", i)
        self.kb.dma(q, view, src_ap, reads=list(reads), writes=[key])
        return view, key

    def carve(self, spec, keep=()):
        off = 0
        views = {}
        names = []
        for name, shape, dt in spec:
            esz = 4 if dt == F32 else 2
            nel = int(np.prod(shape[1:]))
            nbytes = (nel * esz + 31) // 32 * 32
            assert off + nbytes <= self.ARENA, ("arena overflow", name, off, nbytes)
            v = self.arena[0:shape[0], off // 4: (off + nbytes) // 4]
            if dt != F32:
                v = v.bitcast(BF16)
            v = v[:, 0:nel]
            if len(shape) == 3:
                v = v.rearrange("p (a b) -> p a b", a=shape[1])
            elif len(shape) == 4:
                v = v.rearrange("p (a b c) -> p a b c", a=shape[1], b=shape[2])
            if name in keep:
                assert self.arena_off.get(name) == off, ("kept buffer moved", name)
            views[name] = v
            names.append(name)
            self.arena_off[name] = off
            off += nbytes
        old = [n for n in self.arena_names if n not in keep]
        new = [n for n in names if n not in keep]
        self.kb.new_phase(old, new)
        self.arena_names = names
        return views

    def build(self):
        cfg = self.cfg
        nc = self.nc
        kb = self.kb
        T, NP, L = cfg.T, cfg.NP, cfg.L
        NT = len(cfg.tiles)
        with ExitStack() as stack:
            self.stack = stack
            kb.setup_sems(stack)
            d = {}
            self.d = d
            d["xT"] = self.din("xT", [128, NCH, T])
            d["cT"] = self.din("cT", [128, NCH, 2])
            d["ada_w"] = self.din("ada_w", [L, 36, 128, NCH, 256])
            d["ada_b"] = self.din("ada_b", [L, 128, 72])
            d["norm_g"] = self.din("norm_g", [128, L * 3 * NCH])
            d["ffn_w_in"] = self.din("ffn_w_in", [L, 2, NHC, 128, NCH, 256])
            d["ffn_w_out"] = self.din("ffn_w_out", [L, 2, NHC // 2, 128, 2, D])
            d["pos"] = self.din("pos", [128, 160])
            d["ident"] = self.din("ident", [128, 128])
            NA, NB, NC_ = max(cfg.NA, 1), max(cfg.NB, 1), max(cfg.NCc, 1)
            d["lru_win"] = self.din("lru_win", [NA, NBLK, 128, NCH, 160])
            d["lru_g"] = self.din("lru_g", [NA, BS, NBLK, 160])
            d["lru_wout"] = self.din("lru_wout", [NA, NBLK // 2, BS, 2, D])
            d["lru_small"] = self.din("lru_small", [NA, BS, NBLK, 12])
            d["pool_w"] = self.din("pool_w", [NB, 4, 128, 2, 256])
            d["pool_small"] = self.din("pool_small", [NB, 128, 2 * NCH + NCH * 15])
            d["attn_win"] = self.din("attn_win", [NC_, 12, 128, NCH, 256])
            d["attn_wout"] = self.din("attn_wout", [NC_, 4, 128, 2, D])
            d["attn_small"] = self.din("attn_small", [NC_, 128, 256 + 256 + 1 + 4])
            d["rb_rep"] = self.din("rb_rep", [128, 256])
            d["planes"] = self.din("planes", [128, 49 * 128], BF16)
            d["cache_kT"] = self.din("cache_kT", [NC_, NHEAD, 128, cfg.PAST])
            d["cache_v"] = self.din("cache_v", [NC_, NHEAD, 128, cfg.PAST // 128, 128])
            d["yT"] = self.dout("yT", [128, NCH, T])
            d["lru_o"] = self.dout("lru_o", [NA, BS, NBLK, 8])
            d["pool_o"] = self.dout("pool_o", [NB, 128, NCH, 30])
            d["k_o"] = self.dout("k_o", [NC_, T, D])
            d["v_o"] = self.dout("v_o", [NC_, T, D])
            if cfg.debug:
                d["dbg"] = self.dout("dbg", [128, NCH, T])
                d["dbg2"] = self.dout("dbg2", [128, 64])
            self.ag_h_in = nc.dram_tensor("ag_h_in", [128, 120], F32)
            self.ag_h_out = nc.dram_tensor("ag_h_out", [4 * 128, 120], F32)
            self.ag_c_in = nc.dram_tensor("ag_c_in", [BS, 32], F32)
            self.ag_c_out = nc.dram_tensor("ag_c_out", [4 * BS, 32], F32)
            self.ag_kv_in = [nc.dram_tensor("ag_kv_in%d" % h, [256, NP], BF16) for h in range(NHEAD)]
            self.ag_kv_out = [nc.dram_tensor("ag_kv_out%d" % h, [4 * 256, NP], BF16) for h in range(NHEAD)]
            self.groups = [[0, 1, 2, 3], [4, 5, 6, 7]]
            self.lru_a = nc.dram_tensor("lru_a_scr", [NBLK, BS, NP + 38], F32)
            self.lru_b = nc.dram_tensor("lru_b_scr", [NBLK, BS, NP + 38], F32)

            self.xT = self.sb("xT_sb", [128, NCH, T])
            self.uT = self.sb("uT_sb", [128, NCH, cfg.TA], BF16)
            self.NRING = 7
            self.ring = [self.sb("ring%d" % i, [128, 2048], BF16) for i in range(self.NRING)]
            self.ones_f = self.sb("ones_f", [128, 128])
            self.ones_b = self.sb("ones_b", [128, 128], BF16)
            self.ident = self.sb("ident_sb", [128, 128])
            self.mods = [self.sb("mod%d" % i, [128, 72, 2]) for i in range(2)]
            self.gss = [self.sb("gs%d" % i, [128, 3, NCH, 2]) for i in range(2)]
            self.ghs = [self.sb("gh%d" % i, [128, 2, NCH, 2]) for i in range(2)]
            self.adabs = [self.sb("adab%d" % i, [128, 72]) for i in range(2)]
            self.ng = self.sb("ng", [128, L * 3 * NCH])
            self.cs = self.sb("cs", [128, NCH, 2])
            self.csb = self.sb("csb", [128, NCH, 2], BF16)
            self.pos = self.sb("pos_sb", [128, 160])
            self.ARENA = 76 * 1024
            self.arena = self.sb("arena", [128, self.ARENA // 4])
            self.arena_names = []
            self.arena_off = {}
            self.psall = stack.enter_context(nc.psum_tensor("psall", [128, 8 * 512], F32))
            self.ps = [self.psall[:, i * 512:(i + 1) * 512] for i in range(8)]

            kb.op("dve", lambda e: e.memset(self.ones_f[:], 1.0), writes=[("ones_f",)])
            kb.op("dve", lambda e: e.memset(self.ones_b[:], 1.0), writes=[("ones_b",)])
            kb.op("dve", lambda e: e.memset(self.uT[:, :, T:cfg.TA], 0.0), writes=[("uT_pad",)])
            kb.dma("sp", self.ng[:], d["norm_g"], writes=[("ng",)])
            kb.dma("sp", self.cs[:], d["cT"], writes=[("cs",)])
            kb.dma("sp", self.pos[:], d["pos"], writes=[("pos",)])
            kb.dma("sp", self.ident[:], d["ident"], writes=[("ident",)])
            for c in range(NCH):
                kb.dma("sp", self.xT[:, c, :], d["xT"][:, c, :], writes=[("xT", c, ti) for ti in range(NT)])
            kb.op("act", lambda e: e.activation(out=self.csb[:], in_=self.cs[:], func=AF.Silu),
                  reads=[("cs",)], writes=[("csb",)])

            ia = ib = ic = 0
            self.set_layer(0)
            ada0 = self.adaln(0)
            for _ in range(8):
                next(ada0)
            for l in range(L):
                self.set_layer(l)
                self.ffn(l, 0, ada_next=(ada0 if l == 0 else None))
                kind = cfg.kinds[l]
                if kind == 0:
                    self.mixer_lru(l, ia)
                    ia += 1
                elif kind == 1:
                    self.mixer_pool(l, ib)
                    ib += 1
                elif kind == 2:
                    self.mixer_attn(l, ic)
                    ic += 1
                self.ffn(l, 1, ada_next=(self.adaln(l + 1) if l + 1 < L else None))

            for c in range(NCH):
                self.final.append(kb.dma("sp", d["yT"][:, c, :], self.xT[:, c, :],
                                         reads=[("xT", c, ti) for ti in range(NT)]))
            kb.finish(self.final)
        return nc

    def set_layer(self, l):
        i = l % 2
        self.mod, self.gs, self.gh = self.mods[i], self.gss[i], self.ghs[i]
        self.kmod, self.kgs, self.kgh = ("mod", i), ("gs", i), ("gh", i)

    def adaln(self, l):
        kb, d = self.kb, self.d
        i2 = l % 2
        mod, adab, gs, gh = self.mods[i2], self.adabs[i2], self.gss[i2], self.ghs[i2]
        kmod, kgs, kgh, kadab = ("mod", i2), ("gs", i2), ("gh", i2), ("adab", i2)
        kb.dma("sp", adab[:], d["ada_b"][l], writes=[kadab])
        for grp in range(9):
            bank = 6 + grp % 2
            pb = self.ps[bank]
            for st4 in range(4):
                st = grp * 4 + st4
                w, wk = self.rload(d["ada_w"][l, st], [128, NCH, 256])
                for jj in range(2):
                    jl = st4 * 2 + jj
                    for kc in range(NCH):
                        kb.op("pe", lambda e, w=w, kc=kc, jj=jj, jl=jl, pb=pb: e.matmul(
                            pb[:, jl * 2: jl * 2 + 2], lhsT=w[:, kc, jj * 128:(jj + 1) * 128], rhs=self.csb[:, kc, :],
                            start=(kc == 0), stop=(kc == NCH - 1)),
                            reads=[wk, ("csb",)], writes=[("ps", bank)])
                if st4 == 3:
                    j0 = grp * 8
                    kb.op("dve", lambda e, pb=pb, j0=j0: e.tensor_tensor(
                        out=mod[:, j0:j0 + 8, :], in0=pb[:, 0:16].rearrange("p (a b) -> p a b", b=2),
                        in1=adab[:, j0:j0 + 8].unsqueeze(2).to_broadcast([128, 8, 2]), op=ALU.add),
                        reads=[("ps", bank), kadab], writes=[kmod])
                    v = grp
                    if v in (1, 4, 7):
                        i = (v - 1) // 3
                        kb.op("dve", lambda e, i=i, v=v: e.scalar_tensor_tensor(
                            out=gs[:, i, :, :], in0=mod[:, v * 8:(v + 1) * 8, :], scalar=1.0,
                            in1=self.ng[:, (l * 3 + i) * 8:(l * 3 + i + 1) * 8].unsqueeze(2).to_broadcast([128, 8, 2]),
                            op0=ALU.add, op1=ALU.mult), reads=[kmod, ("ng",)], writes=[kgs])
                    if v in (2, 8):
                        i = 0 if v == 2 else 1
                        kb.op("dve", lambda e, i=i, v=v: e.tensor_scalar(
                            out=gh[:, i, :, :], in0=mod[:, v * 8:(v + 1) * 8, :], scalar1=0.5, scalar2=None,
                            op0=ALU.mult), reads=[kmod], writes=[kgh])
                yield

    def norm_stats(self, V, ti, rs):
        kb = self.kb
        t0, n, s = self.cfg.tiles[ti]
        pb = self.ps[6]
        for c in range(NCH):
            k = self.nxt() % 2
            sq = V["nsq"][:, k, :n]
            kb.op("act", lambda e, sq=sq, c=c: e.activation(out=sq, in_=self.xT[:, c, t0:t0 + n], func=AF.Square),
                  reads=[("xT", c, ti)], writes=[("nsq", k)])
            kb.op("pe", lambda e, sq=sq, c=c: e.matmul(pb[:, :n], lhsT=self.ones_f[:], rhs=sq,
                                                       start=(c == 0), stop=(c == NCH - 1)),
                  reads=[("nsq", k), ("ones_f",)], writes=[("ps", 6)])
        return pb

    def norm_to_uT(self, V, ni, tail=None):
        kb, cfg = self.kb, self.cfg
        shv = 3 * ni
        NT = len(cfg.tiles)
        for ti, (t0, n, s) in enumerate(cfg.tiles):
            pb = self.norm_stats(V, ti, None)
            rk = ("nrstd", ti % 2)
            rs = V["nrstd"][:, ti % 2, :n]
            kb.op("act", lambda e, rs=rs, pb=pb, n=n: e.activation(out=rs, in_=pb[:, :n], func=AF.Sqrt,
                                                                   scale=1.0 / D, bias=EPS),
                  reads=[("ps", 6)], writes=[rk])
            kb.op("dve", lambda e, rs=rs: e.reciprocal(out=rs, in_=rs), reads=[rk], writes=[rk])
            for c in range(NCH):
                k = self.nxt() % 2
                tmp = V["ntmp"][:, k, :n]
                kb.op("dve", lambda e, tmp=tmp, c=c, rs=rs, t0=t0, n=n, s=s: e.scalar_tensor_tensor(
                    out=tmp, in0=self.xT[:, c, t0:t0 + n], scalar=self.gs[:, ni, c, s:s + 1], in1=rs,
                    op0=ALU.mult, op1=ALU.mult),
                    reads=[("xT", c, ti), self.kgs, rk], writes=[("ntmp", k)])
                kb.op("act", lambda e, tmp=tmp, c=c, t0=t0, n=n, s=s: e.activation(
                    out=self.uT[:, c, t0:t0 + n], in_=tmp, func=AF.Identity,
                    bias=self.mod[:, shv * 8 + c, s:s + 1], scale=1.0),
                    reads=[("ntmp", k), self.kmod], writes=[("uT", c, ti)])
                if tail is not None and ti == NT - 2:
                    kb.op("act", lambda e, tmp=tmp, c=c, n=n, s=s: e.activation(
                        out=tail[:, c, :], in_=tmp[:, n - 15:n], func=AF.Identity,
                        bias=self.mod[:, shv * 8 + c, s:s + 1], scale=1.0),
                        reads=[("ntmp", k), self.kmod], writes=[("utail",)])

    NORM_SPEC = [("nsq", [128, 2, 512], F32), ("nrstd", [128, 2, 512], F32), ("ntmp", [128, 2, 512], F32)]

    def ffn(self, l, i, ada_next=None):
        kb, cfg, d = self.kb, self.cfg, self.d

        def pull(nst):
            if ada_next is None:
                return
            for _ in range(nst):
                try:
                    next(ada_next)
                except StopIteration:
                    return
        T = cfg.T
        V = self.carve(self.NORM_SPEC + [("hT", [128, 6, T], BF16), ("sa", [128, 2, 512], F32)])
        self.norm_to_uT(V, 0 if i == 0 else 2)
        hT, sa = V["hT"], V["sa"]
        for (j0, j1) in FFN_GROUPS:
            for j in range(j0, j1):
                w, wk = self.rload(d["ffn_w_in"][l, i, j], [128, NCH, 256])
                for ti, (t0, n, s) in enumerate(cfg.tiles):
                    k = self.nxt() % 2
                    ba, bb = k, 2 + k
                    pa, pb = self.ps[ba], self.ps[bb]
                    for half, (bank, pp) in enumerate(((ba, pa), (bb, pb))):
                        for kc in range(NCH):
                            kb.op("pe", lambda e, w=w, kc=kc, pp=pp, half=half, t0=t0, n=n: e.matmul(
                                pp[:, :n], lhsT=w[:, kc, half * 128:(half + 1) * 128], rhs=self.uT[:, kc, t0:t0 + n],
                                start=(kc == 0), stop=(kc == NCH - 1)),
                                reads=[wk, ("uT", kc, ti)], writes=[("ps", bank)])
                    kb.op("act", lambda e, pa=pa, k=k, n=n: e.activation(out=sa[:, k, :n], in_=pa[:, :n], func=AF.Silu),
                          reads=[("ps", ba)], writes=[("sa", k)])
                    kb.op("dve", lambda e, pb=pb, k=k, n=n, j=j, t0=t0: e.tensor_tensor(
                        out=hT[:, j - j0, t0:t0 + n], in0=sa[:, k, :n], in1=pb[:, :n], op=ALU.mult),
                        reads=[("sa", k), ("ps", bb)], writes=[("hT", j - j0, ti)])
                pull(2)
            w2 = []
            for jp in range(j0 // 2, j1 // 2):
                w2.append(self.rload(d["ffn_w_out"][l, i, jp], [128, 2, D]))
            ng_ = j1 - j0
            for ti, (t0, n, s) in enumerate(cfg.tiles):
                for dc in range(NCH):
                    k = self.nxt() % 2
                    bank = 4 + k
                    py = self.ps[bank]
                    for jj in range(ng_):
                        wv, wk = w2[jj // 2]
                        kb.op("pe", lambda e, wv=wv, jj=jj, dc=dc, py=py, t0=t0, n=n: e.matmul(
                            py[:, :n], lhsT=wv[:, jj % 2, dc * 128:(dc + 1) * 128], rhs=hT[:, jj, t0:t0 + n],
                            start=(jj == 0), stop=(jj == ng_ - 1)),
                            reads=[wk, ("hT", jj, ti)], writes=[("ps", bank)])
                    kb.op("dve", lambda e, py=py, dc=dc, t0=t0, n=n, s=s: e.scalar_tensor_tensor(
                        out=self.xT[:, dc, t0:t0 + n], in0=py[:, :n], scalar=self.gh[:, i, dc, s:s + 1],
                        in1=self.xT[:, dc, t0:t0 + n], op0=ALU.mult, op1=ALU.add),
                        reads=[("ps", bank), self.kgh, ("xT", dc, ti)], writes=[("xT", dc, ti)])
        pull(1000)

    def halo_exchange(self, V, tail):
        kb = self.kb
        hal, hsel = V["hal"], V["hsel"]
        kb.dma("sp", self.ag_h_in.ap(), tail.rearrange("p a b -> p (a b)"), reads=[("utail",)], writes=[("ag_h_in",)])
        kb.collective(self.groups, self.ag_h_in.ap().opt(), self.ag_h_out.ap().opt(),
                      reads=[("ag_h_in",)], writes=[("ag_h_out",)])
        kb.dma("sp", hal, self.ag_h_out.ap().rearrange("(r p) f -> p r f", p=128), reads=[("ag_h_out",)],
               writes=[("hal",)])
        kb.op("dve", lambda e: e.tensor_scalar(out=hsel, in0=hal[:, 0, :], scalar1=self.pos[:, 0:1], scalar2=None,
                                               op0=ALU.mult), reads=[("hal",), ("pos",)], writes=[("hsel",)])
        for q in range(1, 4):
            kb.op("dve", lambda e, q=q: e.scalar_tensor_tensor(
                out=hsel, in0=hal[:, q, :], scalar=self.pos[:, q:q + 1], in1=hsel, op0=ALU.mult, op1=ALU.add),
                reads=[("hal",), ("pos",), ("hsel",)], writes=[("hsel",)])
        return hsel

    def mixer_lru(self, l, j):
        kb, cfg, d = self.kb, self.cfg, self.d
        NP, T = cfg.NP, cfg.T
        C = NP + 38
        PS0, SS0 = 3, NP + 6
        SMALL = [("lsm", [BS, NBLK, 12], F32), ("coef", [BS, NBLK], F32), ("gw", [BS, NBLK, 160], BF16),
                 ("utail", [128, NCH, 15], F32), ("uh", [128, NCH, 4], BF16), ("hal", [128, 4, 120], F32),
                 ("hsel", [128, 120], F32), ("cin", [BS, NBLK, 2], F32), ("cg", [BS, 4, 32], F32),
                 ("Hc", [BS, 4, NBLK], F32), ("h0p", [BS, NBLK], F32), ("lo", [BS, NBLK, 8], F32)]
        smalln = [x[0] for x in SMALL]
        V = self.carve(SMALL + self.NORM_SPEC)
        lsm, coef, gw = V["lsm"], V["coef"], V["gw"]
        lo, cin, h0p = V["lo"], V["cin"], V["h0p"]
        uh = V["uh"]
        self.norm_to_uT(V, 1, tail=V["utail"])
        hsel = self.halo_exchange(V, V["utail"])
        kb.op("dve", lambda e: e.tensor_copy(out=uh[:, :, 0:3],
                                             in_=hsel.rearrange("p (c t) -> p c t", t=15)[:, :, 12:15]),
              reads=[("hsel",)], writes=[("uh",)])
        kb.dma("sp", lsm, d["lru_small"][j], writes=[("lsm",)])
        kb.dma("pool", gw, d["lru_g"][j], writes=[("gw",)])
        kb.op("act", lambda e: e.activation(out=coef, in_=lsm[:, :, 7], func=AF.Exp, scale=-1.0),
              reads=[("lsm",)], writes=[("coef",)])
        kb.op("act", lambda e: e.activation(out=coef, in_=coef, func=AF.Ln, bias=1.0), reads=[("coef",)], writes=[("coef",)])
        kb.op("dve", lambda e: e.tensor_scalar(out=coef, in0=coef, scalar1=-8.0, scalar2=None, op0=ALU.mult),
              reads=[("coef",)], writes=[("coef",)])
        V1 = self.carve(SMALL + [("W0a", [BS, C], F32), ("W1a", [BS, C], F32), ("W2a", [BS, C], F32),
                                 ("xcba", [BS, C], BF16), ("W0b", [BS, C], F32), ("W1b", [BS, C], F32),
                                 ("W2b", [BS, C], F32), ("xcbb", [BS, C], BF16), ("rs4", [BS, 2, 8], F32),
                                 ("rsum", [BS, 2], F32)], keep=smalln)
        sets1 = [dict(W0=V1["W0" + t], W1=V1["W1" + t], W2=V1["W2" + t], xcb=V1["xcb" + t],
                      rs4=V1["rs4"][:, i_, :], rsum=V1["rsum"][:, i_:i_ + 1],
                      kW0=("W0" + t,), kW1=("W1" + t,), kW2=("W2" + t,), kxcb=("xcb" + t,),
                      krs4=("rs4", i_), krsum=("rsum", i_)) for i_, t in enumerate("ab")]

        pieces = [(PS0 + t0, n, t0) for (t0, n, s) in cfg.tiles[:-1]]
        tailpiece = (PS0 + NP, 38)
        npt = len(pieces)

        def block2(hb):
            S_ = sets2[hb % 2]
            W0, W2, kW0, kW2 = S_["W0"], S_["W2"], S_["kW0"], S_["kW2"]
            W1, zT, ggt = P2["W1"], P2["zT"], P2["ggt"]
            w, wk = self.rload(d["lru_win"][j, hb][:, :, 0:80], [128, NCH, 80])
            kb.dma("sp", W2[:, 3:C], self.lru_a.ap()[hb][:, 3:C], reads=[("lra", hb)], writes=[kW2])
            kb.dma("sp", W0[:, 3:C], self.lru_b.ap()[hb][:, 3:C], reads=[("lrb", hb)], writes=[kW0])
            kb.op("dve", lambda e: e.tensor_tensor_scan(out=W1[:, PS0:PS0 + NP], data0=W2[:, PS0:PS0 + NP],
                                                        data1=W0[:, PS0:PS0 + NP], initial=h0p[:, hb:hb + 1],
                                                        op0=ALU.mult, op1=ALU.add),
                  reads=[kW2, kW0, ("h0p",)], writes=[("W1",)])
            kb.op("dve", lambda e: e.tensor_tensor_scan(out=W1[:, SS0:SS0 + 32], data0=W2[:, SS0:SS0 + 32],
                                                        data1=W0[:, SS0:SS0 + 32], initial=lsm[:, hb, 8:9],
                                                        op0=ALU.mult, op1=ALU.add),
                  reads=[kW2, kW0, ("lsm",)], writes=[("W1",)])
            kb.op("dve", lambda e: e.tensor_copy(out=lo[:, hb, 0:1], in_=W1[:, PS0 + NP - 1:PS0 + NP]),
                  reads=[("W1",)], writes=[("lo",)])
            kb.op("dve", lambda e: e.tensor_copy(out=lo[:, hb, 1:2], in_=W1[:, C - 1:C]),
                  reads=[("W1",)], writes=[("lo",)])
            zsrc = [(PS0 + t0, n, t0, ti) for ti, (t0, n, s) in enumerate(cfg.tiles[:-1])]
            zsrc.append((SS0, 32, NP, len(cfg.tiles) - 1))
            for (c0, n, t0, ti) in zsrc:
                k = self.nxt() % 2
                pp = self.ps[k]
                for kc in range(NCH):
                    kb.op("pe", lambda e, pp=pp, kc=kc, n=n, t0=t0: e.matmul(
                        pp[0:BS, :n], lhsT=w[:, kc, 0:80], rhs=self.uT[:, kc, t0:t0 + n],
                        start=(kc == 0), stop=(kc == NCH - 1)),
                        reads=[wk, ("uT", kc, ti)], writes=[("ps", k)])
                kb.op("act", lambda e, pp=pp, k=k, n=n: e.activation(out=ggt[:, k, :n], in_=pp[0:BS, :n],
                                                                     func=AF.Gelu),
                      reads=[("ps", k)], writes=[("ggt", k)])
                kb.op("dve", lambda e, k=k, n=n, c0=c0, t0=t0: e.tensor_tensor(
                    out=zT[:, hb % 4, t0:t0 + n], in0=ggt[:, k, :n], in1=W1[:, c0:c0 + n], op=ALU.mult),
                    reads=[("ggt", k), ("W1",)], writes=[("zT", hb % 4, ti)])

        def blockA(hb):
            S_ = sets1[hb % 2]
            W0, W1, W2, xcb, rs4, rsum = S_["W0"], S_["W1"], S_["W2"], S_["xcb"], S_["rs4"], S_["rsum"]
            kW0, kW1, kW2, kxcb, krs4, krsum = S_["kW0"], S_["kW1"], S_["kW2"], S_["kxcb"], S_["krs4"], S_["krsum"]
            w, wk = self.rload(d["lru_win"][j, hb][:, :, 80:160], [128, NCH, 80])
            xsrc = [(PS0 + t0, n, self.uT, t0, ti) for ti, (t0, n, s) in enumerate(cfg.tiles[:-1])]
            xsrc.append((SS0, 32, self.uT, NP, len(cfg.tiles) - 1))
            xsrc.append((0, 3, uh, 0, None))
            for (c0, n, src, s0, ti) in xsrc:
                k = self.nxt() % 2
                pp = self.ps[4 + k]
                for kc in range(NCH):
                    kb.op("pe", lambda e, pp=pp, kc=kc, n=n, src=src, s0=s0: e.matmul(
                        pp[0:BS, :n], lhsT=w[:, kc, 0:80], rhs=src[:, kc, s0:s0 + n],
                        start=(kc == 0), stop=(kc == NCH - 1)),
                        reads=[wk, ("uT", kc, ti) if ti is not None else ("uh",)], writes=[("ps", 4 + k)])
                kb.op("act", lambda e, pp=pp, c0=c0, n=n: e.activation(out=W0[:, c0:c0 + n], in_=pp[0:BS, :n],
                                                                       func=AF.Identity),
                      reads=[("ps", 4 + k)], writes=[kW0])
            kb.op("dve", lambda e: e.tensor_copy(out=W0[:, NP + 3:NP + 6], in_=lsm[:, hb, 9:12]),
                  reads=[("lsm",)], writes=[kW0])
            kb.op("dve", lambda e: e.tensor_copy(out=lo[:, hb, 2:5], in_=W0[:, NP:NP + 3]),
                  reads=[kW0], writes=[("lo",)])
            kb.op("dve", lambda e: e.tensor_copy(out=lo[:, hb, 5:8], in_=W0[:, C - 3:C]),
                  reads=[kW0], writes=[("lo",)])
            kb.op("act", lambda e: e.activation(out=W1[:, 3:C], in_=W0[:, 0:C - 3], func=AF.Identity,
                                                scale=lsm[:, hb, 0:1], bias=lsm[:, hb, 4:5]),
                  reads=[kW0, ("lsm",)], writes=[kW1])
            for kk in range(1, 4):
                kb.op("dve", lambda e, kk=kk: e.scalar_tensor_tensor(
                    out=W1[:, 3:C], in0=W0[:, kk:C - 3 + kk], scalar=lsm[:, hb, kk:kk + 1], in1=W1[:, 3:C],
                    op0=ALU.mult, op1=ALU.add), reads=[kW0, kW1, ("lsm",)], writes=[kW1])
            kb.op("act", lambda e: e.activation(out=xcb[:, 3:C], in_=W1[:, 3:C], func=AF.Identity),
                  reads=[kW1], writes=[kxcb])

        def blockB(hb, full):
            S_ = sets1[hb % 2]
            W0, W1, W2, xcb, rs4, rsum = S_["W0"], S_["W1"], S_["W2"], S_["xcb"], S_["rs4"], S_["rsum"]
            kW0, kW1, kW2, kxcb, krs4, krsum = S_["kW0"], S_["kW1"], S_["kW2"], S_["kxcb"], S_["krs4"], S_["krsum"]
            gp = [(PS0 + t0, n) for (t0, n, s) in cfg.tiles[:-1]] + [(PS0 + NP, 35)]
            for gi, (c0, n) in enumerate(gp):
                k = self.nxt() % 2
                pr, pi = self.ps[k], self.ps[2 + k]
                kb.op("pe", lambda e, pr=pr, c0=c0, n=n: e.matmul(pr[0:BS, :n], lhsT=gw[:, hb, 0:80],
                                                                 rhs=xcb[:, c0:c0 + n], start=True, stop=True),
                      reads=[("gw",), kxcb], writes=[("ps", k)])
                kb.op("pe", lambda e, pi=pi, c0=c0, n=n: e.matmul(pi[0:BS, :n], lhsT=gw[:, hb, 80:160],
                                                                 rhs=xcb[:, c0:c0 + n], start=True, stop=True),
                      reads=[("gw",), kxcb], writes=[("ps", 2 + k)])
                if (not full) and gi < npt:
                    kb.op("act", lambda e, pr=pr, c0=c0, n=n, gi=gi: e.activation(
                        out=W2[:, c0:c0 + n], in_=pr[0:BS, :n], func=AF.Sigmoid, bias=lsm[:, hb, 5:6], scale=1.0,
                        accum_out=rs4[:, gi:gi + 1]),
                        reads=[("ps", k), ("lsm",)], writes=[kW2, krs4])
                else:
                    kb.op("act", lambda e, pr=pr, c0=c0, n=n: e.activation(
                        out=W2[:, c0:c0 + n], in_=pr[0:BS, :n], func=AF.Sigmoid, bias=lsm[:, hb, 5:6], scale=1.0),
                        reads=[("ps", k), ("lsm",)], writes=[kW2])
                kb.op("act", lambda e, pi=pi, c0=c0, n=n: e.activation(
                    out=W0[:, c0:c0 + n], in_=pi[0:BS, :n], func=AF.Sigmoid, bias=lsm[:, hb, 6:7], scale=1.0),
                    reads=[("ps", 2 + k), ("lsm",), ("lo",)], writes=[kW0])
            kb.op("dve", lambda e: e.tensor_tensor(out=W0[:, 3:C], in0=W0[:, 3:C], in1=W1[:, 3:C], op=ALU.mult),
                  reads=[kW0, kW1], writes=[kW0])
            kb.op("act", lambda e: e.activation(out=W2[:, 3:C], in_=W2[:, 3:C], func=AF.Exp, scale=coef[:, hb:hb + 1]),
                  reads=[kW2, ("coef",)], writes=[kW2])
            kb.op("dve", lambda e: e.tensor_tensor(out=W1[:, 3:C], in0=W2[:, 3:C], in1=W2[:, 3:C], op=ALU.mult),
                  reads=[kW2, kW1, kxcb], writes=[kW1])
            kb.op("act", lambda e: e.activation(out=W1[:, 3:C], in_=W1[:, 3:C], func=AF.Sqrt, scale=-1.0, bias=1.0),
                  reads=[kW1], writes=[kW1])
            kb.op("dve", lambda e: e.tensor_tensor(out=W0[:, 3:C], in0=W0[:, 3:C], in1=W1[:, 3:C], op=ALU.mult),
                  reads=[kW0, kW1], writes=[kW0])
            if full:
                kb.op("dve", lambda e: e.tensor_tensor_scan(out=W1[:, PS0:PS0 + NP], data0=W2[:, PS0:PS0 + NP],
                                                            data1=W0[:, PS0:PS0 + NP], initial=h0p[:, hb:hb + 1],
                                                            op0=ALU.mult, op1=ALU.add),
                      reads=[kW2, kW0, ("h0p",)], writes=[kW1])
                kb.op("dve", lambda e: e.tensor_tensor_scan(out=W1[:, SS0:SS0 + 32], data0=W2[:, SS0:SS0 + 32],
                                                            data1=W0[:, SS0:SS0 + 32], initial=lsm[:, hb, 8:9],
                                                            op0=ALU.mult, op1=ALU.add),
                      reads=[kW2, kW0, ("lsm",)], writes=[kW1])
                kb.op("dve", lambda e: e.tensor_copy(out=lo[:, hb, 0:1], in_=W1[:, PS0 + NP - 1:PS0 + NP]),
                      reads=[kW1], writes=[("lo",)])
                kb.op("dve", lambda e: e.tensor_copy(out=lo[:, hb, 1:2], in_=W1[:, C - 1:C]),
                      reads=[kW1], writes=[("lo",)])
                zsrc = [(PS0 + t0, n, t0, ti) for ti, (t0, n, s) in enumerate(cfg.tiles[:-1])]
                zsrc.append((SS0, 32, NP, len(cfg.tiles) - 1))
                for (c0, n, t0, ti) in zsrc:
                    k = self.nxt() % 2
                    pp = self.ps[k]
                    for kc in range(NCH):
                        kb.op("pe", lambda e, pp=pp, kc=kc, n=n, t0=t0: e.matmul(
                            pp[0:BS, :n], lhsT=w[:, kc, 0:80], rhs=self.uT[:, kc, t0:t0 + n],
                            start=(kc == 0), stop=(kc == NCH - 1)),
                            reads=[wk, ("uT", kc, ti)], writes=[("ps", k)])
                    kb.op("act", lambda e, pp=pp, k=k, n=n: e.activation(out=V["ggt"][:, k, :n], in_=pp[0:BS, :n],
                                                                         func=AF.Gelu),
                          reads=[("ps", k)], writes=[("ggt", k)])
                    kb.op("dve", lambda e, k=k, n=n, c0=c0, t0=t0: e.tensor_tensor(
                        out=zT[:, hb % 4, t0:t0 + n], in0=V["ggt"][:, k, :n], in1=W1[:, c0:c0 + n], op=ALU.mult),
                        reads=[("ggt", k), kW1], writes=[("zT", hb % 4, ti)])
            else:
                kb.op("dve", lambda e: e.tensor_tensor_scan(out=W1[:, PS0:PS0 + NP], data0=W2[:, PS0:PS0 + NP],
                                                            data1=W0[:, PS0:PS0 + NP], initial=0.0,
                                                            op0=ALU.mult, op1=ALU.add),
                      reads=[kW2, kW0], writes=[kW1])
                kb.op("dve", lambda e: e.tensor_copy(out=cin[:, hb, 1:2], in_=W1[:, PS0 + NP - 1:PS0 + NP]),
                      reads=[kW1], writes=[("cin",)])
                kb.op("dve", lambda e: e.tensor_reduce(out=rsum, in_=rs4[:, 0:npt], axis=AX.X, op=ALU.add),
                      reads=[krs4], writes=[krsum])
                kb.op("act", lambda e: e.activation(out=cin[:, hb, 0:1], in_=rsum, func=AF.Exp,
                                                    scale=coef[:, hb:hb + 1]),
                      reads=[krsum, ("coef",)], writes=[("cin",)])
                kb.dma("sp", self.lru_a.ap()[hb][:, 3:C], W2[:, 3:C], reads=[kW2], writes=[("lra", hb)])
                kb.dma("sp", self.lru_b.ap()[hb][:, 3:C], W0[:, 3:C], reads=[kW0], writes=[("lrb", hb)])

        blockA(0)
        for hb in range(NBLK):
            if hb + 1 < NBLK:
                blockA(hb + 1)
            blockB(hb, False)
        cg, Hc = V["cg"], V["Hc"]
        kb.dma("sp", self.ag_c_in.ap(), cin.rearrange("p a b -> p (a b)"), reads=[("cin",)], writes=[("ag_c_in",)])
        kb.collective(self.groups, self.ag_c_in.ap().opt(), self.ag_c_out.ap().opt(),
                      reads=[("ag_c_in",)], writes=[("ag_c_out",)])
        kb.dma("sp", cg, self.ag_c_out.ap().rearrange("(r p) f -> p r f", p=BS), reads=[("ag_c_out",)], writes=[("cg",)])
        cg4 = cg.rearrange("p r (b t) -> p r b t", t=2)
        kb.op("dve", lambda e: e.tensor_copy(out=Hc[:, 1, :], in_=cg4[:, 0, :, 1]), reads=[("cg",)], writes=[("Hc",)])
        for q in (1, 2):
            kb.op("dve", lambda e, q=q: e.tensor_tensor(out=Hc[:, q + 1, :], in0=cg4[:, q, :, 0], in1=Hc[:, q, :],
                                                        op=ALU.mult), reads=[("cg",), ("Hc",)], writes=[("Hc",)])
            kb.op("dve", lambda e, q=q: e.tensor_tensor(out=Hc[:, q + 1, :], in0=Hc[:, q + 1, :], in1=cg4[:, q, :, 1],
                                                        op=ALU.add), reads=[("cg",), ("Hc",)], writes=[("Hc",)])
        kb.op("dve", lambda e: e.tensor_scalar(out=h0p, in0=Hc[:, 1, :], scalar1=self.pos[0:BS, 5:6], scalar2=None,
                                               op0=ALU.mult), reads=[("Hc",), ("pos",)], writes=[("h0p",)])
        for q in (2, 3):
            kb.op("dve", lambda e, q=q: e.scalar_tensor_tensor(
                out=h0p, in0=Hc[:, q, :], scalar=self.pos[0:BS, 4 + q:5 + q], in1=h0p, op0=ALU.mult, op1=ALU.add),
                reads=[("Hc",), ("pos",), ("h0p",)], writes=[("h0p",)])
        P2 = self.carve(SMALL + [("W0a", [BS, C], F32), ("W2a", [BS, C], F32), ("W0b", [BS, C], F32),
                                 ("W2b", [BS, C], F32), ("W1", [BS, C], F32), ("ggt", [BS, 2, 512], F32),
                                 ("zT", [BS, 4, T], BF16)], keep=smalln)
        sets2 = [dict(W0=P2["W0" + t], W2=P2["W2" + t], kW0=("W0" + t,), kW2=("W2" + t,)) for t in "ab"]
        zT = P2["zT"]
        for hb in range(NBLK):
            block2(hb)
            if hb % 4 == 3:
                w2 = [self.rload(d["lru_wout"][j, (hb - 3) // 2 + q], [BS, 2, D]) for q in range(2)]
                for ti, (t0, n, s) in enumerate(cfg.tiles):
                    for dc in range(NCH):
                        k = self.nxt() % 2
                        bank = 4 + k
                        py = self.ps[bank]
                        for b in range(4):
                            wv, wk2 = w2[b // 2]
                            kb.op("pe", lambda e, wv=wv, b=b, dc=dc, py=py, t0=t0, n=n: e.matmul(
                                py[:, :n], lhsT=wv[:, b % 2, dc * 128:(dc + 1) * 128], rhs=zT[:, b, t0:t0 + n],
                                start=(b == 0), stop=(b == 3)),
                                reads=[wk2, ("zT", b, ti)], writes=[("ps", bank)])
                        kb.op("dve", lambda e, py=py, dc=dc, t0=t0, n=n, s=s: e.scalar_tensor_tensor(
                            out=self.xT[:, dc, t0:t0 + n], in0=py[:, :n], scalar=self.mod[:, 40 + dc, s:s + 1],
                            in1=self.xT[:, dc, t0:t0 + n], op0=ALU.mult, op1=ALU.add),
                            reads=[("ps", bank), self.kmod, ("xT", dc, ti)], writes=[("xT", dc, ti)])
        self.final.append(kb.dma("sp", d["lru_o"][j], lo, reads=[("lo",)]))

    def mixer_pool(self, l, j):
        kb, cfg, d = self.kb, self.cfg, self.d
        NP, T = cfg.NP, cfg.T
        NT = len(cfg.tiles)
        CE = NP + 62
        V = self.carve([("nsq", [128, 2, 512], F32), ("rall", [128, T], F32), ("ue", [128, 2, CE], F32),
                        ("Sa", [128, CE], F32), ("Sb", [128, CE], F32), ("utail", [128, NCH, 15], F32),
                        ("hal", [128, 4, 120], F32), ("hsel", [128, 120], F32), ("psm", [128, 136], F32),
                        ("pk", [128, 2, NCH, 2], F32), ("pso", [128, NCH, 30], F32), ("t15", [128, 16], F32),
                        ("ptmp", [128, 2, 512], F32)])
        rall, ue, utail, psm, pk, pso = V["rall"], V["ue"], V["utail"], V["psm"], V["pk"], V["pso"]
        kb.dma("sp", psm, d["pool_small"][j], writes=[("psm",)])
        kb.op("dve", lambda e: e.tensor_tensor(out=pk[:, 0, :, :], in0=self.mod[:, 40:48, :],
                                               in1=psm[:, 8:16].unsqueeze(2).to_broadcast([128, 8, 2]), op=ALU.mult),
              reads=[self.kmod, ("psm",)], writes=[("pk",)])
        kb.op("dve", lambda e: e.tensor_tensor(out=pk[:, 1, :, :], in0=pk[:, 0, :, :],
                                               in1=psm[:, 0:8].unsqueeze(2).to_broadcast([128, 8, 2]), op=ALU.mult),
              reads=[("pk",), ("psm",)], writes=[("pk",)])
        for ti, (t0, n, s) in enumerate(cfg.tiles):
            pb = self.norm_stats(V, ti, None)
            kb.op("act", lambda e, pb=pb, t0=t0, n=n: e.activation(out=rall[:, t0:t0 + n], in_=pb[:, :n], func=AF.Sqrt,
                                                                   scale=1.0 / D, bias=EPS),
                  reads=[("ps", 6)], writes=[("rall", ti)])
            kb.op("dve", lambda e, t0=t0, n=n: e.reciprocal(out=rall[:, t0:t0 + n], in_=rall[:, t0:t0 + n]),
                  reads=[("rall", ti)], writes=[("rall", ti)])
        for c in range(NCH):
            kb.op("dve", lambda e, c=c: e.scalar_tensor_tensor(
                out=utail[:, c, :], in0=self.xT[:, c, NP - 15:NP], scalar=self.gs[:, 1, c, 0:1],
                in1=rall[:, NP - 15:NP], op0=ALU.mult, op1=ALU.mult),
                reads=[("xT", c, NT - 2), self.kgs, ("rall", NT - 2)], writes=[("utail",)])
            kb.op("act", lambda e, c=c: e.activation(out=utail[:, c, :], in_=utail[:, c, :], func=AF.Identity,
                                                     bias=self.mod[:, 24 + c, 0:1], scale=1.0),
                  reads=[("utail",), self.kmod], writes=[("utail",)])
        hsel = self.halo_exchange(V, utail)
        kb.op("dve", lambda e: e.tensor_copy(out=pso[:, :, 0:15], in_=utail), reads=[("utail",)], writes=[("pso",)])
        for c in range(NCH):
            g = c // 2
            wnd = 2 ** (g + 1)
            k = c % 2
            uc = ue[:, k, :]
            uk = ("ue", k)
            kb.op("dve", lambda e, uc=uc, c=c: e.tensor_copy(out=uc[:, 0:15], in_=hsel[:, c * 15:(c + 1) * 15]),
                  reads=[("hsel",)], writes=[uk])
            kb.op("dve", lambda e, uc=uc, c=c: e.tensor_copy(out=uc[:, 15 + NP:30 + NP],
                                                             in_=psm[:, 16 + c * 15:16 + (c + 1) * 15]),
                  reads=[("psm",)], writes=[uk])
            for (u0, x0, n, s) in ((15, 0, NP, 0), (30 + NP, NP, 32, 1)):
                kb.op("dve", lambda e, uc=uc, c=c, u0=u0, x0=x0, n=n, s=s: e.scalar_tensor_tensor(
                    out=V["Sa"][:, u0:u0 + n], in0=self.xT[:, c, x0:x0 + n], scalar=self.gs[:, 1, c, s:s + 1],
                    in1=rall[:, x0:x0 + n], op0=ALU.mult, op1=ALU.mult),
                    reads=[("xT", c, ti) for ti in range(NT)] + [self.kgs] + [("rall", ti) for ti in range(NT)],
                    writes=[("Sa",)])
                kb.op("act", lambda e, uc=uc, c=c, u0=u0, n=n, s=s: e.activation(
                    out=uc[:, u0:u0 + n], in_=V["Sa"][:, u0:u0 + n], func=AF.Identity,
                    bias=self.mod[:, 24 + c, s:s + 1], scale=1.0),
                    reads=[("Sa",), self.kmod], writes=[uk])
            kb.op("dve", lambda e, uc=uc, c=c: e.tensor_copy(out=pso[:, c, 15:30], in_=uc[:, CE - 15:CE]),
                  reads=[uk], writes=[("pso",)])
            cur, curk = uc, uk
            bufs = [(V["Sa"], ("Sa",)), (V["Sb"], ("Sb",))]
            step = 1
            bi = 0
            while step < wnd:
                nb_, nk = bufs[bi % 2]
                kb.op("dve", lambda e, cur=cur, nb_=nb_, step=step: e.tensor_tensor(
                    out=nb_[:, step:CE], in0=cur[:, step:CE], in1=cur[:, 0:CE - step], op=ALU.add),
                    reads=[curk], writes=[nk])
                if step > 1 or True:
                    pass
                cur, curk = nb_, nk
                step *= 2
                bi += 1
            fin, fink = cur, curk
            inv = 1.0 / wnd
            for (u0, x0, n, ti) in ((15, 0, NP, None), (30 + NP, NP, 32, NT - 1)):
                kb.op("dve", lambda e, fin=fin, uc=uc, c=c, u0=u0, x0=x0, n=n: e.scalar_tensor_tensor(
                    out=self.uT[:, c, x0:x0 + n], in0=fin[:, u0:u0 + n], scalar=inv, in1=uc[:, u0:u0 + n],
                    op0=ALU.mult, op1=ALU.subtract),
                    reads=[fink, uk], writes=[("uT", c, t) for t in (range(NT - 1) if ti is None else [ti])])
            kb.op("dve", lambda e, fin=fin, c=c: e.tensor_tensor(out=V["t15"][:, 0:15], in0=fin[:, 15:30],
                                                                 in1=self.pos[:, 16 + c * 15:16 + (c + 1) * 15],
                                                                 op=ALU.mult),
                  reads=[fink, ("pos",)], writes=[("t15",)])
            kb.op("dve", lambda e, uc=uc, c=c: e.scalar_tensor_tensor(
                out=self.uT[:, c, 0:15], in0=V["t15"][:, 0:15], scalar=inv, in1=uc[:, 15:30],
                op0=ALU.mult, op1=ALU.subtract), reads=[("t15",), uk], writes=[("uT", c, 0)])
        self.final.append(kb.dma("sp", d["pool_o"][j], pso, reads=[("pso",)]))
        for g in range(4):
            wg, wk = self.rload(d["pool_w"][j, g], [128, 2, 256])
            for ti, (t0, n, s) in enumerate(cfg.tiles):
                for oc in range(2):
                    k = self.nxt() % 2
                    bank = 4 + k
                    py = self.ps[bank]
                    cc = 2 * g + oc
                    for ic in range(2):
                        kb.op("pe", lambda e, py=py, ic=ic, oc=oc, t0=t0, n=n: e.matmul(
                            py[:, :n], lhsT=wg[:, ic, oc * 128:(oc + 1) * 128], rhs=self.uT[:, 2 * g + ic, t0:t0 + n],
                            start=(ic == 0), stop=(ic == 1)),
                            reads=[wk, ("uT", 2 * g + ic, ti)], writes=[("ps", bank)])
                    kb.op("act", lambda e, py=py, k=k, n=n, cc=cc, s=s: e.activation(
                        out=V["ptmp"][:, k, :n], in_=py[:, :n], func=AF.Identity, scale=pk[:, 0, cc, s:s + 1],
                        bias=pk[:, 1, cc, s:s + 1]), reads=[("ps", bank), ("pk",)], writes=[("ptmp", k)])
                    kb.op("dve", lambda e, k=k, n=n, cc=cc, t0=t0: e.tensor_tensor(
                        out=self.xT[:, cc, t0:t0 + n], in0=self.xT[:, cc, t0:t0 + n], in1=V["ptmp"][:, k, :n],
                        op=ALU.add), reads=[("ptmp", k), ("xT", cc, ti)], writes=[("xT", cc, ti)])

    def mixer_attn(self, l, j):
        kb, cfg, d = self.kb, self.cfg, self.d
        NP, T, TA = cfg.NP, cfg.T, cfg.TA
        NT = len(cfg.tiles)
        NTB = NP // 128
        NG = NP // 512
        lam_init = 0.8 - 0.6 * math.exp(-0.3 * l)
        KEEP = [("qT", [128, NHEAD, TA], BF16), ("ksT", [128, NHEAD, 128], BF16), ("vs", [128, NHEAD, 128], BF16),
                ("asm", [128, 517], F32), ("neglam", [128, 2], F32), ("subg", [128, 1], F32)]
        keepn = [k[0] for k in KEEP]
        TILES = [("T0", [128, NHEAD, 128], F32), ("T1", [128, NHEAD, 128], F32), ("Ts0", [128, NHEAD, 32], F32),
                 ("bcol", [128, NHEAD, 3], F32), ("col15", [128, NHEAD, 3], F32)]
        V = self.carve(KEEP + TILES + [("rbb", [128, 256], F32)] + self.NORM_SPEC + [
            ("sq", [128, 2, 256], F32), ("ss", [128, 2, 4], F32), ("qn", [128, 2, 256], F32),
            ("vb", [128, 2, 2, 512], BF16), ("kTt", [128, 2, 2, 512], BF16), ("lpr", [128, 2], F32)])
        qT, ksT, vs, asm = V["qT"], V["ksT"], V["vs"], V["asm"]
        lpr, neglam1, subg1 = V["lpr"], V["neglam"], V["subg"]
        self.norm_to_uT(V, 1)
        T0, T1, Ts0, bcol, col15, rbb = (V[k_] for k_ in ("T0", "T1", "Ts0", "bcol", "col15", "rbb"))
        kb.dma("sp", rbb, d["rb_rep"], writes=[("rbb",)])
        pslots = []
        for q4 in range(4):
            npl = 16 if q4 < 3 else 1
            pslots.append(self.rload(d["planes"][:, q4 * 2048:q4 * 2048 + npl * 128], [128, npl * 128], q="sp", reserve=True))
        pres = [int(pk_[1][1]) for pk_ in pslots]

        def plane(b_):
            v_, k_ = pslots[b_ // 16]
            return v_[:, (b_ % 16) * 128:(b_ % 16 + 1) * 128], k_
        tile_ops = []
        for h in range(NHEAD):
            pv, pk_ = plane(48)
            tile_ops.append(lambda h=h, pv=pv, pk_=pk_: kb.op("dve", lambda e: e.tensor_scalar(
                out=T0[:, h, :], in0=pv, scalar1=NEG, scalar2=None, op0=ALU.mult), reads=[pk_], writes=[("T0", h)]))
            for b in range(32):
                pv, pk_ = plane(b)
                tile_ops.append(lambda h=h, b=b, pv=pv, pk_=pk_: kb.op("dve", lambda e: e.scalar_tensor_tensor(
                    out=T0[:, h, :], in0=pv, scalar=rbb[:, b * 8 + h:b * 8 + h + 1], in1=T0[:, h, :],
                    op0=ALU.mult, op1=ALU.add), reads=[pk_, ("rbb",), ("T0", h)], writes=[("T0", h)]))
            pv, pk_ = plane(32)
            tile_ops.append(lambda h=h, pv=pv, pk_=pk_: kb.op("dve", lambda e: e.tensor_scalar(
                out=T1[:, h, :], in0=pv, scalar1=rbb[:, h:h + 1], scalar2=None, op0=ALU.mult),
                reads=[pk_, ("rbb",)], writes=[("T1", h)]))
            for b in range(1, 16):
                pv, pk_ = plane(32 + b)
                tile_ops.append(lambda h=h, b=b, pv=pv, pk_=pk_: kb.op("dve", lambda e: e.scalar_tensor_tensor(
                    out=T1[:, h, :], in0=pv, scalar=rbb[:, b * 8 + h:b * 8 + h + 1], in1=T1[:, h, :],
                    op0=ALU.mult, op1=ALU.add), reads=[pk_, ("rbb",), ("T1", h)], writes=[("T1", h)]))
        tile_ops.append(lambda: kb.op("dve", lambda e: e.memset(Ts0[:], NEG), writes=[("Ts0",)]))
        tile_ops.append(lambda: kb.op("dve", lambda e: e.tensor_copy(out=Ts0[0:32, :, :], in_=T0[0:32, :, 0:32]),
                                      reads=[("T0", h) for h in range(NHEAD)] + [("Ts0",)], writes=[("Ts0",)]))
        c15 = rbb[:, 120:128]
        for s_ in range(3):
            tile_ops.append(lambda s=s_: kb.op("dve", lambda e: e.tensor_scalar(
                out=bcol[:, :, s], in0=c15, scalar1=self.pos[:, 8 + s:9 + s], scalar2=None, op0=ALU.add),
                reads=[("rbb",), ("pos",)], writes=[("bcol",)]))
            tile_ops.append(lambda s=s_: kb.op("dve", lambda e: e.tensor_scalar(
                out=col15[:, :, s], in0=c15, scalar1=self.pos[:, 11 + s:12 + s], scalar2=None, op0=ALU.mult),
                reads=[("rbb",), ("pos",)], writes=[("col15",)]))
            tile_ops.append(lambda s=s_: kb.op("dve", lambda e: e.tensor_tensor(
                out=col15[:, :, s], in0=bcol[:, :, s], in1=col15[:, :, s], op=ALU.subtract),
                reads=[("bcol",), ("col15",)], writes=[("col15",)]))
        tile_ops.reverse()

        def pull_tiles(n_):
            for _ in range(n_):
                if tile_ops:
                    tile_ops.pop()()
        kb.dma("sp", asm, d["attn_small"][j], writes=[("asm",)])
        kb.op("dve", lambda e: e.tensor_tensor(out=lpr[:, 0:1], in0=asm[:, 513:514], in1=asm[:, 514:515], op=ALU.mult),
              reads=[("asm",)], writes=[("lpr",)])
        kb.op("dve", lambda e: e.tensor_tensor(out=lpr[:, 1:2], in0=asm[:, 515:516], in1=asm[:, 516:517], op=ALU.mult),
              reads=[("asm",), ("lpr",)], writes=[("lpr",)])
        kb.op("pe", lambda e: e.matmul(self.ps[7][:, 0:2], lhsT=self.ones_f[:], rhs=lpr, start=True, stop=True),
              reads=[("lpr",), ("ones_f",)], writes=[("ps", 7)])
        kb.op("act", lambda e: e.activation(out=lpr, in_=self.ps[7][:, 0:2], func=AF.Exp),
              reads=[("ps", 7), ("lpr",)], writes=[("lpr",)])
        kb.op("dve", lambda e: e.tensor_tensor(out=neglam1[:, 0:1], in0=lpr[:, 1:2], in1=lpr[:, 0:1],
                                               op=ALU.subtract), reads=[("lpr",)], writes=[("neglam",)])
        kb.op("dve", lambda e: e.tensor_scalar(out=neglam1[:, 0:1], in0=neglam1[:, 0:1], scalar1=-lam_init,
                                               scalar2=None, op0=ALU.add), reads=[("neglam",)], writes=[("neglam",)])
        kb.op("dve", lambda e: e.tensor_scalar(out=subg1, in0=asm[:, 512:513], scalar1=1.0 - lam_init, scalar2=None,
                                               op0=ALU.mult), reads=[("asm",)], writes=[("subg",)])
        import os
        skip = os.environ.get("ATT_SKIP", "")
        prev_v = None
        deferred = []
        for cs in [4, 5, 6, 7, 8, 9, 10, 11, 0, 1, 2, 3]:
            if prev_v is not None:
                for h in (prev_v * 2, prev_v * 2 + 1):
                    kb.collective(self.groups, self.ag_kv_in[h].ap().opt(), self.ag_kv_out[h].ap().opt(),
                                  reads=[("akv", h, tb, kv) for tb in range(NTB) for kv in range(2)],
                                  writes=[("ag_kv_out", h)])
                prev_v = None
            if cs >= 8:
                prev_v = cs - 8
            if "p" in skip:
                break
            if "v" in skip and cs >= 8:
                break
            if "k" in skip and cs >= 4:
                break
            w, wk = self.rload(d["attn_win"][j, cs], [128, NCH, 256])
            kind = cs // 4
            hh0 = (cs % 4) * 2
            for tb in range(NTB + 1):
                pull_tiles(3)
                ti = tb // 4 if tb < NTB else NT - 1
                rows = 128 if tb < NTB else 32
                k = self.nxt() % 2
                pp = self.ps[k]
                for kc in range(NCH):
                    kb.op("pe", lambda e, pp=pp, kc=kc, tb=tb: e.matmul(
                        pp[:, 0:256], lhsT=self.uT[:, kc, tb * 128:(tb + 1) * 128], rhs=w[:, kc, :],
                        start=(kc == 0), stop=(kc == NCH - 1)),
                        reads=[wk, ("uT", kc, ti), ("uT_pad",)], writes=[("ps", k)])
                if deferred:
                    deferred.pop()()
                if kind == 2:
                    vf = V["qn"][:, k, :]
                    kb.op("act", lambda e, pp=pp, vf=vf: e.activation(out=vf, in_=pp[:, 0:256], func=AF.Identity),
                          reads=[("ps", k)], writes=[("qn", k)])
                    self.final.append(kb.dma("sp", d["v_o"][j, tb * 128:tb * 128 + rows, hh0 * 128:hh0 * 128 + 256],
                                             vf[0:rows, :], reads=[("qn", k)]))
                    if tb < NTB:
                        gset = (tb // 4) % 2
                        vb = V["vb"][:, gset, :, :]
                        kb.op("dve", lambda e, vf=vf, vb=vb, tb=tb: e.tensor_copy(
                            out=vb[:, :, (tb % 4) * 128:(tb % 4 + 1) * 128], in_=vf.rearrange("p (h d) -> p h d", h=2)),
                            reads=[("qn", k)], writes=[("vb", gset)])
                        if tb % 4 == 3:
                            for hh in range(2):
                                kb.dma("sp", self.ag_kv_in[hh0 + hh].ap()[128:256, (tb - 3) * 128:(tb + 1) * 128],
                                       vb[:, hh, :], reads=[("vb", gset)],
                                       writes=[("akv", hh0 + hh, t_, 1) for t_ in range(tb - 3, tb + 1)])
                    else:
                        kb.op("dve", lambda e, vf=vf: e.tensor_copy(
                            out=vs[:, hh0:hh0 + 2, :], in_=vf.rearrange("p (h d) -> p h d", h=2)),
                            reads=[("qn", k)], writes=[("vs",)])
                    continue
                sq, ss, qn = V["sq"][:, k, :], V["ss"][:, k, :], V["qn"][:, k, :]
                kb.op("act", lambda e, pp=pp, sq=sq: e.activation(out=sq, in_=pp[:, 0:256], func=AF.Square),
                      reads=[("ps", k)], writes=[("sq", k)])
                kb.op("dve", lambda e, sq=sq, ss=ss: e.tensor_reduce(out=ss, in_=sq.rearrange("p (g x) -> p g x", x=64),
                                                                     axis=AX.X, op=ALU.add),
                      reads=[("sq", k)], writes=[("ss", k)])
                kb.op("act", lambda e, ss=ss: e.activation(out=ss, in_=ss, func=AF.Sqrt, scale=1.0 / 64, bias=EPS),
                      reads=[("ss", k)], writes=[("ss", k)])
                kb.op("dve", lambda e, ss=ss: e.reciprocal(out=ss, in_=ss), reads=[("ss", k)], writes=[("ss", k)])
                kb.op("dve", lambda e, pp=pp, ss=ss, qn=qn: e.tensor_tensor(
                    out=qn.rearrange("p (g x) -> p g x", x=64), in0=pp[:, 0:256].rearrange("p (g x) -> p g x", x=64),
                    in1=ss.unsqueeze(2).to_broadcast([128, 4, 64]), op=ALU.mult),
                    reads=[("ps", k), ("ss", k)], writes=[("qn", k)])
                gcol = 0 if kind == 0 else 256
                kb.op("dve", lambda e, qn=qn, gcol=gcol: e.tensor_tensor(out=qn, in0=qn, in1=asm[:, gcol:gcol + 256],
                                                                         op=ALU.mult),
                      reads=[("qn", k), ("asm",)], writes=[("qn", k)])
                if kind == 1:
                    self.final.append(kb.dma("sp", d["k_o"][j, tb * 128:tb * 128 + rows, hh0 * 128:hh0 * 128 + 256],
                                             qn[0:rows, :], reads=[("qn", k)]))
                def X(k=k, qn=qn, kind=kind, tb=tb, hh0=hh0):
                    pt = self.ps[2 + k]
                    for hh in range(2):
                        kb.op("pe", lambda e, pt=pt, qn=qn, hh=hh: e.transpose(
                            out=pt[:, hh * 128:(hh + 1) * 128], in_=qn[:, hh * 128:(hh + 1) * 128], identity=self.ident[:]),
                            reads=[("qn", k), ("ident",)], writes=[("ps", 2 + k)])
                    ptv = pt[:, 0:256].rearrange("p (h t) -> p h t", h=2)
                    if kind == 0:
                        kb.op("act", lambda e, ptv=ptv, tb=tb: e.activation(
                            out=qT[:, hh0:hh0 + 2, tb * 128:(tb + 1) * 128], in_=ptv, func=AF.Identity),
                            reads=[("ps", 2 + k)], writes=[("qT", hh0, tb), ("qT", hh0 + 1, tb)])
                    elif tb < NTB:
                        gset = (tb // 4) % 2
                        kt = V["kTt"][:, gset, :, :]
                        kb.op("act", lambda e, ptv=ptv, kt=kt, tb=tb: e.activation(
                            out=kt[:, :, (tb % 4) * 128:(tb % 4 + 1) * 128], in_=ptv, func=AF.Identity),
                            reads=[("ps", 2 + k)], writes=[("kTt", gset)])
                        if tb % 4 == 3:
                            for hh in range(2):
                                kb.dma("sp", self.ag_kv_in[hh0 + hh].ap()[0:128, (tb - 3) * 128:(tb + 1) * 128],
                                       kt[:, hh, :], reads=[("kTt", gset)],
                                       writes=[("akv", hh0 + hh, t_, 0) for t_ in range(tb - 3, tb + 1)])
                    else:
                        kb.op("act", lambda e, ptv=ptv: e.activation(out=ksT[:, hh0:hh0 + 2, :], in_=ptv, func=AF.Identity),
                              reads=[("ps", 2 + k)], writes=[("ksT",)])
                deferred.append(X)
            while deferred:
                deferred.pop()()
        pull_tiles(100000)
        for i_ in pres:
            self.ring_reserved.discard(i_)
        import os
        stop = int(os.environ.get("ATT_STOP", "99"))
        tilen = [t[0] for t in TILES]
        V = self.carve(KEEP + TILES + [("c15s", [128, NHEAD], F32), ("E", [128, 4, 512], BF16),
                                       ("qpad", [128, 2, 2, 512], BF16), ("ft", [128, 4, 512], F32),
                                       ("stmp", [128, 2, 512], F32)], keep=keepn + tilen)
        qT, ksT, vs = V["qT"], V["ksT"], V["vs"]
        T0, T1, Ts0, bcol, col15 = (V[k] for k in ("T0", "T1", "Ts0", "bcol", "col15"))
        E, qpad, ft, stmp, c15s = V["E"], V["qpad"], V["ft"], V["stmp"], V["c15s"]
        stmp = stmp.rearrange("p a (c w) -> p a c w", c=2)
        neglam, subg = V["neglam"], V["subg"]
        kb.dma("sp", c15s, d["rb_rep"][:, 120:128], writes=[("c15s",)])
        for kq in range(2):
            kb.op("dve", lambda e, kq=kq: e.memset(qpad[64:128, kq, 0, :], 0.0), writes=[("qpad", kq)])
            kb.op("dve", lambda e, kq=kq: e.memset(qpad[0:64, kq, 1, :], 0.0), reads=[("qpad", kq)], writes=[("qpad", kq)])
        oT = self.uT

        def attend(h, qcol0, n, passes, okeys):
            kq = self.nxt() % 2
            qp = qpad[:, kq, :, :]
            qk = ("qpad", kq)
            kb.op("act", lambda e: e.activation(out=qp[0:64, 0, :n], in_=qT[0:64, h, qcol0:qcol0 + n], func=AF.Identity),
                  reads=[("qT", h, tb) for tb in range(qcol0 // 128, (qcol0 + n + 127) // 128)] + [qk], writes=[qk])
            kb.op("act", lambda e: e.activation(out=qp[64:128, 1, :n], in_=qT[64:128, h, qcol0:qcol0 + n], func=AF.Identity),
                  reads=[("qT", h, tb) for tb in range(qcol0 // 128, (qcol0 + n + 127) // 128)] + [qk], writes=[qk])
            nb = sum(pk_[0] for pk_ in passes)

            def gen():
                for (cnt_, loader) in passes:
                    ksl_, vsl_, blocks_ = loader()
                    assert len(blocks_) == cnt_
                    for b_ in blocks_:
                        yield (ksl_, vsl_), b_
            def emit_pv(bi, vsl, vv, kk):
                for c in range(2):
                    ek = ("E", kk * 2 + c)
                    Ec = E[:, kk * 2 + c, :]
                    kb.op("pe", lambda e, vv=vv, Ec=Ec, c=c, bi=bi: e.matmul(
                        self.ps[4 + c][:, :n], lhsT=vv, rhs=Ec[:, :n], start=(bi == 0), stop=(bi == nb - 1)),
                        reads=[vsl, ek], writes=[("ps", 4 + c)])
                    kb.op("pe", lambda e, Ec=Ec, c=c, bi=bi: e.matmul(
                        self.ps[6 + c][:, :n], lhsT=self.ones_b[:], rhs=Ec[:, :n], start=(bi == 0), stop=(bi == nb - 1)),
                        reads=[("ones_b",), ek], writes=[("ps", 6 + c)])

            pending = None
            for bi, ((ksl, vsl), (kT, vv, subs)) in enumerate(gen()):
                kk = bi % 2
                banks = (kk * 2, kk * 2 + 1)
                pkeys = [("ps", banks[0]), ("ps", banks[1])]
                ekeys = [("E", banks[0]), ("E", banks[1])]
                for c in range(2):
                    S = self.ps[banks[c]]
                    kb.op("pe", lambda e, S=S, kT=kT, c=c: e.matmul(S[:, :n], lhsT=kT, rhs=qp[:, c, :n], start=True, stop=True),
                          reads=[ksl, qk], writes=[pkeys[c]])
                S2 = self.psall[:, banks[0] * 512:(banks[0] + 2) * 512].rearrange("p (c w) -> p c w", c=2)
                E2 = E[:, banks[0]:banks[0] + 2, :]
                for (q0, q1, kind, arg) in subs:
                    if kind == "z":
                        kb.op("dve", lambda e, q0=q0, q1=q1: e.memset(E2[:, :, q0:q1], 0.0), writes=ekeys)
                    elif kind == "c":
                        kb.op("act", lambda e, q0=q0, q1=q1, arg=arg: e.activation(
                            out=E2[:, :, q0:q1], in_=S2[:, :, q0:q1], func=AF.Exp, scale=0.125, bias=arg),
                            reads=pkeys + [("bcol",), ("c15s",)], writes=ekeys + [("psr", kk)])
                    else:
                        sk = self.nxt() % 2
                        w_ = q1 - q0
                        st_ = stmp[:, sk, :, 0:w_]
                        bias_ap = arg(None, ("stmp", sk))
                        kb.op("dve", lambda e, st_=st_, q0=q0, q1=q1, bias_ap=bias_ap, w_=w_: e.scalar_tensor_tensor(
                            out=st_, in0=S2[:, :, q0:q1], scalar=0.125,
                            in1=bias_ap.unsqueeze(1).to_broadcast([128, 2, w_]), op0=ALU.mult, op1=ALU.add),
                            reads=pkeys + [("stmp", sk), ("ft", 1)] + [("T0", h), ("T1", h), ("Ts0",)],
                            writes=[("stmp", sk), ("psr", kk)])
                        kb.op("act", lambda e, st_=st_, q0=q0, q1=q1: e.activation(
                            out=E2[:, :, q0:q1], in_=st_, func=AF.Exp), reads=[("stmp", sk)], writes=ekeys)
                if pending is not None:
                    emit_pv(*pending)
                pending = (bi, vsl, vv, kk)
            emit_pv(*pending)
            for c in range(2):
                kb.op("dve", lambda e, c=c: e.reciprocal(out=ft[:, c, :n], in_=self.ps[6 + c][:, :n]),
                      reads=[("ps", 6 + c)], writes=[("ft", c)])
                kb.op("dve", lambda e, c=c: e.tensor_tensor(out=ft[:, 2 + c, :n], in0=self.ps[4 + c][:, :n],
                                                            in1=ft[:, c, :n], op=ALU.mult),
                      reads=[("ps", 4 + c), ("ft", c)], writes=[("ft", 2 + c)])
            kb.op("dve", lambda e: e.scalar_tensor_tensor(out=ft[:, 2, :n], in0=ft[:, 3, :n], scalar=neglam[:, 0:1],
                                                          in1=ft[:, 2, :n], op0=ALU.mult, op1=ALU.add),
                  reads=[("ft", 3), ("ft", 2), ("neglam",)], writes=[("ft", 2)])
            kb.op("act", lambda e: e.activation(out=ft[:, 3, :n], in_=ft[:, 2, :n], func=AF.Square),
                  reads=[("ft", 2), ("ft", 3)], writes=[("ft", 3)])
            kb.op("pe", lambda e: e.matmul(self.ps[0][:, :n], lhsT=self.ones_f[:], rhs=ft[:, 3, :n], start=True, stop=True),
                  reads=[("ft", 3), ("ones_f",)], writes=[("ps", 0)])
            kb.op("act", lambda e: e.activation(out=ft[:, 0, :n], in_=self.ps[0][:, :n], func=AF.Sqrt, scale=1.0 / 128,
                                                bias=EPS), reads=[("ps", 0), ("ft", 0)], writes=[("ft", 0)])
            kb.op("dve", lambda e: e.reciprocal(out=ft[:, 0, :n], in_=ft[:, 0, :n]), reads=[("ft", 0)], writes=[("ft", 0)])
            kb.op("dve", lambda e: e.tensor_tensor(out=ft[:, 2, :n], in0=ft[:, 2, :n], in1=ft[:, 0, :n], op=ALU.mult),
                  reads=[("ft", 0), ("ft", 2)], writes=[("ft", 2)])
            kb.op("act", lambda e: e.activation(out=oT[:, h, qcol0:qcol0 + n], in_=ft[:, 2, :n], func=AF.Identity,
                                                scale=subg[:, 0:1]), reads=[("ft", 2), ("subg",)], writes=okeys)

        for h in range(NHEAD):
            cconst = c15s[:, h:h + 1]
            for g in range(NG):
                passes = []
                nkb = 4 * g + 4

                def load_own(h=h, g=g, nkb=nkb):
                    akv_in = self.ag_kv_in[h].ap()
                    ksl = self.rload(akv_in[0:128, 0:nkb * 128], [128, nkb * 128], q="sp",
                                     reads=[("akv", h, tb, 0) for tb in range(nkb)])
                    vsl = self.rload(akv_in[128:256, 0:nkb * 128], [128, nkb * 128], q="sp",
                                     reads=[("akv", h, tb, 1) for tb in range(nkb)])
                    blocks = []
                    for jb in range(nkb):
                        subs = []
                        for i in range(4):
                            off = 4 * g + i - jb
                            q0, q1 = i * 128, (i + 1) * 128
                            if off < 0:
                                subs.append((q0, q1, "z", None))
                            elif off == 0:
                                subs.append((q0, q1, "t", lambda st_, sk, h=h: T0[:, h, :]))
                            elif off == 1:
                                subs.append((q0, q1, "t", lambda st_, sk, h=h: T1[:, h, :]))
                            else:
                                if subs and subs[-1][2] == "c":
                                    subs[-1] = (subs[-1][0], q1, "c", cconst)
                                else:
                                    subs.append((q0, q1, "c", cconst))
                        blocks.append((ksl[0][:, jb * 128:(jb + 1) * 128], vsl[0][:, jb * 128:(jb + 1) * 128], subs))
                    return ksl[1], vsl[1], blocks
                passes.append((nkb, load_own))
                for s in range(3):
                    if os.environ.get("ATT_NOREMOTE") == "1":
                        break

                    def load_rem(h=h, g=g, s=s):
                        akv_out = self.ag_kv_out[h].ap()
                        r0 = s * 256
                        ksl = self.rload(akv_out[r0:r0 + 128, :], [128, NP], q="sp", reads=[("ag_kv_out", h)])
                        vsl = self.rload(akv_out[r0 + 128:r0 + 256, :], [128, NP], q="sp", reads=[("ag_kv_out", h)])
                        bc = bcol[:, h, s:s + 1]
                        blocks = []
                        for jb in range(NTB):
                            if jb == NTB - 1 and g == 0:
                                def mk(st_, sk, h=h, s=s):
                                    bt = ft[:, 1, 0:128]
                                    kb.op("dve", lambda e: e.tensor_scalar(
                                        out=bt, in0=T1[:, h, :], scalar1=self.pos[:, 11 + s:12 + s],
                                        scalar2=col15[:, h, s:s + 1], op0=ALU.mult, op1=ALU.add),
                                        reads=[("T1", h), ("pos",), ("col15",), ("ft", 1)], writes=[("ft", 1)])
                                    return bt
                                subs = [(0, 128, "t", mk), (128, 512, "c", bc)]
                            else:
                                subs = [(0, 512, "c", bc)]
                            blocks.append((ksl[0][:, jb * 128:(jb + 1) * 128], vsl[0][:, jb * 128:(jb + 1) * 128], subs))
                        return ksl[1], vsl[1], blocks
                    passes.append((NTB, load_rem))
                attend(h, g * 512, 512, passes, [("uT", h, g)])
            NPB = cfg.PAST // 128

            def load_cache(h=h):
                ksl = self.rload(d["cache_kT"][j, h], [128, cfg.PAST], q="pool")
                vsl = self.rload(d["cache_v"][j, h].rearrange("p a b -> p (a b)"), [128, cfg.PAST], q="pool")
                blocks = []
                for jb in range(NPB):
                    if jb == NPB - 1:
                        subs = [(0, 32, "t", lambda st_, sk, h=h: T1[:, h, 0:32])]
                    else:
                        subs = [(0, 32, "c", cconst)]
                    blocks.append((ksl[0][:, jb * 128:(jb + 1) * 128], vsl[0][:, jb * 128:(jb + 1) * 128], subs))
                return ksl[1], vsl[1], blocks

            def load_new(h=h):
                return ("ksT",), ("vs",), [(ksT[:, h, :], vs[:, h, :], [(0, 32, "t", lambda st_, sk, h=h: Ts0[:, h, :])])]
            passes = [(NPB, load_cache), (1, load_new)]
            attend(h, NP, 32, passes, [("uT", h, NT - 1)])

        if cfg.debug:
            self.final.append(kb.dma("sp", d["dbg2"][:, 0:24], bcol.rearrange("p a b -> p (a b)"), reads=[("bcol",)]))
            self.final.append(kb.dma("sp", d["dbg2"][:, 24:48], col15.rearrange("p a b -> p (a b)"), reads=[("col15",)]))
            self.final.append(kb.dma("sp", d["dbg2"][:, 48:56], c15s, reads=[("c15s",)]))
            self.final.append(kb.dma("sp", d["dbg2"][:, 56:64], self.pos[:, 8:16], reads=[("pos",)]))
            self.final.append(kb.dma("pool", d["dbg"], oT[:, :, 0:T],
                                     reads=[("uT", h, t) for h in range(NHEAD) for t in range(NT)]))
        w2 = [self.rload(d["attn_wout"][j, q], [128, 2, D]) for q in range(4)]
        for ti, (t0, n, s) in enumerate(cfg.tiles):
            for dc in range(NCH):
                k = self.nxt() % 2
                bank = 4 + k
                py = self.ps[bank]
                for h in range(NHEAD):
                    wv, wk2 = w2[h // 2]
                    kb.op("pe", lambda e, wv=wv, h=h, dc=dc, py=py, t0=t0, n=n: e.matmul(
                        py[:, :n], lhsT=wv[:, h % 2, dc * 128:(dc + 1) * 128], rhs=oT[:, h, t0:t0 + n],
                        start=(h == 0), stop=(h == NHEAD - 1)),
                        reads=[wk2, ("uT", h, ti)], writes=[("ps", bank)])
                kb.op("dve", lambda e, py=py, dc=dc, t0=t0, n=n, s=s: e.scalar_tensor_tensor(
                    out=self.xT[:, dc, t0:t0 + n], in0=py[:, :n], scalar=self.mod[:, 40 + dc, s:s + 1],
                    in1=self.xT[:, dc, t0:t0 + n], op0=ALU.mult, op1=ALU.add),
                    reads=[("ps", bank), self.kmod, ("xT", dc, ti)], writes=[("xT", dc, ti)])


def _static_tables(cfg):
    k = np.arange(128)[:, None]
    q = np.arange(128)[None, :]
    planes = np.zeros((128, 49, 128), np.float32)
    b0 = _t5_bucket(k - q)
    b1 = _t5_bucket(k - 128 - q)
    for b in range(32):
        planes[:, b, :] = (b0 == b)
    for b in range(16):
        planes[:, 32 + b, :] = (b1 == b)
    planes[:, 48, :] = ((k // 64) > (q // 64))
    import ml_dtypes
    return planes.reshape(128, 49 * 128).astype(ml_dtypes.bfloat16), np.eye(128, dtype=np.float32)


def _pos_table(p):
    pos = np.zeros((128, 160), np.float32)
    if p >= 1:
        pos[:, p - 1] = 1.0
    pos[:, 4 + p] = 1.0
    for s in range(3):
        pos[:, 8 + s] = 0.0 if s < p else NEG
        pos[:, 11 + s] = 1.0 if s == p - 1 else 0.0
    wnds = (2, 4, 8, 16)
    for c in range(NCH):
        w = wnds[c // 2]
        for t in range(15):
            pos[:, 16 + c * 15 + t] = (w / min(t + 1, w)) if p == 0 else 1.0
    return pos


def _fm(v):
    v = np.asarray(v)
    lead = v.shape[:-1]
    return np.ascontiguousarray(np.moveaxis(v.reshape(lead + (NCH, 128)), (-2, -1), (1, 0)).reshape((128, NCH) + lead))


def prep_shared(cfg, I):
    L = cfg.L
    S = {}
    aw = np.asarray(I["ada_w"])[:L]
    S["ada_w"] = np.ascontiguousarray(aw.reshape(L, NCH, 128, 36, 256).transpose(0, 3, 2, 1, 4))
    S["ada_b"] = np.ascontiguousarray(np.asarray(I["ada_b"])[:L].reshape(L, 72, 128).transpose(0, 2, 1))
    S["norm_g"] = np.ascontiguousarray(np.asarray(I["norm_g"])[:L].reshape(L * 3 * NCH, 128).T)
    fw = np.asarray(I["ffn_w_in"])[:L].reshape(L, 2, NCH, 128, 2, NHC, 128)
    S["ffn_w_in"] = np.ascontiguousarray(fw.transpose(0, 1, 5, 3, 2, 4, 6).reshape(L, 2, NHC, 128, NCH, 256))
    fo = np.asarray(I["ffn_w_out"])[:L].reshape(L, 2, NHC // 2, 2, 128, D)
    S["ffn_w_out"] = np.ascontiguousarray(fo.transpose(0, 1, 2, 4, 3, 5))
    ia = [l for l in range(L) if cfg.kinds[l] == 0]
    NA = max(cfg.NA, 1)
    lw = np.asarray(I["lru_w_in"])[:NA].reshape(NA, NCH, 128, 2, NBLK, BS)
    S["lru_win"] = np.ascontiguousarray(lw.transpose(0, 4, 2, 1, 3, 5).reshape(NA, NBLK, 128, NCH, 160))
    ga = np.asarray(I["lru_ga_w"])[:NA]
    gx = np.asarray(I["lru_gx_w"])[:NA]
    S["lru_g"] = np.ascontiguousarray(np.concatenate([ga, gx], axis=-1).transpose(0, 2, 1, 3))
    lo = np.asarray(I["lru_w_out"])[:NA].reshape(NA, NBLK // 2, 2, BS, D)
    S["lru_wout"] = np.ascontiguousarray(lo.transpose(0, 1, 3, 2, 4))
    NB = max(cfg.NB, 1)
    pw = np.asarray(I["pool_w"])[:NB].reshape(NB, 4, 2, 128, 256)
    S["pool_w"] = np.ascontiguousarray(pw.transpose(0, 1, 3, 2, 4))
    NC_ = max(cfg.NCc, 1)
    awi = np.asarray(I["attn_w_in"])[:NC_].reshape(NC_, NCH, 128, 12, 256)
    S["attn_win"] = np.ascontiguousarray(awi.transpose(0, 3, 2, 1, 4))
    awo = np.asarray(I["attn_w_out"])[:NC_].reshape(NC_, 4, 2, 128, D)
    S["attn_wout"] = np.ascontiguousarray(awo.transpose(0, 1, 3, 2, 4))
    asm = np.zeros((NC_, 128, 517), np.float32)
    asm[:, :, 0:256] = np.tile(np.asarray(I["attn_q_g"])[:NC_], (1, 4))[:, None, :]
    asm[:, :, 256:512] = np.tile(np.asarray(I["attn_k_g"])[:NC_], (1, 4))[:, None, :]
    asm[:, :, 512] = np.asarray(I["attn_sub_g"])[:NC_]
    asm[:, 0:64, 513:517] = np.asarray(I["attn_lambda"])[:NC_].transpose(0, 2, 1)
    S["attn_small"] = asm
    S["rb_rep"] = np.ascontiguousarray(np.broadcast_to(np.asarray(I["rel_bias"]).reshape(1, 256), (128, 256)))
    S["planes"], S["ident"] = _static_tables(cfg)
    return S


def prep_core(cfg, I, k):
    NP = cfg.NP
    b, p = k // 4, k % 4
    C = {}
    xp = np.asarray(I["x_prompt"])[b, p * NP:(p + 1) * NP]
    xs = np.asarray(I["x_sample"])[k]
    C["xT"] = _fm(np.concatenate([xp, xs], axis=0))
    C["cT"] = _fm(np.stack([np.asarray(I["c_prompt"])[b], np.asarray(I["c_sample"])[k]], 0))
    C["pos"] = _pos_table(p)
    NA = max(cfg.NA, 1)

    def blk(v):
        v = np.asarray(v)
        lead = v.shape[:-1]
        return np.moveaxis(v.reshape(lead + (NBLK, BS)), (-2, -1), (1, 0)).reshape((BS, NBLK) + lead)
    ls = np.zeros((NA, BS, NBLK, 12), np.float32)
    for j in range(cfg.NA):
        ls[j, :, :, 0:4] = blk(I["lru_conv_w"][j])
        ls[j, :, :, 4] = blk(I["lru_conv_b"][j])
        ls[j, :, :, 5] = blk(I["lru_ga_b"][j])
        ls[j, :, :, 6] = blk(I["lru_gx_b"][j])
        ls[j, :, :, 7] = blk(I["lru_lambda"][j])
        ls[j, :, :, 8] = blk(I["state_lru_h"][j, k])
        ls[j, :, :, 9:12] = blk(I["state_lru_conv"][j, k])
    C["lru_small"] = ls
    NB = max(cfg.NB, 1)
    psm = np.zeros((NB, 128, 136), np.float32)
    for j in range(cfg.NB):
        psm[j, :, 0:8] = _fm(I["pool_b"][j])
        psm[j, :, 8:16] = _fm(I["pool_scale"][j])
        psm[j, :, 16:136] = _fm(I["state_pool"][j, k]).reshape(128, 120)
    C["pool_small"] = psm
    NC_ = max(cfg.NCc, 1)
    ck = np.zeros((NC_, NHEAD, 128, cfg.PAST), np.float32)
    cv = np.zeros((NC_, NHEAD, 128, cfg.PAST // 128, 128), np.float32)
    for j in range(cfg.NCc):
        ck[j] = np.asarray(I["cache_k"])[j, k].transpose(1, 2, 0)
        cv[j] = np.asarray(I["cache_v"])[j, k].reshape(cfg.PAST // 128, 128, NHEAD, 128).transpose(2, 1, 0, 3)
    C["cache_kT"] = ck
    C["cache_v"] = cv
    return C


_PROG_CACHE = {}


def run_cfg(cfg, I):
    key = (cfg.NP, cfg.PAST, cfg.kinds, cfg.debug)
    if key not in _PROG_CACHE:
        bld = Builder(cfg)
        nc = bld.build()
        _PROG_CACHE[key] = (bld, nc)
    bld, nc = _PROG_CACHE[key]
    S = prep_shared(cfg, I)
    in_maps = []
    for k in range(8):
        m = dict(S)
        m.update(prep_core(cfg, I, k))
        in_maps.append({n: np.ascontiguousarray(m[n]) for n in bld.in_names})
    res = run_bass_kernel_spmd(nc, in_maps, core_ids=list(range(8)))
    if cfg.debug:
        global _DBG
        _DBG = [np.asarray(r["dbg"]) for r in res.results]
        global _DBG2
        _DBG2 = [np.asarray(r["dbg2"]) for r in res.results]
    return assemble(cfg, res.results)


def _unfm(a):
    return np.ascontiguousarray(a.transpose(2, 1, 0).reshape(a.shape[2], D))


def assemble(cfg, R):
    NP = cfg.NP
    B = 2
    SEQ = 4 * NP
    y_p = np.zeros((B, SEQ, D), np.float32)
    y_s = np.zeros((8, 32, D), np.float32)
    for k in range(8):
        b, p = k // 4, k % 4
        yt = _unfm(np.asarray(R[k]["yT"]))
        y_p[b, p * NP:(p + 1) * NP] = yt[:NP]
        y_s[k] = yt[NP:]
    NA, NB, NCc = cfg.NA, cfg.NB, cfg.NCc

    def unblk(a):
        return np.ascontiguousarray(a.T.reshape(-1))
    h_p = np.zeros((NA, B, DRNN), np.float32)
    h_s = np.zeros((NA, 8, DRNN), np.float32)
    cv_p = np.zeros((NA, B, 3, DRNN), np.float32)
    cv_s = np.zeros((NA, 8, 3, DRNN), np.float32)
    for k in range(8):
        b, p = k // 4, k % 4
        lo = np.asarray(R[k]["lru_o"])
        for j in range(NA):
            h_s[j, k] = unblk(lo[j, :, :, 1])
            for t in range(3):
                cv_s[j, k, t] = unblk(lo[j, :, :, 5 + t])
            if p == 3:
                h_p[j, b] = unblk(lo[j, :, :, 0])
                for t in range(3):
                    cv_p[j, b, t] = unblk(lo[j, :, :, 2 + t])
    pl_p = np.zeros((NB, B, 15, D), np.float32)
    pl_s = np.zeros((NB, 8, 15, D), np.float32)
    for k in range(8):
        b, p = k // 4, k % 4
        po = np.asarray(R[k]["pool_o"])
        for j in range(NB):
            pl_s[j, k] = _unfm(po[j][:, :, 15:30])
            if p == 3:
                pl_p[j, b] = _unfm(po[j][:, :, 0:15])
    k_p = np.zeros((NCc, B, SEQ, NHEAD, 128), np.float32)
    v_p = np.zeros((NCc, B, SEQ, NHEAD, 128), np.float32)
    k_s = np.zeros((NCc, 8, 32, NHEAD, 128), np.float32)
    v_s = np.zeros((NCc, 8, 32, NHEAD, 128), np.float32)
    for k in range(8):
        b, p = k // 4, k % 4
        ko = np.asarray(R[k]["k_o"])
        vo = np.asarray(R[k]["v_o"])
        for j in range(NCc):
            k_p[j, b, p * NP:(p + 1) * NP] = ko[j, :NP].reshape(NP, NHEAD, 128)
            v_p[j, b, p * NP:(p + 1) * NP] = vo[j, :NP].reshape(NP, NHEAD, 128)
            k_s[j, k] = ko[j, NP:].reshape(32, NHEAD, 128)
            v_s[j, k] = vo[j, NP:].reshape(32, NHEAD, 128)
    return (y_p, y_s, h_p, h_s, cv_p, cv_s, pl_p, pl_s, k_p, v_p, k_s, v_s)


def kernel(**inputs):
    cfg = Cfg(NP=2048, PAST=1024, kinds=(0, 1, 2, 0))
    return run_cfg(cfg, inputs)
```

```python
import math
from contextlib import ExitStack

import numpy as np
import concourse.bass as bass
import concourse.mybir as mybir
from concourse.bass_utils import run_bass_kernel_spmd

F32 = mybir.dt.float32
BF16 = mybir.dt.bfloat16
AF = mybir.ActivationFunctionType
ALU = mybir.AluOpType
AX = mybir.AxisListType

D = 1024
NCH = 8
DFF = 2816
NHC = 22
DRNN = 1280
NBLK = 16
BS = 80
NHEAD = 8
EPS = 1e-6
NEG = -30000.0
ENGS = ["pe", "act", "dve", "pool", "sp"]
FFN_GROUPS = [(0, 6), (6, 12), (12, 18), (18, 22)]


class _Rec:
    def __init__(self):
        self.call = None

    def __getattr__(self, name):
        def f(*a, **k):
            self.call = (name, a, k)
            return self
        return f


class KB:
    def __init__(self, nc, ndma=24):
        self.nc = nc
        self.streams = {e: [] for e in ENGS}
        self.seq = {e: 0 for e in ENGS}
        self.prog = {}
        self.waited = {}
        self.state = {}
        self.ndma = ndma
        self.dsems = {}
        self.duse = {}
        self.dcnt = {}
        self.ccsems = []
        self.ccn = 0
        self.inherit = {}

    def setup_sems(self, stack):
        for e in ENGS:
            self.prog[e] = stack.enter_context(self.nc.semaphore("prog_" + e))
        for q in ["sp", "pool"]:
            self.dsems[q] = [stack.enter_context(self.nc.semaphore("d%s%d" % (q, i))) for i in range(self.ndma)]
            self.duse[q] = [0] * self.ndma
            self.dcnt[q] = 0
        self.ccsems = [stack.enter_context(self.nc.semaphore("cc%d" % i)) for i in range(24)]

    def _tk(self, tok):
        kind, a, v = tok
        return (kind, a if kind == "E" else id(a))

    def _wait(self, eng, tok):
        if tok is None:
            return
        kind, a, v = tok
        if kind == "E":
            if a == "pe" and eng == "pe":
                return
            sem = self.prog[a]
        else:
            sem = a
        key = (eng,) + self._tk(tok)
        if self.waited.get(key, 0) >= v:
            return
        self.waited[key] = v
        self.streams[eng].append(lambda e, sem=sem, v=v: e.wait_ge(sem, v))

    def _get(self, k):
        st = self.state.get(k)
        if st is None and k[0] in self.inherit:
            st = {"w": None, "r": dict(self.inherit[k[0]])}
            self.state[k] = st
        return st

    def new_phase(self, old_names, new_names):
        toks = {}
        old_names = set(old_names)
        for k in list(self.state.keys()):
            if k[0] in old_names:
                st = self.state.pop(k)
                cands = list(st["r"].values())
                if st["w"] is not None:
                    cands.append(st["w"])
                for t in cands:
                    tk = self._tk(t)
                    if tk not in toks or toks[tk][2] < t[2]:
                        toks[tk] = t
        for n in old_names:
            for tk, t in self.inherit.pop(n, {}).items():
                if tk not in toks or toks[tk][2] < t[2]:
                    toks[tk] = t
        for n in new_names:
            self.inherit[n] = dict(toks)

    def _deps(self, eng, reads, writes):
        for k in reads:
            st = self._get(k)
            if st:
                self._wait(eng, st["w"])
        for k in writes:
            st = self._get(k)
            if st:
                self._wait(eng, st["w"])
                for t in st["r"].values():
                    self._wait(eng, t)

    def _commit(self, tok, reads, writes):
        for k in writes:
            self.state[k] = {"w": tok, "r": {}}
        tk = self._tk(tok)
        for k in reads:
            st = self._get(k)
            if st is None:
                st = self.state.setdefault(k, {"w": None, "r": {}})
            old = st["r"].get(tk)
            if old is None or old[2] < tok[2]:
                st["r"][tk] = tok

    def op(self, eng, fn, reads=(), writes=()):
        self._deps(eng, reads, writes)
        self.seq[eng] += 1
        n = self.seq[eng]
        sem = self.prog[eng]
        rec = _Rec()
        fn(rec)
        name, a, k = rec.call
        self.streams[eng].append(lambda e, name=name, a=a, k=k, sem=sem: getattr(e, name)(*a, **k).then_inc(sem, 1))
        tok = ("E", eng, n)
        self._commit(tok, reads, writes)
        return tok

    def dma(self, q, out, in_, reads=(), writes=(), **kw):
        self._deps(q, reads, writes)
        i = self.dcnt[q] % self.ndma
        self.dcnt[q] += 1
        sem = self.dsems[q][i]
        if self.duse[q][i] > 0:
            self._wait(q, ("D", sem, 16 * self.duse[q][i]))
        self.duse[q][i] += 1
        v = 16 * self.duse[q][i]
        self.streams[q].append(
            lambda e, out=out, in_=in_, sem=sem, kw=kw: e.dma_start(out=out, in_=in_, **kw).then_inc(sem, 16))
        tok = ("D", sem, v)
        self._commit(tok, reads, writes)
        return tok

    def collective(self, groups, in_ap, out_ap, reads=(), writes=()):
        q = "pool"
        self._deps(q, reads, writes)
        sem = self.ccsems[self.ccn]
        self.ccn += 1
        self.streams[q].append(
            lambda e, sem=sem: e.collective_compute("AllGather", ALU.bypass, replica_groups=groups,
                                                    ins=[in_ap], outs=[out_ap]).then_inc(sem, 1))
        tok = ("D", sem, 1)
        self._commit(tok, reads, writes)
        return tok

    def alias(self, new_keys, old_keys):
        toks = {}
        for k in old_keys:
            st = self.state.get(k)
            if not st:
                continue
            cands = list(st["r"].values())
            if st["w"] is not None:
                cands.append(st["w"])
            for t in cands:
                tk = self._tk(t)
                if tk not in toks or toks[tk][2] < t[2]:
                    toks[tk] = t
        for k in new_keys:
            st = self.state.setdefault(k, {"w": None, "r": {}})
            for tk, t in toks.items():
                if tk not in st["r"] or st["r"][tk][2] < t[2]:
                    st["r"][tk] = t

    def keys_with_prefix(self, prefixes):
        return [k for k in self.state if (k[0] if isinstance(k, tuple) else k) in prefixes]

    def finish(self, final_tokens):
        for t in final_tokens:
            self._wait("sp", t)
        nc = self.nc
        S = self.streams
        with nc.Block() as block:
            @block.tensor
            def _(e):
                for f in S["pe"]:
                    f(e)

            @block.scalar
            def _(e):
                for f in S["act"]:
                    f(e)

            @block.vector
            def _(e):
                for f in S["dve"]:
                    f(e)

            @block.gpsimd
            def _(e):
                for f in S["pool"]:
                    f(e)

            @block.sync
            def _(e):
                for f in S["sp"]:
                    f(e)


def _t5_bucket(rel):
    nb, max_exact = 16, 8
    rel = np.asarray(rel)
    ret = np.where(rel > 0, nb, 0)
    n = np.abs(rel)
    nf = np.maximum(n, 1).astype(np.float32)
    large = max_exact + (np.log(nf / np.float32(max_exact)) / np.float32(math.log(128 / max_exact))
                         * np.float32(nb - max_exact)).astype(np.int32)
    large = np.minimum(large, nb - 1)
    return ret + np.where(n < max_exact, n, large)


class Cfg:
    def __init__(self, NP=2048, PAST=1024, kinds=(0, 1, 2, 0), debug=False):
        self.NP = NP
        self.NS = 32
        self.T = NP + 32
        self.TA = NP + 128
        self.PAST = PAST
        self.kinds = tuple(kinds)
        self.L = len(kinds)
        self.NA = sum(1 for k in kinds if k == 0)
        self.NB = sum(1 for k in kinds if k == 1)
        self.NCc = sum(1 for k in kinds if k == 2)
        self.debug = debug
        self.tiles = [(i * 512, 512, 0) for i in range(NP // 512)] + [(NP, 32, 1)]


class Builder:
    def __init__(self, cfg):
        self.cfg = cfg
        self.nc = bass.Bass("TRN2", target_bir_lowering=False)
        self.kb = KB(self.nc)
        self.final = []
        self.ring_n = 0
        self.ring_reserved = set()
        self.cnt = 0
        self.in_names = []
        self.out_names = []

    def din(self, name, shape, dt=F32):
        self.in_names.append(name)
        return self.nc.dram_tensor(name, list(shape), dt, kind="ExternalInput").ap()

    def dout(self, name, shape, dt=F32):
        self.out_names.append(name)
        return self.nc.dram_tensor(name, list(shape), dt, kind="ExternalOutput").ap()

    def sb(self, name, shape, dt=F32):
        return self.stack.enter_context(self.nc.sbuf_tensor(name, list(shape), dt))

    def nxt(self):
        self.cnt += 1
        return self.cnt

    def rload(self, src_ap, shape, q="pool", reads=(), reserve=False):
        while (self.ring_n % self.NRING) in self.ring_reserved:
            self.ring_n += 1
        i = self.ring_n % self.NRING
        self.ring_n += 1
        if reserve:
            self.ring_reserved.add(i)
        nelem = int(np.prod(shape[1:]))
        assert nelem <= 2048
        view = self.ring[i][0:shape[0], 0:nelem]
        if len(shape) == 3:
            view = view.rearrange("p (a b) -> p a b", a=shape[1])
        key = ("ring", i)
        self.kb.dma(q, view, src_ap, reads=list(reads), writes=[key])
        return view, key

    def carve(self, spec, keep=()):
        off = 0
        views = {}
        names = []
        for name, shape, dt in spec:
            esz = 4 if dt == F32 else 2
            nel = int(np.prod(shape[1:]))
            nbytes = (nel * esz + 31) // 32 * 32
            assert off + nbytes <= self.ARENA, ("arena overflow", name, off, nbytes)
            v = self.arena[0:shape[0], off // 4: (off + nbytes) // 4]
            if dt != F32:
                v = v.bitcast(BF16)
            v = v[:, 0:nel]
            if len(shape) == 3:
                v = v.rearrange("p (a b) -> p a b", a=shape[1])
            elif len(shape) == 4:
                v = v.rearrange("p (a b c) -> p a b c", a=shape[1], b=shape[2])
            if name in keep:
                assert self.arena_off.get(name) == off, ("kept buffer moved", name)
            views[name] = v
            names.append(name)
            self.arena_off[name] = off
            off += nbytes
        old = [n for n in self.arena_names if n not in keep]
        new = [n for n in names if n not in keep]
        self.kb.new_phase(old, new)
        self.arena_names = names
        return views

    def build(self):
        cfg = self.cfg
        nc = self.nc
        kb = self.kb
        T, NP, L = cfg.T, cfg.NP, cfg.L
        NT = len(cfg.tiles)
        with ExitStack() as stack:
            self.stack = stack
            kb.setup_sems(stack)
            d = {}
            self.d = d
            d["xT"] = self.din("xT", [128, NCH, T])
            d["cT"] = self.din("cT", [128, NCH, 2])
            d["ada_w"] = self.din("ada_w", [L, 36, 128, NCH, 256])
            d["ada_b"] = self.din("ada_b", [L, 128, 72])
            d["norm_g"] = self.din("norm_g", [128, L * 3 * NCH])
            d["ffn_w_in"] = self.din("ffn_w_in", [L, 2, NHC, 128, NCH, 256])
            d["ffn_w_out"] = self.din("ffn_w_out", [L, 2, NHC // 2, 128, 2, D])
            d["pos"] = self.din("pos", [128, 160])
            d["ident"] = self.din("ident", [128, 128])
            NA, NB, NC_ = max(cfg.NA, 1), max(cfg.NB, 1), max(cfg.NCc, 1)
            d["lru_win"] = self.din("lru_win", [NA, NBLK, 128, NCH, 160])
            d["lru_g"] = self.din("lru_g", [NA, BS, NBLK, 160])
            d["lru_wout"] = self.din("lru_wout", [NA, NBLK // 2, BS, 2, D])
            d["lru_small"] = self.din("lru_small", [NA, BS, NBLK, 12])
            d["pool_w"] = self.din("pool_w", [NB, 4, 128, 2, 256])
            d["pool_small"] = self.din("pool_small", [NB, 128, 2 * NCH + NCH * 15])
            d["attn_win"] = self.din("attn_win", [NC_, 12, 128, NCH, 256])
            d["attn_wout"] = self.din("attn_wout", [NC_, 4, 128, 2, D])
            d["attn_small"] = self.din("attn_small", [NC_, 128, 256 + 256 + 1 + 4])
            d["rb_rep"] = self.din("rb_rep", [128, 256])
            d["planes"] = self.din("planes", [128, 49 * 128], BF16)
            d["cache_kT"] = self.din("cache_kT", [NC_, NHEAD, 128, cfg.PAST])
            d["cache_v"] = self.din("cache_v", [NC_, NHEAD, 128, cfg.PAST // 128, 128])
            d["yT"] = self.dout("yT", [128, NCH, T])
            d["lru_o"] = self.dout("lru_o", [NA, BS, NBLK, 8])
            d["pool_o"] = self.dout("pool_o", [NB, 128, NCH, 30])
            d["k_o"] = self.dout("k_o", [NC_, T, D])
            d["v_o"] = self.dout("v_o", [NC_, T, D])
            if cfg.debug:
                d["dbg"] = self.dout("dbg", [128, NCH, T])
                d["dbg2"] = self.dout("dbg2", [128, 64])
            self.ag_h_in = nc.dram_tensor("ag_h_in", [128, 120], F32)
            self.ag_h_out = nc.dram_tensor("ag_h_out", [4 * 128, 120], F32)
            self.ag_c_in = nc.dram_tensor("ag_c_in", [BS, 32], F32)
            self.ag_c_out = nc.dram_tensor("ag_c_out", [4 * BS, 32], F32)
            self.ag_kv_in = [nc.dram_tensor("ag_kv_in%d" % h, [256, NP], BF16) for h in range(NHEAD)]
            self.ag_kv_out = [nc.dram_tensor("ag_kv_out%d" % h, [4 * 256, NP], BF16) for h in range(NHEAD)]
            self.groups = [[0, 1, 2, 3], [4, 5, 6, 7]]
            self.lru_a = nc.dram_tensor("lru_a_scr", [NBLK, BS, NP + 38], F32)
            self.lru_b = nc.dram_tensor("lru_b_scr", [NBLK, BS, NP + 38], F32)

            self.xT = self.sb("xT_sb", [128, NCH, T])
            self.uT = self.sb("uT_sb", [128, NCH, cfg.TA], BF16)
            self.NRING = 7
            self.ring = [self.sb("ring%d" % i, [128, 2048], BF16) for i in range(self.NRING)]
            self.ones_f = self.sb("ones_f", [128, 128])
            self.ones_b = self.sb("ones_b", [128, 128], BF16)
            self.ident = self.sb("ident_sb", [128, 128])
            self.mods = [self.sb("mod%d" % i, [128, 72, 2]) for i in range(2)]
            self.gss = [self.sb("gs%d" % i, [128, 3, NCH, 2]) for i in range(2)]
            self.ghs = [self.sb("gh%d" % i, [128, 2, NCH, 2]) for i in range(2)]
            self.adabs = [self.sb("adab%d" % i, [128, 72]) for i in range(2)]
            self.ng = self.sb("ng", [128, L * 3 * NCH])
            self.cs = self.sb("cs", [128, NCH, 2])
            self.csb = self.sb("csb", [128, NCH, 2], BF16)
            self.pos = self.sb("pos_sb", [128, 160])
            self.ARENA = 76 * 1024
            self.arena = self.sb("arena", [128, self.ARENA // 4])
            self.arena_names = []
            self.arena_off = {}
            self.ps = [stack.enter_context(nc.psum_tensor("ps%d" % i, [128, 512], F32)) for i in range(8)]

            kb.op("dve", lambda e: e.memset(self.ones_f[:], 1.0), writes=[("ones_f",)])
            kb.op("dve", lambda e: e.memset(self.ones_b[:], 1.0), writes=[("ones_b",)])
            kb.op("dve", lambda e: e.memset(self.uT[:, :, T:cfg.TA], 0.0), writes=[("uT_pad",)])
            kb.dma("sp", self.ng[:], d["norm_g"], writes=[("ng",)])
            kb.dma("sp", self.cs[:], d["cT"], writes=[("cs",)])
            kb.dma("sp", self.pos[:], d["pos"], writes=[("pos",)])
            kb.dma("sp", self.ident[:], d["ident"], writes=[("ident",)])
            for c in range(NCH):
                kb.dma("sp", self.xT[:, c, :], d["xT"][:, c, :], writes=[("xT", c, ti) for ti in range(NT)])
            kb.op("act", lambda e: e.activation(out=self.csb[:], in_=self.cs[:], func=AF.Silu),
                  reads=[("cs",)], writes=[("csb",)])

            ia = ib = ic = 0
            self.set_layer(0)
            ada0 = self.adaln(0)
            for _ in range(8):
                next(ada0)
            for l in range(L):
                self.set_layer(l)
                self.ffn(l, 0, ada_next=(ada0 if l == 0 else None))
                kind = cfg.kinds[l]
                if kind == 0:
                    self.mixer_lru(l, ia)
                    ia += 1
                elif kind == 1:
                    self.mixer_pool(l, ib)
                    ib += 1
                elif kind == 2:
                    self.mixer_attn(l, ic)
                    ic += 1
                self.ffn(l, 1, ada_next=(self.adaln(l + 1) if l + 1 < L else None))

            for c in range(NCH):
                self.final.append(kb.dma("sp", d["yT"][:, c, :], self.xT[:, c, :],
                                         reads=[("xT", c, ti) for ti in range(NT)]))
            kb.finish(self.final)
        return nc

    def set_layer(self, l):
        i = l % 2
        self.mod, self.gs, self.gh = self.mods[i], self.gss[i], self.ghs[i]
        self.kmod, self.kgs, self.kgh = ("mod", i), ("gs", i), ("gh", i)

    def adaln(self, l):
        kb, d = self.kb, self.d
        i2 = l % 2
        mod, adab, gs, gh = self.mods[i2], self.adabs[i2], self.gss[i2], self.ghs[i2]
        kmod, kgs, kgh, kadab = ("mod", i2), ("gs", i2), ("gh", i2), ("adab", i2)
        kb.dma("sp", adab[:], d["ada_b"][l], writes=[kadab])
        for grp in range(9):
            bank = 6 + grp % 2
            pb = self.ps[bank]
            for st4 in range(4):
                st = grp * 4 + st4
                w, wk = self.rload(d["ada_w"][l, st], [128, NCH, 256])
                for jj in range(2):
                    jl = st4 * 2 + jj
                    for kc in range(NCH):
                        kb.op("pe", lambda e, w=w, kc=kc, jj=jj, jl=jl, pb=pb: e.matmul(
                            pb[:, jl * 2: jl * 2 + 2], lhsT=w[:, kc, jj * 128:(jj + 1) * 128], rhs=self.csb[:, kc, :],
                            start=(kc == 0), stop=(kc == NCH - 1)),
                            reads=[wk, ("csb",)], writes=[("ps", bank)])
                if st4 == 3:
                    j0 = grp * 8
                    kb.op("dve", lambda e, pb=pb, j0=j0: e.tensor_tensor(
                        out=mod[:, j0:j0 + 8, :], in0=pb[:, 0:16].rearrange("p (a b) -> p a b", b=2),
                        in1=adab[:, j0:j0 + 8].unsqueeze(2).to_broadcast([128, 8, 2]), op=ALU.add),
                        reads=[("ps", bank), kadab], writes=[kmod])
                    v = grp
                    if v in (1, 4, 7):
                        i = (v - 1) // 3
                        kb.op("dve", lambda e, i=i, v=v: e.scalar_tensor_tensor(
                            out=gs[:, i, :, :], in0=mod[:, v * 8:(v + 1) * 8, :], scalar=1.0,
                            in1=self.ng[:, (l * 3 + i) * 8:(l * 3 + i + 1) * 8].unsqueeze(2).to_broadcast([128, 8, 2]),
                            op0=ALU.add, op1=ALU.mult), reads=[kmod, ("ng",)], writes=[kgs])
                    if v in (2, 8):
                        i = 0 if v == 2 else 1
                        kb.op("dve", lambda e, i=i, v=v: e.tensor_scalar(
                            out=gh[:, i, :, :], in0=mod[:, v * 8:(v + 1) * 8, :], scalar1=0.5, scalar2=None,
                            op0=ALU.mult), reads=[kmod], writes=[kgh])
                yield

    def norm_stats(self, V, ti, rs):
        kb = self.kb
        t0, n, s = self.cfg.tiles[ti]
        pb = self.ps[6]
        for c in range(NCH):
            k = self.nxt() % 2
            sq = V["nsq"][:, k, :n]
            kb.op("act", lambda e, sq=sq, c=c: e.activation(out=sq, in_=self.xT[:, c, t0:t0 + n], func=AF.Square),
                  reads=[("xT", c, ti)], writes=[("nsq", k)])
            kb.op("pe", lambda e, sq=sq, c=c: e.matmul(pb[:, :n], lhsT=self.ones_f[:], rhs=sq,
                                                       start=(c == 0), stop=(c == NCH - 1)),
                  reads=[("nsq", k), ("ones_f",)], writes=[("ps", 6)])
        return pb

    def norm_to_uT(self, V, ni, tail=None):
        kb, cfg = self.kb, self.cfg
        shv = 3 * ni
        NT = len(cfg.tiles)
        for ti, (t0, n, s) in enumerate(cfg.tiles):
            pb = self.norm_stats(V, ti, None)
            rk = ("nrstd", ti % 2)
            rs = V["nrstd"][:, ti % 2, :n]
            kb.op("act", lambda e, rs=rs, pb=pb, n=n: e.activation(out=rs, in_=pb[:, :n], func=AF.Sqrt,
                                                                   scale=1.0 / D, bias=EPS),
                  reads=[("ps", 6)], writes=[rk])
            kb.op("dve", lambda e, rs=rs: e.reciprocal(out=rs, in_=rs), reads=[rk], writes=[rk])
            for c in range(NCH):
                k = self.nxt() % 2
                tmp = V["ntmp"][:, k, :n]
                kb.op("dve", lambda e, tmp=tmp, c=c, rs=rs, t0=t0, n=n, s=s: e.scalar_tensor_tensor(
                    out=tmp, in0=self.xT[:, c, t0:t0 + n], scalar=self.gs[:, ni, c, s:s + 1], in1=rs,
                    op0=ALU.mult, op1=ALU.mult),
                    reads=[("xT", c, ti), self.kgs, rk], writes=[("ntmp", k)])
                kb.op("act", lambda e, tmp=tmp, c=c, t0=t0, n=n, s=s: e.activation(
                    out=self.uT[:, c, t0:t0 + n], in_=tmp, func=AF.Identity,
                    bias=self.mod[:, shv * 8 + c, s:s + 1], scale=1.0),
                    reads=[("ntmp", k), self.kmod], writes=[("uT", c, ti)])
                if tail is not None and ti == NT - 2:
                    kb.op("act", lambda e, tmp=tmp, c=c, n=n, s=s: e.activation(
                        out=tail[:, c, :], in_=tmp[:, n - 15:n], func=AF.Identity,
                        bias=self.mod[:, shv * 8 + c, s:s + 1], scale=1.0),
                        reads=[("ntmp", k), self.kmod], writes=[("utail",)])

    NORM_SPEC = [("nsq", [128, 2, 512], F32), ("nrstd", [128, 2, 512], F32), ("ntmp", [128, 2, 512], F32)]

    def ffn(self, l, i, ada_next=None):
        kb, cfg, d = self.kb, self.cfg, self.d

        def pull(nst):
            if ada_next is None:
                return
            for _ in range(nst):
                try:
                    next(ada_next)
                except StopIteration:
                    return
        T = cfg.T
        V = self.carve(self.NORM_SPEC + [("hT", [128, 6, T], BF16), ("sa", [128, 2, 512], F32)])
        self.norm_to_uT(V, 0 if i == 0 else 2)
        hT, sa = V["hT"], V["sa"]
        for (j0, j1) in FFN_GROUPS:
            for j in range(j0, j1):
                w, wk = self.rload(d["ffn_w_in"][l, i, j], [128, NCH, 256])
                for ti, (t0, n, s) in enumerate(cfg.tiles):
                    k = self.nxt() % 2
                    ba, bb = k, 2 + k
                    pa, pb = self.ps[ba], self.ps[bb]
                    for half, (bank, pp) in enumerate(((ba, pa), (bb, pb))):
                        for kc in range(NCH):
                            kb.op("pe", lambda e, w=w, kc=kc, pp=pp, half=half, t0=t0, n=n: e.matmul(
                                pp[:, :n], lhsT=w[:, kc, half * 128:(half + 1) * 128], rhs=self.uT[:, kc, t0:t0 + n],
                                start=(kc == 0), stop=(kc == NCH - 1)),
                                reads=[wk, ("uT", kc, ti)], writes=[("ps", bank)])
                    kb.op("act", lambda e, pa=pa, k=k, n=n: e.activation(out=sa[:, k, :n], in_=pa[:, :n], func=AF.Silu),
                          reads=[("ps", ba)], writes=[("sa", k)])
                    kb.op("dve", lambda e, pb=pb, k=k, n=n, j=j, t0=t0: e.tensor_tensor(
                        out=hT[:, j - j0, t0:t0 + n], in0=sa[:, k, :n], in1=pb[:, :n], op=ALU.mult),
                        reads=[("sa", k), ("ps", bb)], writes=[("hT", j - j0, ti)])
                pull(2)
            w2 = []
            for jp in range(j0 // 2, j1 // 2):
                w2.append(self.rload(d["ffn_w_out"][l, i, jp], [128, 2, D]))
            ng_ = j1 - j0
            for ti, (t0, n, s) in enumerate(cfg.tiles):
                for dc in range(NCH):
                    k = self.nxt() % 2
                    bank = 4 + k
                    py = self.ps[bank]
                    for jj in range(ng_):
                        wv, wk = w2[jj // 2]
                        kb.op("pe", lambda e, wv=wv, jj=jj, dc=dc, py=py, t0=t0, n=n: e.matmul(
                            py[:, :n], lhsT=wv[:, jj % 2, dc * 128:(dc + 1) * 128], rhs=hT[:, jj, t0:t0 + n],
                            start=(jj == 0), stop=(jj == ng_ - 1)),
                            reads=[wk, ("hT", jj, ti)], writes=[("ps", bank)])
                    kb.op("dve", lambda e, py=py, dc=dc, t0=t0, n=n, s=s: e.scalar_tensor_tensor(
                        out=self.xT[:, dc, t0:t0 + n], in0=py[:, :n], scalar=self.gh[:, i, dc, s:s + 1],
                        in1=self.xT[:, dc, t0:t0 + n], op0=ALU.mult, op1=ALU.add),
                        reads=[("ps", bank), self.kgh, ("xT", dc, ti)], writes=[("xT", dc, ti)])
        pull(1000)

    def halo_exchange(self, V, tail):
        kb = self.kb
        hal, hsel = V["hal"], V["hsel"]
        kb.dma("sp", self.ag_h_in.ap(), tail.rearrange("p a b -> p (a b)"), reads=[("utail",)], writes=[("ag_h_in",)])
        kb.collective(self.groups, self.ag_h_in.ap().opt(), self.ag_h_out.ap().opt(),
                      reads=[("ag_h_in",)], writes=[("ag_h_out",)])
        kb.dma("sp", hal, self.ag_h_out.ap().rearrange("(r p) f -> p r f", p=128), reads=[("ag_h_out",)],
               writes=[("hal",)])
        kb.op("dve", lambda e: e.tensor_scalar(out=hsel, in0=hal[:, 0, :], scalar1=self.pos[:, 0:1], scalar2=None,
                                               op0=ALU.mult), reads=[("hal",), ("pos",)], writes=[("hsel",)])
        for q in range(1, 4):
            kb.op("dve", lambda e, q=q: e.scalar_tensor_tensor(
                out=hsel, in0=hal[:, q, :], scalar=self.pos[:, q:q + 1], in1=hsel, op0=ALU.mult, op1=ALU.add),
                reads=[("hal",), ("pos",), ("hsel",)], writes=[("hsel",)])
        return hsel

    def mixer_lru(self, l, j):
        kb, cfg, d = self.kb, self.cfg, self.d
        NP, T = cfg.NP, cfg.T
        C = NP + 38
        PS0, SS0 = 3, NP + 6
        SMALL = [("lsm", [BS, NBLK, 12], F32), ("coef", [BS, NBLK], F32), ("gw", [BS, NBLK, 160], BF16),
                 ("utail", [128, NCH, 15], F32), ("uh", [128, NCH, 4], BF16), ("hal", [128, 4, 120], F32),
                 ("hsel", [128, 120], F32), ("cin", [BS, NBLK, 2], F32), ("cg", [BS, 4, 32], F32),
                 ("Hc", [BS, 4, NBLK], F32), ("h0p", [BS, NBLK], F32), ("lo", [BS, NBLK, 8], F32)]
        smalln = [x[0] for x in SMALL]
        V = self.carve(SMALL + self.NORM_SPEC)
        lsm, coef, gw = V["lsm"], V["coef"], V["gw"]
        lo, cin, h0p = V["lo"], V["cin"], V["h0p"]
        uh = V["uh"]
        self.norm_to_uT(V, 1, tail=V["utail"])
        hsel = self.halo_exchange(V, V["utail"])
        kb.op("dve", lambda e: e.tensor_copy(out=uh[:, :, 0:3],
                                             in_=hsel.rearrange("p (c t) -> p c t", t=15)[:, :, 12:15]),
              reads=[("hsel",)], writes=[("uh",)])
        kb.dma("sp", lsm, d["lru_small"][j], writes=[("lsm",)])
        kb.dma("pool", gw, d["lru_g"][j], writes=[("gw",)])
        kb.op("act", lambda e: e.activation(out=coef, in_=lsm[:, :, 7], func=AF.Exp, scale=-1.0),
              reads=[("lsm",)], writes=[("coef",)])
        kb.op("act", lambda e: e.activation(out=coef, in_=coef, func=AF.Ln, bias=1.0), reads=[("coef",)], writes=[("coef",)])
        kb.op("dve", lambda e: e.tensor_scalar(out=coef, in0=coef, scalar1=-8.0, scalar2=None, op0=ALU.mult),
              reads=[("coef",)], writes=[("coef",)])
        V1 = self.carve(SMALL + [("W0a", [BS, C], F32), ("W1a", [BS, C], F32), ("W2a", [BS, C], F32),
                                 ("xcba", [BS, C], BF16), ("W0b", [BS, C], F32), ("W1b", [BS, C], F32),
                                 ("W2b", [BS, C], F32), ("xcbb", [BS, C], BF16), ("rs4", [BS, 2, 8], F32),
                                 ("rsum", [BS, 2], F32)], keep=smalln)
        sets1 = [dict(W0=V1["W0" + t], W1=V1["W1" + t], W2=V1["W2" + t], xcb=V1["xcb" + t],
                      rs4=V1["rs4"][:, i_, :], rsum=V1["rsum"][:, i_:i_ + 1],
                      kW0=("W0" + t,), kW1=("W1" + t,), kW2=("W2" + t,), kxcb=("xcb" + t,),
                      krs4=("rs4", i_), krsum=("rsum", i_)) for i_, t in enumerate("ab")]

        pieces = [(PS0 + t0, n, t0) for (t0, n, s) in cfg.tiles[:-1]]
        tailpiece = (PS0 + NP, 38)
        npt = len(pieces)

        def block2(hb):
            S_ = sets2[hb % 2]
            W0, W2, kW0, kW2 = S_["W0"], S_["W2"], S_["kW0"], S_["kW2"]
            W1, zT, ggt = P2["W1"], P2["zT"], P2["ggt"]
            w, wk = self.rload(d["lru_win"][j, hb][:, :, 0:80], [128, NCH, 80])
            kb.dma("sp", W2[:, 3:C], self.lru_a.ap()[hb][:, 3:C], reads=[("lra", hb)], writes=[kW2])
            kb.dma("sp", W0[:, 3:C], self.lru_b.ap()[hb][:, 3:C], reads=[("lrb", hb)], writes=[kW0])
            kb.op("dve", lambda e: e.tensor_tensor_scan(out=W1[:, PS0:PS0 + NP], data0=W2[:, PS0:PS0 + NP],
                                                        data1=W0[:, PS0:PS0 + NP], initial=h0p[:, hb:hb + 1],
                                                        op0=ALU.mult, op1=ALU.add),
                  reads=[kW2, kW0, ("h0p",)], writes=[("W1",)])
            kb.op("dve", lambda e: e.tensor_tensor_scan(out=W1[:, SS0:SS0 + 32], data0=W2[:, SS0:SS0 + 32],
                                                        data1=W0[:, SS0:SS0 + 32], initial=lsm[:, hb, 8:9],
                                                        op0=ALU.mult, op1=ALU.add),
                  reads=[kW2, kW0, ("lsm",)], writes=[("W1",)])
            kb.op("dve", lambda e: e.tensor_copy(out=lo[:, hb, 0:1], in_=W1[:, PS0 + NP - 1:PS0 + NP]),
                  reads=[("W1",)], writes=[("lo",)])
            kb.op("dve", lambda e: e.tensor_copy(out=lo[:, hb, 1:2], in_=W1[:, C - 1:C]),
                  reads=[("W1",)], writes=[("lo",)])
            zsrc = [(PS0 + t0, n, t0, ti) for ti, (t0, n, s) in enumerate(cfg.tiles[:-1])]
            zsrc.append((SS0, 32, NP, len(cfg.tiles) - 1))
            for (c0, n, t0, ti) in zsrc:
                k = self.nxt() % 2
                pp = self.ps[k]
                for kc in range(NCH):
                    kb.op("pe", lambda e, pp=pp, kc=kc, n=n, t0=t0: e.matmul(
                        pp[0:BS, :n], lhsT=w[:, kc, 0:80], rhs=self.uT[:, kc, t0:t0 + n],
                        start=(kc == 0), stop=(kc == NCH - 1)),
                        reads=[wk, ("uT", kc, ti)], writes=[("ps", k)])
                kb.op("act", lambda e, pp=pp, k=k, n=n: e.activation(out=ggt[:, k, :n], in_=pp[0:BS, :n],
                                                                     func=AF.Gelu),
                      reads=[("ps", k)], writes=[("ggt", k)])
                kb.op("dve", lambda e, k=k, n=n, c0=c0, t0=t0: e.tensor_tensor(
                    out=zT[:, hb % 4, t0:t0 + n], in0=ggt[:, k, :n], in1=W1[:, c0:c0 + n], op=ALU.mult),
                    reads=[("ggt", k), ("W1",)], writes=[("zT", hb % 4, ti)])

        def blockA(hb):
            S_ = sets1[hb % 2]
            W0, W1, W2, xcb, rs4, rsum = S_["W0"], S_["W1"], S_["W2"], S_["xcb"], S_["rs4"], S_["rsum"]
            kW0, kW1, kW2, kxcb, krs4, krsum = S_["kW0"], S_["kW1"], S_["kW2"], S_["kxcb"], S_["krs4"], S_["krsum"]
            w, wk = self.rload(d["lru_win"][j, hb][:, :, 80:160], [128, NCH, 80])
            xsrc = [(PS0 + t0, n, self.uT, t0, ti) for ti, (t0, n, s) in enumerate(cfg.tiles[:-1])]
            xsrc.append((SS0, 32, self.uT, NP, len(cfg.tiles) - 1))
            xsrc.append((0, 3, uh, 0, None))
            for (c0, n, src, s0, ti) in xsrc:
                k = self.nxt() % 2
                pp = self.ps[4 + k]
                for kc in range(NCH):
                    kb.op("pe", lambda e, pp=pp, kc=kc, n=n, src=src, s0=s0: e.matmul(
                        pp[0:BS, :n], lhsT=w[:, kc, 0:80], rhs=src[:, kc, s0:s0 + n],
                        start=(kc == 0), stop=(kc == NCH - 1)),
                        reads=[wk, ("uT", kc, ti) if ti is not None else ("uh",)], writes=[("ps", 4 + k)])
                kb.op("act", lambda e, pp=pp, c0=c0, n=n: e.activation(out=W0[:, c0:c0 + n], in_=pp[0:BS, :n],
                                                                       func=AF.Identity),
                      reads=[("ps", 4 + k)], writes=[kW0])
            kb.op("dve", lambda e: e.tensor_copy(out=W0[:, NP + 3:NP + 6], in_=lsm[:, hb, 9:12]),
                  reads=[("lsm",)], writes=[kW0])
            kb.op("dve", lambda e: e.tensor_copy(out=lo[:, hb, 2:5], in_=W0[:, NP:NP + 3]),
                  reads=[kW0], writes=[("lo",)])
            kb.op("dve", lambda e: e.tensor_copy(out=lo[:, hb, 5:8], in_=W0[:, C - 3:C]),
                  reads=[kW0], writes=[("lo",)])
            kb.op("act", lambda e: e.activation(out=W1[:, 3:C], in_=W0[:, 0:C - 3], func=AF.Identity,
                                                scale=lsm[:, hb, 0:1], bias=lsm[:, hb, 4:5]),
                  reads=[kW0, ("lsm",)], writes=[kW1])

        def blockA2(hb):
            S_ = sets1[hb % 2]
            W0, W1, W2, xcb, rs4, rsum = S_["W0"], S_["W1"], S_["W2"], S_["xcb"], S_["rs4"], S_["rsum"]
            kW0, kW1, kW2, kxcb, krs4, krsum = S_["kW0"], S_["kW1"], S_["kW2"], S_["kxcb"], S_["krs4"], S_["krsum"]
            for kk in range(1, 4):
                kb.op("dve", lambda e, kk=kk: e.scalar_tensor_tensor(
                    out=W1[:, 3:C], in0=W0[:, kk:C - 3 + kk], scalar=lsm[:, hb, kk:kk + 1], in1=W1[:, 3:C],
                    op0=ALU.mult, op1=ALU.add), reads=[kW0, kW1, ("lsm",)], writes=[kW1])
            kb.op("act", lambda e: e.activation(out=xcb[:, 3:C], in_=W1[:, 3:C], func=AF.Identity),
                  reads=[kW1], writes=[kxcb])

        def blockB(hb, full):
            S_ = sets1[hb % 2]
            W0, W1, W2, xcb, rs4, rsum = S_["W0"], S_["W1"], S_["W2"], S_["xcb"], S_["rs4"], S_["rsum"]
            kW0, kW1, kW2, kxcb, krs4, krsum = S_["kW0"], S_["kW1"], S_["kW2"], S_["kxcb"], S_["krs4"], S_["krsum"]
            gp = [(PS0 + t0, n) for (t0, n, s) in cfg.tiles[:-1]] + [(PS0 + NP, 35)]
            for gi, (c0, n) in enumerate(gp):
                k = self.nxt() % 2
                pr, pi = self.ps[k], self.ps[2 + k]
                kb.op("pe", lambda e, pr=pr, c0=c0, n=n: e.matmul(pr[0:BS, :n], lhsT=gw[:, hb, 0:80],
                                                                 rhs=xcb[:, c0:c0 + n], start=True, stop=True),
                      reads=[("gw",), kxcb], writes=[("ps", k)])
                kb.op("pe", lambda e, pi=pi, c0=c0, n=n: e.matmul(pi[0:BS, :n], lhsT=gw[:, hb, 80:160],
                                                                 rhs=xcb[:, c0:c0 + n], start=True, stop=True),
                      reads=[("gw",), kxcb], writes=[("ps", 2 + k)])
                if (not full) and gi < npt:
                    kb.op("act", lambda e, pr=pr, c0=c0, n=n, gi=gi: e.activation(
                        out=W2[:, c0:c0 + n], in_=pr[0:BS, :n], func=AF.Sigmoid, bias=lsm[:, hb, 5:6], scale=1.0,
                        accum_out=rs4[:, gi:gi + 1]),
                        reads=[("ps", k), ("lsm",)], writes=[kW2, krs4])
                else:
                    kb.op("act", lambda e, pr=pr, c0=c0, n=n: e.activation(
                        out=W2[:, c0:c0 + n], in_=pr[0:BS, :n], func=AF.Sigmoid, bias=lsm[:, hb, 5:6], scale=1.0),
                        reads=[("ps", k), ("lsm",)], writes=[kW2])
                kb.op("act", lambda e, pi=pi, c0=c0, n=n: e.activation(
                    out=W0[:, c0:c0 + n], in_=pi[0:BS, :n], func=AF.Sigmoid, bias=lsm[:, hb, 6:7], scale=1.0),
                    reads=[("ps", 2 + k), ("lsm",), ("lo",)], writes=[kW0])

        def blockB2(hb, full):
            S_ = sets1[hb % 2]
            W0, W1, W2, xcb, rs4, rsum = S_["W0"], S_["W1"], S_["W2"], S_["xcb"], S_["rs4"], S_["rsum"]
            kW0, kW1, kW2, kxcb, krs4, krsum = S_["kW0"], S_["kW1"], S_["kW2"], S_["kxcb"], S_["krs4"], S_["krsum"]
            kb.op("dve", lambda e: e.tensor_tensor(out=W0[:, 3:C], in0=W0[:, 3:C], in1=W1[:, 3:C], op=ALU.mult),
                  reads=[kW0, kW1], writes=[kW0])
            kb.op("act", lambda e: e.activation(out=W2[:, 3:C], in_=W2[:, 3:C], func=AF.Exp, scale=coef[:, hb:hb + 1]),
                  reads=[kW2, ("coef",)], writes=[kW2])
            kb.op("dve", lambda e: e.tensor_tensor(out=W1[:, 3:C], in0=W2[:, 3:C], in1=W2[:, 3:C], op=ALU.mult),
                  reads=[kW2, kW1, kxcb], writes=[kW1])
            kb.op("act", lambda e: e.activation(out=W1[:, 3:C], in_=W1[:, 3:C], func=AF.Sqrt, scale=-1.0, bias=1.0),
                  reads=[kW1], writes=[kW1])
            kb.op("dve", lambda e: e.tensor_tensor(out=W0[:, 3:C], in0=W0[:, 3:C], in1=W1[:, 3:C], op=ALU.mult),
                  reads=[kW0, kW1], writes=[kW0])
            if full:
                kb.op("dve", lambda e: e.tensor_tensor_scan(out=W1[:, PS0:PS0 + NP], data0=W2[:, PS0:PS0 + NP],
                                                            data1=W0[:, PS0:PS0 + NP], initial=h0p[:, hb:hb + 1],
                                                            op0=ALU.mult, op1=ALU.add),
                      reads=[kW2, kW0, ("h0p",)], writes=[kW1])
                kb.op("dve", lambda e: e.tensor_tensor_scan(out=W1[:, SS0:SS0 + 32], data0=W2[:, SS0:SS0 + 32],
                                                            data1=W0[:, SS0:SS0 + 32], initial=lsm[:, hb, 8:9],
                                                            op0=ALU.mult, op1=ALU.add),
                      reads=[kW2, kW0, ("lsm",)], writes=[kW1])
                kb.op("dve", lambda e: e.tensor_copy(out=lo[:, hb, 0:1], in_=W1[:, PS0 + NP - 1:PS0 + NP]),
                      reads=[kW1], writes=[("lo",)])
                kb.op("dve", lambda e: e.tensor_copy(out=lo[:, hb, 1:2], in_=W1[:, C - 1:C]),
                      reads=[kW1], writes=[("lo",)])
                zsrc = [(PS0 + t0, n, t0, ti) for ti, (t0, n, s) in enumerate(cfg.tiles[:-1])]
                zsrc.append((SS0, 32, NP, len(cfg.tiles) - 1))
                for (c0, n, t0, ti) in zsrc:
                    k = self.nxt() % 2
                    pp = self.ps[k]
                    for kc in range(NCH):
                        kb.op("pe", lambda e, pp=pp, kc=kc, n=n, t0=t0: e.matmul(
                            pp[0:BS, :n], lhsT=w[:, kc, 0:80], rhs=self.uT[:, kc, t0:t0 + n],
                            start=(kc == 0), stop=(kc == NCH - 1)),
                            reads=[wk, ("uT", kc, ti)], writes=[("ps", k)])
                    kb.op("act", lambda e, pp=pp, k=k, n=n: e.activation(out=V["ggt"][:, k, :n], in_=pp[0:BS, :n],
                                                                         func=AF.Gelu),
                          reads=[("ps", k)], writes=[("ggt", k)])
                    kb.op("dve", lambda e, k=k, n=n, c0=c0, t0=t0: e.tensor_tensor(
                        out=zT[:, hb % 4, t0:t0 + n], in0=V["ggt"][:, k, :n], in1=W1[:, c0:c0 + n], op=ALU.mult),
                        reads=[("ggt", k), kW1], writes=[("zT", hb % 4, ti)])
            else:
                kb.op("dve", lambda e: e.tensor_tensor_scan(out=W1[:, PS0:PS0 + NP], data0=W2[:, PS0:PS0 + NP],
                                                            data1=W0[:, PS0:PS0 + NP], initial=0.0,
                                                            op0=ALU.mult, op1=ALU.add),
                      reads=[kW2, kW0], writes=[kW1])
                kb.op("dve", lambda e: e.tensor_copy(out=cin[:, hb, 1:2], in_=W1[:, PS0 + NP - 1:PS0 + NP]),
                      reads=[kW1], writes=[("cin",)])
                kb.op("dve", lambda e: e.tensor_reduce(out=rsum, in_=rs4[:, 0:npt], axis=AX.X, op=ALU.add),
                      reads=[krs4], writes=[krsum])
                kb.op("act", lambda e: e.activation(out=cin[:, hb, 0:1], in_=rsum, func=AF.Exp,
                                                    scale=coef[:, hb:hb + 1]),
                      reads=[krsum, ("coef",)], writes=[("cin",)])
                kb.dma("sp", self.lru_a.ap()[hb][:, 3:C], W2[:, 3:C], reads=[kW2], writes=[("lra", hb)])
                kb.dma("sp", self.lru_b.ap()[hb][:, 3:C], W0[:, 3:C], reads=[kW0], writes=[("lrb", hb)])

        blockA(0)
        blockA2(0)
        for hb in range(NBLK):
            if hb + 1 < NBLK:
                blockA(hb + 1)
            blockB(hb, False)
            if hb + 1 < NBLK:
                blockA2(hb + 1)
            blockB2(hb, False)
        cg, Hc = V["cg"], V["Hc"]
        kb.dma("sp", self.ag_c_in.ap(), cin.rearrange("p a b -> p (a b)"), reads=[("cin",)], writes=[("ag_c_in",)])
        kb.collective(self.groups, self.ag_c_in.ap().opt(), self.ag_c_out.ap().opt(),
                      reads=[("ag_c_in",)], writes=[("ag_c_out",)])
        kb.dma("sp", cg, self.ag_c_out.ap().rearrange("(r p) f -> p r f", p=BS), reads=[("ag_c_out",)], writes=[("cg",)])
        cg4 = cg.rearrange("p r (b t) -> p r b t", t=2)
        kb.op("dve", lambda e: e.tensor_copy(out=Hc[:, 1, :], in_=cg4[:, 0, :, 1]), reads=[("cg",)], writes=[("Hc",)])
        for q in (1, 2):
            kb.op("dve", lambda e, q=q: e.tensor_tensor(out=Hc[:, q + 1, :], in0=cg4[:, q, :, 0], in1=Hc[:, q, :],
                                                        op=ALU.mult), reads=[("cg",), ("Hc",)], writes=[("Hc",)])
            kb.op("dve", lambda e, q=q: e.tensor_tensor(out=Hc[:, q + 1, :], in0=Hc[:, q + 1, :], in1=cg4[:, q, :, 1],
                                                        op=ALU.add), reads=[("cg",), ("Hc",)], writes=[("Hc",)])
        kb.op("dve", lambda e: e.tensor_scalar(out=h0p, in0=Hc[:, 1, :], scalar1=self.pos[0:BS, 5:6], scalar2=None,
                                               op0=ALU.mult), reads=[("Hc",), ("pos",)], writes=[("h0p",)])
        for q in (2, 3):
            kb.op("dve", lambda e, q=q: e.scalar_tensor_tensor(
                out=h0p, in0=Hc[:, q, :], scalar=self.pos[0:BS, 4 + q:5 + q], in1=h0p, op0=ALU.mult, op1=ALU.add),
                reads=[("Hc",), ("pos",), ("h0p",)], writes=[("h0p",)])
        P2 = self.carve(SMALL + [("W0a", [BS, C], F32), ("W2a", [BS, C], F32), ("W0b", [BS, C], F32),
                                 ("W2b", [BS, C], F32), ("W1", [BS, C], F32), ("ggt", [BS, 2, 512], F32),
                                 ("zT", [BS, 4, T], BF16)], keep=smalln)
        sets2 = [dict(W0=P2["W0" + t], W2=P2["W2" + t], kW0=("W0" + t,), kW2=("W2" + t,)) for t in "ab"]
        zT = P2["zT"]
        for hb in range(NBLK):
            block2(hb)
            if hb % 4 == 3:
                w2 = [self.rload(d["lru_wout"][j, (hb - 3) // 2 + q], [BS, 2, D]) for q in range(2)]
                for ti, (t0, n, s) in enumerate(cfg.tiles):
                    for dc in range(NCH):
                        k = self.nxt() % 2
                        bank = 4 + k
                        py = self.ps[bank]
                        for b in range(4):
                            wv, wk2 = w2[b // 2]
                            kb.op("pe", lambda e, wv=wv, b=b, dc=dc, py=py, t0=t0, n=n: e.matmul(
                                py[:, :n], lhsT=wv[:, b % 2, dc * 128:(dc + 1) * 128], rhs=zT[:, b, t0:t0 + n],
                                start=(b == 0), stop=(b == 3)),
                                reads=[wk2, ("zT", b, ti)], writes=[("ps", bank)])
                        kb.op("dve", lambda e, py=py, dc=dc, t0=t0, n=n, s=s: e.scalar_tensor_tensor(
                            out=self.xT[:, dc, t0:t0 + n], in0=py[:, :n], scalar=self.mod[:, 40 + dc, s:s + 1],
                            in1=self.xT[:, dc, t0:t0 + n], op0=ALU.mult, op1=ALU.add),
                            reads=[("ps", bank), self.kmod, ("xT", dc, ti)], writes=[("xT", dc, ti)])
        self.final.append(kb.dma("sp", d["lru_o"][j], lo, reads=[("lo",)]))

    def mixer_pool(self, l, j):
        kb, cfg, d = self.kb, self.cfg, self.d
        NP, T = cfg.NP, cfg.T
        NT = len(cfg.tiles)
        CE = NP + 62
        V = self.carve([("nsq", [128, 2, 512], F32), ("rall", [128, T], F32), ("ue", [128, 2, CE], F32),
                        ("Sa", [128, CE], F32), ("Sb", [128, CE], F32), ("utail", [128, NCH, 15], F32),
                        ("hal", [128, 4, 120], F32), ("hsel", [128, 120], F32), ("psm", [128, 136], F32),
                        ("pk", [128, 2, NCH, 2], F32), ("pso", [128, NCH, 30], F32), ("t15", [128, 16], F32),
                        ("ptmp", [128, 2, 512], F32)])
        rall, ue, utail, psm, pk, pso = V["rall"], V["ue"], V["utail"], V["psm"], V["pk"], V["pso"]
        kb.dma("sp", psm, d["pool_small"][j], writes=[("psm",)])
        kb.op("dve", lambda e: e.tensor_tensor(out=pk[:, 0, :, :], in0=self.mod[:, 40:48, :],
                                               in1=psm[:, 8:16].unsqueeze(2).to_broadcast([128, 8, 2]), op=ALU.mult),
              reads=[self.kmod, ("psm",)], writes=[("pk",)])
        kb.op("dve", lambda e: e.tensor_tensor(out=pk[:, 1, :, :], in0=pk[:, 0, :, :],
                                               in1=psm[:, 0:8].unsqueeze(2).to_broadcast([128, 8, 2]), op=ALU.mult),
              reads=[("pk",), ("psm",)], writes=[("pk",)])
        for ti, (t0, n, s) in enumerate(cfg.tiles):
            pb = self.norm_stats(V, ti, None)
            kb.op("act", lambda e, pb=pb, t0=t0, n=n: e.activation(out=rall[:, t0:t0 + n], in_=pb[:, :n], func=AF.Sqrt,
                                                                   scale=1.0 / D, bias=EPS),
                  reads=[("ps", 6)], writes=[("rall", ti)])
            kb.op("dve", lambda e, t0=t0, n=n: e.reciprocal(out=rall[:, t0:t0 + n], in_=rall[:, t0:t0 + n]),
                  reads=[("rall", ti)], writes=[("rall", ti)])
        for c in range(NCH):
            kb.op("dve", lambda e, c=c: e.scalar_tensor_tensor(
                out=utail[:, c, :], in0=self.xT[:, c, NP - 15:NP], scalar=self.gs[:, 1, c, 0:1],
                in1=rall[:, NP - 15:NP], op0=ALU.mult, op1=ALU.mult),
                reads=[("xT", c, NT - 2), self.kgs, ("rall", NT - 2)], writes=[("utail",)])
            kb.op("act", lambda e, c=c: e.activation(out=utail[:, c, :], in_=utail[:, c, :], func=AF.Identity,
                                                     bias=self.mod[:, 24 + c, 0:1], scale=1.0),
                  reads=[("utail",), self.kmod], writes=[("utail",)])
        hsel = self.halo_exchange(V, utail)
        kb.op("dve", lambda e: e.tensor_copy(out=pso[:, :, 0:15], in_=utail), reads=[("utail",)], writes=[("pso",)])
        for c in range(NCH):
            g = c // 2
            wnd = 2 ** (g + 1)
            k = c % 2
            uc = ue[:, k, :]
            uk = ("ue", k)
            kb.op("dve", lambda e, uc=uc, c=c: e.tensor_copy(out=uc[:, 0:15], in_=hsel[:, c * 15:(c + 1) * 15]),
                  reads=[("hsel",)], writes=[uk])
            kb.op("dve", lambda e, uc=uc, c=c: e.tensor_copy(out=uc[:, 15 + NP:30 + NP],
                                                             in_=psm[:, 16 + c * 15:16 + (c + 1) * 15]),
                  reads=[("psm",)], writes=[uk])
            for (u0, x0, n, s) in ((15, 0, NP, 0), (30 + NP, NP, 32, 1)):
                kb.op("dve", lambda e, uc=uc, c=c, u0=u0, x0=x0, n=n, s=s: e.scalar_tensor_tensor(
                    out=V["Sa"][:, u0:u0 + n], in0=self.xT[:, c, x0:x0 + n], scalar=self.gs[:, 1, c, s:s + 1],
                    in1=rall[:, x0:x0 + n], op0=ALU.mult, op1=ALU.mult),
                    reads=[("xT", c, ti) for ti in range(NT)] + [self.kgs] + [("rall", ti) for ti in range(NT)],
                    writes=[("Sa",)])
                kb.op("act", lambda e, uc=uc, c=c, u0=u0, n=n, s=s: e.activation(
                    out=uc[:, u0:u0 + n], in_=V["Sa"][:, u0:u0 + n], func=AF.Identity,
                    bias=self.mod[:, 24 + c, s:s + 1], scale=1.0),
                    reads=[("Sa",), self.kmod], writes=[uk])
            kb.op("dve", lambda e, uc=uc, c=c: e.tensor_copy(out=pso[:, c, 15:30], in_=uc[:, CE - 15:CE]),
                  reads=[uk], writes=[("pso",)])
            cur, curk = uc, uk
            bufs = [(V["Sa"], ("Sa",)), (V["Sb"], ("Sb",))]
            step = 1
            bi = 0
            while step < wnd:
                nb_, nk = bufs[bi % 2]
                kb.op("dve", lambda e, cur=cur, nb_=nb_, step=step: e.tensor_tensor(
                    out=nb_[:, step:CE], in0=cur[:, step:CE], in1=cur[:, 0:CE - step], op=ALU.add),
                    reads=[curk], writes=[nk])
                if step > 1 or True:
                    pass
                cur, curk = nb_, nk
                step *= 2
                bi += 1
            fin, fink = cur, curk
            inv = 1.0 / wnd
            for (u0, x0, n, ti) in ((15, 0, NP, None), (30 + NP, NP, 32, NT - 1)):
                kb.op("dve", lambda e, fin=fin, uc=uc, c=c, u0=u0, x0=x0, n=n: e.scalar_tensor_tensor(
                    out=self.uT[:, c, x0:x0 + n], in0=fin[:, u0:u0 + n], scalar=inv, in1=uc[:, u0:u0 + n],
                    op0=ALU.mult, op1=ALU.subtract),
                    reads=[fink, uk], writes=[("uT", c, t) for t in (range(NT - 1) if ti is None else [ti])])
            kb.op("dve", lambda e, fin=fin, c=c: e.tensor_tensor(out=V["t15"][:, 0:15], in0=fin[:, 15:30],
                                                                 in1=self.pos[:, 16 + c * 15:16 + (c + 1) * 15],
                                                                 op=ALU.mult),
                  reads=[fink, ("pos",)], writes=[("t15",)])
            kb.op("dve", lambda e, uc=uc, c=c: e.scalar_tensor_tensor(
                out=self.uT[:, c, 0:15], in0=V["t15"][:, 0:15], scalar=inv, in1=uc[:, 15:30],
                op0=ALU.mult, op1=ALU.subtract), reads=[("t15",), uk], writes=[("uT", c, 0)])
        self.final.append(kb.dma("sp", d["pool_o"][j], pso, reads=[("pso",)]))
        for g in range(4):
            wg, wk = self.rload(d["pool_w"][j, g], [128, 2, 256])
            for ti, (t0, n, s) in enumerate(cfg.tiles):
                for oc in range(2):
                    k = self.nxt() % 2
                    bank = 4 + k
                    py = self.ps[bank]
                    cc = 2 * g + oc
                    for ic in range(2):
                        kb.op("pe", lambda e, py=py, ic=ic, oc=oc, t0=t0, n=n: e.matmul(
                            py[:, :n], lhsT=wg[:, ic, oc * 128:(oc + 1) * 128], rhs=self.uT[:, 2 * g + ic, t0:t0 + n],
                            start=(ic == 0), stop=(ic == 1)),
                            reads=[wk, ("uT", 2 * g + ic, ti)], writes=[("ps", bank)])
                    kb.op("act", lambda e, py=py, k=k, n=n, cc=cc, s=s: e.activation(
                        out=V["ptmp"][:, k, :n], in_=py[:, :n], func=AF.Identity, scale=pk[:, 0, cc, s:s + 1],
                        bias=pk[:, 1, cc, s:s + 1]), reads=[("ps", bank), ("pk",)], writes=[("ptmp", k)])
                    kb.op("dve", lambda e, k=k, n=n, cc=cc, t0=t0: e.tensor_tensor(
                        out=self.xT[:, cc, t0:t0 + n], in0=self.xT[:, cc, t0:t0 + n], in1=V["ptmp"][:, k, :n],
                        op=ALU.add), reads=[("ptmp", k), ("xT", cc, ti)], writes=[("xT", cc, ti)])

    def mixer_attn(self, l, j):
        kb, cfg, d = self.kb, self.cfg, self.d
        NP, T, TA = cfg.NP, cfg.T, cfg.TA
        NT = len(cfg.tiles)
        NTB = NP // 128
        NG = NP // 512
        lam_init = 0.8 - 0.6 * math.exp(-0.3 * l)
        KEEP = [("qT", [128, NHEAD, TA], BF16), ("ksT", [128, NHEAD, 128], BF16), ("vs", [128, NHEAD, 128], BF16),
                ("asm", [128, 517], F32), ("neglam", [128, 2], F32), ("subg", [128, 1], F32)]
        keepn = [k[0] for k in KEEP]
        TILES = [("T0", [128, NHEAD, 128], F32), ("T1", [128, NHEAD, 128], F32), ("Ts0", [128, NHEAD, 32], F32),
                 ("bcol", [128, NHEAD, 3], F32), ("col15", [128, NHEAD, 3], F32)]
        V = self.carve(KEEP + TILES + [("rbb", [128, 256], F32)] + self.NORM_SPEC + [
            ("sq", [128, 2, 256], F32), ("ss", [128, 2, 4], F32), ("qn", [128, 2, 256], F32),
            ("vb", [128, 2, 2, 512], BF16), ("kTt", [128, 2, 2, 512], BF16), ("lpr", [128, 2], F32)])
        qT, ksT, vs, asm = V["qT"], V["ksT"], V["vs"], V["asm"]
        lpr, neglam1, subg1 = V["lpr"], V["neglam"], V["subg"]
        self.norm_to_uT(V, 1)
        T0, T1, Ts0, bcol, col15, rbb = (V[k_] for k_ in ("T0", "T1", "Ts0", "bcol", "col15", "rbb"))
        kb.dma("sp", rbb, d["rb_rep"], writes=[("rbb",)])
        pslots = []
        for q4 in range(4):
            npl = 16 if q4 < 3 else 1
            pslots.append(self.rload(d["planes"][:, q4 * 2048:q4 * 2048 + npl * 128], [128, npl * 128], q="sp", reserve=True))
        pres = [int(pk_[1][1]) for pk_ in pslots]

        def plane(b_):
            v_, k_ = pslots[b_ // 16]
            return v_[:, (b_ % 16) * 128:(b_ % 16 + 1) * 128], k_
        tile_ops = []
        for h in range(NHEAD):
            pv, pk_ = plane(48)
            tile_ops.append(lambda h=h, pv=pv, pk_=pk_: kb.op("dve", lambda e: e.tensor_scalar(
                out=T0[:, h, :], in0=pv, scalar1=NEG, scalar2=None, op0=ALU.mult), reads=[pk_], writes=[("T0", h)]))
            for b in range(32):
                pv, pk_ = plane(b)
                tile_ops.append(lambda h=h, b=b, pv=pv, pk_=pk_: kb.op("dve", lambda e: e.scalar_tensor_tensor(
                    out=T0[:, h, :], in0=pv, scalar=rbb[:, b * 8 + h:b * 8 + h + 1], in1=T0[:, h, :],
                    op0=ALU.mult, op1=ALU.add), reads=[pk_, ("rbb",), ("T0", h)], writes=[("T0", h)]))
            pv, pk_ = plane(32)
            tile_ops.append(lambda h=h, pv=pv, pk_=pk_: kb.op("dve", lambda e: e.tensor_scalar(
                out=T1[:, h, :], in0=pv, scalar1=rbb[:, h:h + 1], scalar2=None, op0=ALU.mult),
                reads=[pk_, ("rbb",)], writes=[("T1", h)]))
            for b in range(1, 16):
                pv, pk_ = plane(32 + b)
                tile_ops.append(lambda h=h, b=b, pv=pv, pk_=pk_: kb.op("dve", lambda e: e.scalar_tensor_tensor(
                    out=T1[:, h, :], in0=pv, scalar=rbb[:, b * 8 + h:b * 8 + h + 1], in1=T1[:, h, :],
                    op0=ALU.mult, op1=ALU.add), reads=[pk_, ("rbb",), ("T1", h)], writes=[("T1", h)]))
        tile_ops.append(lambda: kb.op("dve", lambda e: e.memset(Ts0[:], NEG), writes=[("Ts0",)]))
        tile_ops.append(lambda: kb.op("dve", lambda e: e.tensor_copy(out=Ts0[0:32, :, :], in_=T0[0:32, :, 0:32]),
                                      reads=[("T0", h) for h in range(NHEAD)] + [("Ts0",)], writes=[("Ts0",)]))
        c15 = rbb[:, 120:128]
        for s_ in range(3):
            tile_ops.append(lambda s=s_: kb.op("dve", lambda e: e.tensor_scalar(
                out=bcol[:, :, s], in0=c15, scalar1=self.pos[:, 8 + s:9 + s], scalar2=None, op0=ALU.add),
                reads=[("rbb",), ("pos",)], writes=[("bcol",)]))
            tile_ops.append(lambda s=s_: kb.op("dve", lambda e: e.tensor_scalar(
                out=col15[:, :, s], in0=c15, scalar1=self.pos[:, 11 + s:12 + s], scalar2=None, op0=ALU.mult),
                reads=[("rbb",), ("pos",)], writes=[("col15",)]))
            tile_ops.append(lambda s=s_: kb.op("dve", lambda e: e.tensor_tensor(
                out=col15[:, :, s], in0=bcol[:, :, s], in1=col15[:, :, s], op=ALU.subtract),
                reads=[("bcol",), ("col15",)], writes=[("col15",)]))
        tile_ops.reverse()

        def pull_tiles(n_):
            for _ in range(n_):
                if tile_ops:
                    tile_ops.pop()()
        kb.dma("sp", asm, d["attn_small"][j], writes=[("asm",)])
        kb.op("dve", lambda e: e.tensor_tensor(out=lpr[:, 0:1], in0=asm[:, 513:514], in1=asm[:, 514:515], op=ALU.mult),
              reads=[("asm",)], writes=[("lpr",)])
        kb.op("dve", lambda e: e.tensor_tensor(out=lpr[:, 1:2], in0=asm[:, 515:516], in1=asm[:, 516:517], op=ALU.mult),
              reads=[("asm",), ("lpr",)], writes=[("lpr",)])
        kb.op("pe", lambda e: e.matmul(self.ps[7][:, 0:2], lhsT=self.ones_f[:], rhs=lpr, start=True, stop=True),
              reads=[("lpr",), ("ones_f",)], writes=[("ps", 7)])
        kb.op("act", lambda e: e.activation(out=lpr, in_=self.ps[7][:, 0:2], func=AF.Exp),
              reads=[("ps", 7), ("lpr",)], writes=[("lpr",)])
        kb.op("dve", lambda e: e.tensor_tensor(out=neglam1[:, 0:1], in0=lpr[:, 1:2], in1=lpr[:, 0:1],
                                               op=ALU.subtract), reads=[("lpr",)], writes=[("neglam",)])
        kb.op("dve", lambda e: e.tensor_scalar(out=neglam1[:, 0:1], in0=neglam1[:, 0:1], scalar1=-lam_init,
                                               scalar2=None, op0=ALU.add), reads=[("neglam",)], writes=[("neglam",)])
        kb.op("dve", lambda e: e.tensor_scalar(out=subg1, in0=asm[:, 512:513], scalar1=1.0 - lam_init, scalar2=None,
                                               op0=ALU.mult), reads=[("asm",)], writes=[("subg",)])
        import os
        skip = os.environ.get("ATT_SKIP", "")
        prev_v = None
        deferred = []
        for cs in [4, 5, 6, 7, 8, 9, 10, 11, 0, 1, 2, 3]:
            if prev_v is not None:
                for h in (prev_v * 2, prev_v * 2 + 1):
                    kb.collective(self.groups, self.ag_kv_in[h].ap().opt(), self.ag_kv_out[h].ap().opt(),
                                  reads=[("akv", h, tb, kv) for tb in range(NTB) for kv in range(2)],
                                  writes=[("ag_kv_out", h)])
                prev_v = None
            if cs >= 8:
                prev_v = cs - 8
            if "p" in skip:
                break
            if "v" in skip and cs >= 8:
                break
            if "k" in skip and cs >= 4:
                break
            w, wk = self.rload(d["attn_win"][j, cs], [128, NCH, 256])
            kind = cs // 4
            hh0 = (cs % 4) * 2
            for tb in range(NTB + 1):
                pull_tiles(3)
                ti = tb // 4 if tb < NTB else NT - 1
                rows = 128 if tb < NTB else 32
                k = self.nxt() % 2
                pp = self.ps[k]
                for kc in range(NCH):
                    kb.op("pe", lambda e, pp=pp, kc=kc, tb=tb: e.matmul(
                        pp[:, 0:256], lhsT=self.uT[:, kc, tb * 128:(tb + 1) * 128], rhs=w[:, kc, :],
                        start=(kc == 0), stop=(kc == NCH - 1)),
                        reads=[wk, ("uT", kc, ti), ("uT_pad",)], writes=[("ps", k)])
                if deferred:
                    deferred.pop()()
                if kind == 2:
                    vf = V["qn"][:, k, :]
                    kb.op("act", lambda e, pp=pp, vf=vf: e.activation(out=vf, in_=pp[:, 0:256], func=AF.Identity),
                          reads=[("ps", k)], writes=[("qn", k)])
                    self.final.append(kb.dma("sp", d["v_o"][j, tb * 128:tb * 128 + rows, hh0 * 128:hh0 * 128 + 256],
                                             vf[0:rows, :], reads=[("qn", k)]))
                    if tb < NTB:
                        gset = (tb // 4) % 2
                        vb = V["vb"][:, gset, :, :]
                        kb.op("dve", lambda e, vf=vf, vb=vb, tb=tb: e.tensor_copy(
                            out=vb[:, :, (tb % 4) * 128:(tb % 4 + 1) * 128], in_=vf.rearrange("p (h d) -> p h d", h=2)),
                            reads=[("qn", k)], writes=[("vb", gset)])
                        if tb % 4 == 3:
                            for hh in range(2):
                                kb.dma("sp", self.ag_kv_in[hh0 + hh].ap()[128:256, (tb - 3) * 128:(tb + 1) * 128],
                                       vb[:, hh, :], reads=[("vb", gset)],
                                       writes=[("akv", hh0 + hh, t_, 1) for t_ in range(tb - 3, tb + 1)])
                    else:
                        kb.op("dve", lambda e, vf=vf: e.tensor_copy(
                            out=vs[:, hh0:hh0 + 2, :], in_=vf.rearrange("p (h d) -> p h d", h=2)),
                            reads=[("qn", k)], writes=[("vs",)])
                    continue
                sq, ss, qn = V["sq"][:, k, :], V["ss"][:, k, :], V["qn"][:, k, :]
                kb.op("act", lambda e, pp=pp, sq=sq: e.activation(out=sq, in_=pp[:, 0:256], func=AF.Square),
                      reads=[("ps", k)], writes=[("sq", k)])
                kb.op("dve", lambda e, sq=sq, ss=ss: e.tensor_reduce(out=ss, in_=sq.rearrange("p (g x) -> p g x", x=64),
                                                                     axis=AX.X, op=ALU.add),
                      reads=[("sq", k)], writes=[("ss", k)])
                kb.op("act", lambda e, ss=ss: e.activation(out=ss, in_=ss, func=AF.Sqrt, scale=1.0 / 64, bias=EPS),
                      reads=[("ss", k)], writes=[("ss", k)])
                kb.op("dve", lambda e, ss=ss: e.reciprocal(out=ss, in_=ss), reads=[("ss", k)], writes=[("ss", k)])
                kb.op("dve", lambda e, pp=pp, ss=ss, qn=qn: e.tensor_tensor(
                    out=qn.rearrange("p (g x) -> p g x", x=64), in0=pp[:, 0:256].rearrange("p (g x) -> p g x", x=64),
                    in1=ss.unsqueeze(2).to_broadcast([128, 4, 64]), op=ALU.mult),
                    reads=[("ps", k), ("ss", k)], writes=[("qn", k)])
                gcol = 0 if kind == 0 else 256
                kb.op("dve", lambda e, qn=qn, gcol=gcol: e.tensor_tensor(out=qn, in0=qn, in1=asm[:, gcol:gcol + 256],
                                                                         op=ALU.mult),
                      reads=[("qn", k), ("asm",)], writes=[("qn", k)])
                if kind == 1:
                    self.final.append(kb.dma("sp", d["k_o"][j, tb * 128:tb * 128 + rows, hh0 * 128:hh0 * 128 + 256],
                                             qn[0:rows, :], reads=[("qn", k)]))
                def X(k=k, qn=qn, kind=kind, tb=tb, hh0=hh0):
                    pt = self.ps[2 + k]
                    for hh in range(2):
                        kb.op("pe", lambda e, pt=pt, qn=qn, hh=hh: e.transpose(
                            out=pt[:, hh * 128:(hh + 1) * 128], in_=qn[:, hh * 128:(hh + 1) * 128], identity=self.ident[:]),
                            reads=[("qn", k), ("ident",)], writes=[("ps", 2 + k)])
                    ptv = pt[:, 0:256].rearrange("p (h t) -> p h t", h=2)
                    if kind == 0:
                        kb.op("act", lambda e, ptv=ptv, tb=tb: e.activation(
                            out=qT[:, hh0:hh0 + 2, tb * 128:(tb + 1) * 128], in_=ptv, func=AF.Identity),
                            reads=[("ps", 2 + k)], writes=[("qT", hh0, tb), ("qT", hh0 + 1, tb)])
                    elif tb < NTB:
                        gset = (tb // 4) % 2
                        kt = V["kTt"][:, gset, :, :]
                        kb.op("act", lambda e, ptv=ptv, kt=kt, tb=tb: e.activation(
                            out=kt[:, :, (tb % 4) * 128:(tb % 4 + 1) * 128], in_=ptv, func=AF.Identity),
                            reads=[("ps", 2 + k)], writes=[("kTt", gset)])
                        if tb % 4 == 3:
                            for hh in range(2):
                                kb.dma("sp", self.ag_kv_in[hh0 + hh].ap()[0:128, (tb - 3) * 128:(tb + 1) * 128],
                                       kt[:, hh, :], reads=[("kTt", gset)],
                                       writes=[("akv", hh0 + hh, t_, 0) for t_ in range(tb - 3, tb + 1)])
                    else:
                        kb.op("act", lambda e, ptv=ptv: e.activation(out=ksT[:, hh0:hh0 + 2, :], in_=ptv, func=AF.Identity),
                              reads=[("ps", 2 + k)], writes=[("ksT",)])
                deferred.append(X)
            while deferred:
                deferred.pop()()
        pull_tiles(100000)
        for i_ in pres:
            self.ring_reserved.discard(i_)
        import os
        stop = int(os.environ.get("ATT_STOP", "99"))
        tilen = [t[0] for t in TILES]
        V = self.carve(KEEP + TILES + [("c15s", [128, NHEAD], F32), ("E", [128, 4, 512], BF16),
                                       ("qpad", [128, 2, 2, 512], BF16), ("ft", [128, 4, 512], F32),
                                       ("stmp", [128, 2, 512], F32)], keep=keepn + tilen)
        qT, ksT, vs = V["qT"], V["ksT"], V["vs"]
        T0, T1, Ts0, bcol, col15 = (V[k] for k in ("T0", "T1", "Ts0", "bcol", "col15"))
        E, qpad, ft, stmp, c15s = V["E"], V["qpad"], V["ft"], V["stmp"], V["c15s"]
        neglam, subg = V["neglam"], V["subg"]
        kb.dma("sp", c15s, d["rb_rep"][:, 120:128], writes=[("c15s",)])
        for kq in range(2):
            kb.op("dve", lambda e, kq=kq: e.memset(qpad[64:128, kq, 0, :], 0.0), writes=[("qpad", kq)])
            kb.op("dve", lambda e, kq=kq: e.memset(qpad[0:64, kq, 1, :], 0.0), reads=[("qpad", kq)], writes=[("qpad", kq)])
        oT = self.uT

        def attend(h, qcol0, n, passes, okeys):
            kq = self.nxt() % 2
            qp = qpad[:, kq, :, :]
            qk = ("qpad", kq)
            kb.op("act", lambda e: e.activation(out=qp[0:64, 0, :n], in_=qT[0:64, h, qcol0:qcol0 + n], func=AF.Identity),
                  reads=[("qT", h, tb) for tb in range(qcol0 // 128, (qcol0 + n + 127) // 128)] + [qk], writes=[qk])
            kb.op("act", lambda e: e.activation(out=qp[64:128, 1, :n], in_=qT[64:128, h, qcol0:qcol0 + n], func=AF.Identity),
                  reads=[("qT", h, tb) for tb in range(qcol0 // 128, (qcol0 + n + 127) // 128)] + [qk], writes=[qk])
            nb = sum(pk_[0] for pk_ in passes)

            def gen():
                for (cnt_, loader) in passes:
                    ksl_, vsl_, blocks_ = loader()
                    assert len(blocks_) == cnt_
                    for b_ in blocks_:
                        yield (ksl_, vsl_), b_
            def emit_pv(bi, vsl, vv, kk):
                for c in range(2):
                    ek = ("E", kk * 2 + c)
                    Ec = E[:, kk * 2 + c, :]
                    kb.op("pe", lambda e, vv=vv, Ec=Ec, c=c, bi=bi: e.matmul(
                        self.ps[4 + c][:, :n], lhsT=vv, rhs=Ec[:, :n], start=(bi == 0), stop=(bi == nb - 1)),
                        reads=[vsl, ek], writes=[("ps", 4 + c)])
                    kb.op("pe", lambda e, Ec=Ec, c=c, bi=bi: e.matmul(
                        self.ps[6 + c][:, :n], lhsT=self.ones_b[:], rhs=Ec[:, :n], start=(bi == 0), stop=(bi == nb - 1)),
                        reads=[("ones_b",), ek], writes=[("ps", 6 + c)])

            pending = None
            for bi, ((ksl, vsl), (kT, vv, subs)) in enumerate(gen()):
                kk = bi % 2
                for c in range(2):
                    bank = kk * 2 + c
                    S = self.ps[bank]
                    kb.op("pe", lambda e, S=S, kT=kT, c=c: e.matmul(S[:, :n], lhsT=kT, rhs=qp[:, c, :n], start=True, stop=True),
                          reads=[ksl, qk], writes=[("ps", bank)])
                for c in range(2):
                    bank = kk * 2 + c
                    S = self.ps[bank]
                    ek = ("E", kk * 2 + c)
                    Ec = E[:, kk * 2 + c, :]
                    for (q0, q1, kind, arg) in subs:
                        if kind == "z":
                            kb.op("dve", lambda e, Ec=Ec, q0=q0, q1=q1: e.memset(Ec[:, q0:q1], 0.0), writes=[ek])
                        elif kind == "c":
                            kb.op("act", lambda e, Ec=Ec, S=S, q0=q0, q1=q1, arg=arg: e.activation(
                                out=Ec[:, q0:q1], in_=S[:, q0:q1], func=AF.Exp, scale=0.125, bias=arg),
                                reads=[("ps", bank), ("bcol",), ("c15s",)], writes=[ek, ("psr", bank)])
                        else:
                            sk = self.nxt() % 2
                            st_ = stmp[:, sk, 0:q1 - q0]
                            bias_ap = arg(st_, ("stmp", sk))
                            kb.op("dve", lambda e, st_=st_, S=S, q0=q0, q1=q1, bias_ap=bias_ap: e.scalar_tensor_tensor(
                                out=st_, in0=S[:, q0:q1], scalar=0.125, in1=bias_ap, op0=ALU.mult, op1=ALU.add),
                                reads=[("ps", bank), ("stmp", sk), ("ft", 1)] + [("T0", h), ("T1", h), ("Ts0",)],
                                writes=[("stmp", sk), ("psr", bank)])
                            kb.op("act", lambda e, Ec=Ec, st_=st_, q0=q0, q1=q1: e.activation(
                                out=Ec[:, q0:q1], in_=st_, func=AF.Exp), reads=[("stmp", sk)], writes=[ek])
                if pending is not None:
                    emit_pv(*pending)
                pending = (bi, vsl, vv, kk)
            emit_pv(*pending)
            for c in range(2):
                kb.op("dve", lambda e, c=c: e.reciprocal(out=ft[:, c, :n], in_=self.ps[6 + c][:, :n]),
                      reads=[("ps", 6 + c)], writes=[("ft", c)])
                kb.op("dve", lambda e, c=c: e.tensor_tensor(out=ft[:, 2 + c, :n], in0=self.ps[4 + c][:, :n],
                                                            in1=ft[:, c, :n], op=ALU.mult),
                      reads=[("ps", 4 + c), ("ft", c)], writes=[("ft", 2 + c)])
            kb.op("dve", lambda e: e.scalar_tensor_tensor(out=ft[:, 2, :n], in0=ft[:, 3, :n], scalar=neglam[:, 0:1],
                                                          in1=ft[:, 2, :n], op0=ALU.mult, op1=ALU.add),
                  reads=[("ft", 3), ("ft", 2), ("neglam",)], writes=[("ft", 2)])
            kb.op("act", lambda e: e.activation(out=ft[:, 3, :n], in_=ft[:, 2, :n], func=AF.Square),
                  reads=[("ft", 2), ("ft", 3)], writes=[("ft", 3)])
            kb.op("pe", lambda e: e.matmul(self.ps[0][:, :n], lhsT=self.ones_f[:], rhs=ft[:, 3, :n], start=True, stop=True),
                  reads=[("ft", 3), ("ones_f",)], writes=[("ps", 0)])
            kb.op("act", lambda e: e.activation(out=ft[:, 0, :n], in_=self.ps[0][:, :n], func=AF.Sqrt, scale=1.0 / 128,
                                                bias=EPS), reads=[("ps", 0), ("ft", 0)], writes=[("ft", 0)])
            kb.op("dve", lambda e: e.reciprocal(out=ft[:, 0, :n], in_=ft[:, 0, :n]), reads=[("ft", 0)], writes=[("ft", 0)])
            kb.op("dve", lambda e: e.tensor_tensor(out=ft[:, 2, :n], in0=ft[:, 2, :n], in1=ft[:, 0, :n], op=ALU.mult),
                  reads=[("ft", 0), ("ft", 2)], writes=[("ft", 2)])
            kb.op("act", lambda e: e.activation(out=oT[:, h, qcol0:qcol0 + n], in_=ft[:, 2, :n], func=AF.Identity,
                                                scale=subg[:, 0:1]), reads=[("ft", 2), ("subg",)], writes=okeys)

        for h in range(NHEAD):
            cconst = c15s[:, h:h + 1]
            for g in range(NG):
                passes = []
                nkb = 4 * g + 4

                def load_own(h=h, g=g, nkb=nkb):
                    akv_in = self.ag_kv_in[h].ap()
                    ksl = self.rload(akv_in[0:128, 0:nkb * 128], [128, nkb * 128], q="sp",
                                     reads=[("akv", h, tb, 0) for tb in range(nkb)])
                    vsl = self.rload(akv_in[128:256, 0:nkb * 128], [128, nkb * 128], q="sp",
                                     reads=[("akv", h, tb, 1) for tb in range(nkb)])
                    blocks = []
                    for jb in range(nkb):
                        subs = []
                        for i in range(4):
                            off = 4 * g + i - jb
                            q0, q1 = i * 128, (i + 1) * 128
                            if off < 0:
                                subs.append((q0, q1, "z", None))
                            elif off == 0:
                                subs.append((q0, q1, "t", lambda st_, sk, h=h: T0[:, h, :]))
                            elif off == 1:
                                subs.append((q0, q1, "t", lambda st_, sk, h=h: T1[:, h, :]))
                            else:
                                if subs and subs[-1][2] == "c":
                                    subs[-1] = (subs[-1][0], q1, "c", cconst)
                                else:
                                    subs.append((q0, q1, "c", cconst))
                        blocks.append((ksl[0][:, jb * 128:(jb + 1) * 128], vsl[0][:, jb * 128:(jb + 1) * 128], subs))
                    return ksl[1], vsl[1], blocks
                passes.append((nkb, load_own))
                for s in range(3):
                    if os.environ.get("ATT_NOREMOTE") == "1":
                        break

                    def load_rem(h=h, g=g, s=s):
                        akv_out = self.ag_kv_out[h].ap()
                        r0 = s * 256
                        ksl = self.rload(akv_out[r0:r0 + 128, :], [128, NP], q="sp", reads=[("ag_kv_out", h)])
                        vsl = self.rload(akv_out[r0 + 128:r0 + 256, :], [128, NP], q="sp", reads=[("ag_kv_out", h)])
                        bc = bcol[:, h, s:s + 1]
                        blocks = []
                        for jb in range(NTB):
                            if jb == NTB - 1 and g == 0:
                                def mk(st_, sk, h=h, s=s):
                                    bt = ft[:, 1, 0:128]
                                    kb.op("dve", lambda e: e.tensor_scalar(
                                        out=bt, in0=T1[:, h, :], scalar1=self.pos[:, 11 + s:12 + s],
                                        scalar2=col15[:, h, s:s + 1], op0=ALU.mult, op1=ALU.add),
                                        reads=[("T1", h), ("pos",), ("col15",), ("ft", 1)], writes=[("ft", 1)])
                                    return bt
                                subs = [(0, 128, "t", mk), (128, 512, "c", bc)]
                            else:
                                subs = [(0, 512, "c", bc)]
                            blocks.append((ksl[0][:, jb * 128:(jb + 1) * 128], vsl[0][:, jb * 128:(jb + 1) * 128], subs))
                        return ksl[1], vsl[1], blocks
                    passes.append((NTB, load_rem))
                attend(h, g * 512, 512, passes, [("uT", h, g)])
            NPB = cfg.PAST // 128

            def load_cache(h=h):
                ksl = self.rload(d["cache_kT"][j, h], [128, cfg.PAST], q="pool")
                vsl = self.rload(d["cache_v"][j, h].rearrange("p a b -> p (a b)"), [128, cfg.PAST], q="pool")
                blocks = []
                for jb in range(NPB):
                    if jb == NPB - 1:
                        subs = [(0, 32, "t", lambda st_, sk, h=h: T1[:, h, 0:32])]
                    else:
                        subs = [(0, 32, "c", cconst)]
                    blocks.append((ksl[0][:, jb * 128:(jb + 1) * 128], vsl[0][:, jb * 128:(jb + 1) * 128], subs))
                return ksl[1], vsl[1], blocks

            def load_new(h=h):
                return ("ksT",), ("vs",), [(ksT[:, h, :], vs[:, h, :], [(0, 32, "t", lambda st_, sk, h=h: Ts0[:, h, :])])]
            passes = [(NPB, load_cache), (1, load_new)]
            attend(h, NP, 32, passes, [("uT", h, NT - 1)])

        if cfg.debug:
            self.final.append(kb.dma("sp", d["dbg2"][:, 0:24], bcol.rearrange("p a b -> p (a b)"), reads=[("bcol",)]))
            self.final.append(kb.dma("sp", d["dbg2"][:, 24:48], col15.rearrange("p a b -> p (a b)"), reads=[("col15",)]))
            self.final.append(kb.dma("sp", d["dbg2"][:, 48:56], c15s, reads=[("c15s",)]))
            self.final.append(kb.dma("sp", d["dbg2"][:, 56:64], self.pos[:, 8:16], reads=[("pos",)]))
            self.final.append(kb.dma("pool", d["dbg"], oT[:, :, 0:T],
                                     reads=[("uT", h, t) for h in range(NHEAD) for t in range(NT)]))
        w2 = [self.rload(d["attn_wout"][j, q], [128, 2, D]) for q in range(4)]
        for ti, (t0, n, s) in enumerate(cfg.tiles):
            for dc in range(NCH):
                k = self.nxt() % 2
                bank = 4 + k
                py = self.ps[bank]
                for h in range(NHEAD):
                    wv, wk2 = w2[h // 2]
                    kb.op("pe", lambda e, wv=wv, h=h, dc=dc, py=py, t0=t0, n=n: e.matmul(
                        py[:, :n], lhsT=wv[:, h % 2, dc * 128:(dc + 1) * 128], rhs=oT[:, h, t0:t0 + n],
                        start=(h == 0), stop=(h == NHEAD - 1)),
                        reads=[wk2, ("uT", h, ti)], writes=[("ps", bank)])
                kb.op("dve", lambda e, py=py, dc=dc, t0=t0, n=n, s=s: e.scalar_tensor_tensor(
                    out=self.xT[:, dc, t0:t0 + n], in0=py[:, :n], scalar=self.mod[:, 40 + dc, s:s + 1],
                    in1=self.xT[:, dc, t0:t0 + n], op0=ALU.mult, op1=ALU.add),
                    reads=[("ps", bank), self.kmod, ("xT", dc, ti)], writes=[("xT", dc, ti)])


def _static_tables(cfg):
    k = np.arange(128)[:, None]
    q = np.arange(128)[None, :]
    planes = np.zeros((128, 49, 128), np.float32)
    b0 = _t5_bucket(k - q)
    b1 = _t5_bucket(k - 128 - q)
    for b in range(32):
        planes[:, b, :] = (b0 == b)
    for b in range(16):
        planes[:, 32 + b, :] = (b1 == b)
    planes[:, 48, :] = ((k // 64) > (q // 64))
    import ml_dtypes
    return planes.reshape(128, 49 * 128).astype(ml_dtypes.bfloat16), np.eye(128, dtype=np.float32)


def _pos_table(p):
    pos = np.zeros((128, 160), np.float32)
    if p >= 1:
        pos[:, p - 1] = 1.0
    pos[:, 4 + p] = 1.0
    for s in range(3):
        pos[:, 8 + s] = 0.0 if s < p else NEG
        pos[:, 11 + s] = 1.0 if s == p - 1 else 0.0
    wnds = (2, 4, 8, 16)
    for c in range(NCH):
        w = wnds[c // 2]
        for t in range(15):
            pos[:, 16 + c * 15 + t] = (w / min(t + 1, w)) if p == 0 else 1.0
    return pos


def _fm(v):
    v = np.asarray(v)
    lead = v.shape[:-1]
    return np.ascontiguousarray(np.moveaxis(v.reshape(lead + (NCH, 128)), (-2, -1), (1, 0)).reshape((128, NCH) + lead))


def prep_shared(cfg, I):
    L = cfg.L
    S = {}
    aw = np.asarray(I["ada_w"])[:L]
    S["ada_w"] = np.ascontiguousarray(aw.reshape(L, NCH, 128, 36, 256).transpose(0, 3, 2, 1, 4))
    S["ada_b"] = np.ascontiguousarray(np.asarray(I["ada_b"])[:L].reshape(L, 72, 128).transpose(0, 2, 1))
    S["norm_g"] = np.ascontiguousarray(np.asarray(I["norm_g"])[:L].reshape(L * 3 * NCH, 128).T)
    fw = np.asarray(I["ffn_w_in"])[:L].reshape(L, 2, NCH, 128, 2, NHC, 128)
    S["ffn_w_in"] = np.ascontiguousarray(fw.transpose(0, 1, 5, 3, 2, 4, 6).reshape(L, 2, NHC, 128, NCH, 256))
    fo = np.asarray(I["ffn_w_out"])[:L].reshape(L, 2, NHC // 2, 2, 128, D)
    S["ffn_w_out"] = np.ascontiguousarray(fo.transpose(0, 1, 2, 4, 3, 5))
    ia = [l for l in range(L) if cfg.kinds[l] == 0]
    NA = max(cfg.NA, 1)
    lw = np.asarray(I["lru_w_in"])[:NA].reshape(NA, NCH, 128, 2, NBLK, BS)
    S["lru_win"] = np.ascontiguousarray(lw.transpose(0, 4, 2, 1, 3, 5).reshape(NA, NBLK, 128, NCH, 160))
    ga = np.asarray(I["lru_ga_w"])[:NA]
    gx = np.asarray(I["lru_gx_w"])[:NA]
    S["lru_g"] = np.ascontiguousarray(np.concatenate([ga, gx], axis=-1).transpose(0, 2, 1, 3))
    lo = np.asarray(I["lru_w_out"])[:NA].reshape(NA, NBLK // 2, 2, BS, D)
    S["lru_wout"] = np.ascontiguousarray(lo.transpose(0, 1, 3, 2, 4))
    NB = max(cfg.NB, 1)
    pw = np.asarray(I["pool_w"])[:NB].reshape(NB, 4, 2, 128, 256)
    S["pool_w"] = np.ascontiguousarray(pw.transpose(0, 1, 3, 2, 4))
    NC_ = max(cfg.NCc, 1)
    awi = np.asarray(I["attn_w_in"])[:NC_].reshape(NC_, NCH, 128, 12, 256)
    S["attn_win"] = np.ascontiguousarray(awi.transpose(0, 3, 2, 1, 4))
    awo = np.asarray(I["attn_w_out"])[:NC_].reshape(NC_, 4, 2, 128, D)
    S["attn_wout"] = np.ascontiguousarray(awo.transpose(0, 1, 3, 2, 4))
    asm = np.zeros((NC_, 128, 517), np.float32)
    asm[:, :, 0:256] = np.tile(np.asarray(I["attn_q_g"])[:NC_], (1, 4))[:, None, :]
    asm[:, :, 256:512] = np.tile(np.asarray(I["attn_k_g"])[:NC_], (1, 4))[:, None, :]
    asm[:, :, 512] = np.asarray(I["attn_sub_g"])[:NC_]
    asm[:, 0:64, 513:517] = np.asarray(I["attn_lambda"])[:NC_].transpose(0, 2, 1)
    S["attn_small"] = asm
    S["rb_rep"] = np.ascontiguousarray(np.broadcast_to(np.asarray(I["rel_bias"]).reshape(1, 256), (128, 256)))
    S["planes"], S["ident"] = _static_tables(cfg)
    return S


def prep_core(cfg, I, k):
    NP = cfg.NP
    b, p = k // 4, k % 4
    C = {}
    xp = np.asarray(I["x_prompt"])[b, p * NP:(p + 1) * NP]
    xs = np.asarray(I["x_sample"])[k]
    C["xT"] = _fm(np.concatenate([xp, xs], axis=0))
    C["cT"] = _fm(np.stack([np.asarray(I["c_prompt"])[b], np.asarray(I["c_sample"])[k]], 0))
    C["pos"] = _pos_table(p)
    NA = max(cfg.NA, 1)

    def blk(v):
        v = np.asarray(v)
        lead = v.shape[:-1]
        return np.moveaxis(v.reshape(lead + (NBLK, BS)), (-2, -1), (1, 0)).reshape((BS, NBLK) + lead)
    ls = np.zeros((NA, BS, NBLK, 12), np.float32)
    for j in range(cfg.NA):
        ls[j, :, :, 0:4] = blk(I["lru_conv_w"][j])
        ls[j, :, :, 4] = blk(I["lru_conv_b"][j])
        ls[j, :, :, 5] = blk(I["lru_ga_b"][j])
        ls[j, :, :, 6] = blk(I["lru_gx_b"][j])
        ls[j, :, :, 7] = blk(I["lru_lambda"][j])
        ls[j, :, :, 8] = blk(I["state_lru_h"][j, k])
        ls[j, :, :, 9:12] = blk(I["state_lru_conv"][j, k])
    C["lru_small"] = ls
    NB = max(cfg.NB, 1)
    psm = np.zeros((NB, 128, 136), np.float32)
    for j in range(cfg.NB):
        psm[j, :, 0:8] = _fm(I["pool_b"][j])
        psm[j, :, 8:16] = _fm(I["pool_scale"][j])
        psm[j, :, 16:136] = _fm(I["state_pool"][j, k]).reshape(128, 120)
    C["pool_small"] = psm
    NC_ = max(cfg.NCc, 1)
    ck = np.zeros((NC_, NHEAD, 128, cfg.PAST), np.float32)
    cv = np.zeros((NC_, NHEAD, 128, cfg.PAST // 128, 128), np.float32)
    for j in range(cfg.NCc):
        ck[j] = np.asarray(I["cache_k"])[j, k].transpose(1, 2, 0)
        cv[j] = np.asarray(I["cache_v"])[j, k].reshape(cfg.PAST // 128, 128, NHEAD, 128).transpose(2, 1, 0, 3)
    C["cache_kT"] = ck
    C["cache_v"] = cv
    return C


_PROG_CACHE = {}


def run_cfg(cfg, I):
    key = (cfg.NP, cfg.PAST, cfg.kinds, cfg.debug)
    if key not in _PROG_CACHE:
        bld = Builder(cfg)
        nc = bld.build()
        _PROG_CACHE[key] = (bld, nc)
    bld, nc = _PROG_CACHE[key]
    S = prep_shared(cfg, I)
    in_maps = []
    for k in range(8):
        m = dict(S)
        m.update(prep_core(cfg, I, k))
        in_maps.append({n: np.ascontiguousarray(m[n]) for n in bld.in_names})
    res = run_bass_kernel_spmd(nc, in_maps, core_ids=list(range(8)))
    if cfg.debug:
        global _DBG
        _DBG = [np.asarray(r["dbg"]) for r in res.results]
        global _DBG2
        _DBG2 = [np.asarray(r["dbg2"]) for r in res.results]
    return assemble(cfg, res.results)


def _unfm(a):
    return np.ascontiguousarray(a.transpose(2, 1, 0).reshape(a.shape[2], D))


def assemble(cfg, R):
    NP = cfg.NP
    B = 2
    SEQ = 4 * NP
    y_p = np.zeros((B, SEQ, D), np.float32)
    y_s = np.zeros((8, 32, D), np.float32)
    for k in range(8):
        b, p = k // 4, k % 4
        yt = _unfm(np.asarray(R[k]["yT"]))
        y_p[b, p * NP:(p + 1) * NP] = yt[:NP]
        y_s[k] = yt[NP:]
    NA, NB, NCc = cfg.NA, cfg.NB, cfg.NCc

    def unblk(a):
        return np.ascontiguousarray(a.T.reshape(-1))
    h_p = np.zeros((NA, B, DRNN), np.float32)
    h_s = np.zeros((NA, 8, DRNN), np.float32)
    cv_p = np.zeros((NA, B, 3, DRNN), np.float32)
    cv_s = np.zeros((NA, 8, 3, DRNN), np.float32)
    for k in range(8):
        b, p = k // 4, k % 4
        lo = np.asarray(R[k]["lru_o"])
        for j in range(NA):
            h_s[j, k] = unblk(lo[j, :, :, 1])
            for t in range(3):
                cv_s[j, k, t] = unblk(lo[j, :, :, 5 + t])
            if p == 3:
                h_p[j, b] = unblk(lo[j, :, :, 0])
                for t in range(3):
                    cv_p[j, b, t] = unblk(lo[j, :, :, 2 + t])
    pl_p = np.zeros((NB, B, 15, D), np.float32)
    pl_s = np.zeros((NB, 8, 15, D), np.float32)
    for k in range(8):
        b, p = k // 4, k % 4
        po = np.asarray(R[k]["pool_o"])
        for j in range(NB):
            pl_s[j, k] = _unfm(po[j][:, :, 15:30])
            if p == 3:
                pl_p[j, b] = _unfm(po[j][:, :, 0:15])
    k_p = np.zeros((NCc, B, SEQ, NHEAD, 128), np.float32)
    v_p = np.zeros((NCc, B, SEQ, NHEAD, 128), np.float32)
    k_s = np.zeros((NCc, 8, 32, NHEAD, 128), np.float32)
    v_s = np.zeros((NCc, 8, 32, NHEAD, 128), np.float32)
    for k in range(8):
        b, p = k // 4, k % 4
        ko = np.asarray(R[k]["k_o"])
        vo = np.asarray(R[k]["v_o"])
        for j in range(NCc):
            k_p[j, b, p * NP:(p + 1) * NP] = ko[j, :NP].reshape(NP, NHEAD, 128)
            v_p[j, b, p * NP:(p + 1) * NP] = vo[j, :NP].reshape(NP, NHEAD, 128)
            k_s[j, k] = ko[j, NP:].reshape(32, NHEAD, 128)
            v_s[j, k] = vo[j, NP:].reshape(32, NHEAD, 128)
    return (y_p, y_s, h_p, h_s, cv_p, cv_s, pl_p, pl_s, k_p, v_p, k_s, v_s)


def kernel(**inputs):
    cfg = Cfg(NP=2048, PAST=1024, kinds=(0, 1, 2, 0))
    return run_cfg(cfg, inputs)
```
